# Optimizing a Trainium2 kernel written in Bass

```python
import math
import jax, jax.numpy as jnp
from jax import lax
import numpy as np

D_MODEL = 1024
BATCH = 4
SEQ = 4096
DEPTH = 2

N_MIXERS = 2
MEM_LEN = 256
GDN_HEADS = 8
GDN_HEAD_DIM = D_MODEL // GDN_HEADS
GDN_WIDTH = GDN_HEADS * GDN_HEAD_DIM
CONV_WIDTH = 4
CHUNK = 64
DIFF_HEADS = 8
DIFF_HEAD_DIM = D_MODEL // (2 * DIFF_HEADS)
DIFF_WIDTH = 2 * DIFF_HEADS * DIFF_HEAD_DIM
Q_BLOCK = 128
MEM_HEADS = 4
MEM_HEAD_DIM = 128
MEM_WIDTH = MEM_HEADS * MEM_HEAD_DIM
D_FF = 4 * D_MODEL
ALPHA = (2.0 * DEPTH) ** 0.25
BETA_INIT = (8.0 * DEPTH) ** -0.25
LN_EPS = 1e-5
RMS_EPS = 1e-6
GDN_IN = 4 * GDN_WIDTH + 2 * GDN_HEADS + MEM_WIDTH
DIFF_IN = 3 * DIFF_WIDTH + MEM_WIDTH
GDN_CAT = GDN_WIDTH + MEM_WIDTH
DIFF_CAT = DIFF_WIDTH + MEM_WIDTH

kernel_name = 'hybrid_gdn_diffattn_memxattn_deepnorm'


def layer_norm(x, g, b):
    xf = x.astype(jnp.float32)
    mu = xf.mean(-1, keepdims=True)
    var = jnp.square(xf - mu).mean(-1, keepdims=True)
    return ((xf - mu) * lax.rsqrt(var + LN_EPS) * g + b).astype(x.dtype)


def rms_norm(x, w):
    xf = x.astype(jnp.float32)
    return (xf * lax.rsqrt(jnp.square(xf).mean(-1, keepdims=True) + RMS_EPS) * w).astype(x.dtype)


def l2_normalize(x):
    xf = x.astype(jnp.float32)
    return (xf * lax.rsqrt(jnp.square(xf).sum(-1, keepdims=True) + RMS_EPS)).astype(x.dtype)


def causal_depthwise_conv(x, w):
    k = w.shape[0]
    return lax.conv_general_dilated(x, w[:, None, :].astype(x.dtype), window_strides=(1,),
                                    padding=[(k - 1, 0)], dimension_numbers=('NWC', 'WIO', 'NWC'),
                                    feature_group_count=x.shape[-1])


def chunk_gated_delta_rule(q, k, v, g, beta):
    B, T, H, Dk = q.shape
    n = T // CHUNK
    f32 = jnp.float32

    def to_chunks(a):
        return a.astype(f32).reshape(B, n, CHUNK, H, *a.shape[3:]).swapaxes(2, 3)

    qc = to_chunks(q) * (Dk ** -0.5)
    kc = to_chunks(k)
    vc = to_chunks(v)
    gc = jnp.cumsum(to_chunks(g), axis=-1)
    bc = to_chunks(beta)
    k_beta = kc * bc[..., None]
    v_beta = vc * bc[..., None]
    causal = jnp.tril(jnp.ones((CHUNK, CHUNK), bool))
    strict = jnp.tril(jnp.ones((CHUNK, CHUNK), bool), -1)
    decay = jnp.exp(jnp.where(causal, gc[..., :, None] - gc[..., None, :], -jnp.inf))
    lower = jnp.where(strict, jnp.einsum('bnhcd,bnhsd->bnhcs', k_beta, kc) * decay, 0.0)
    a_mat = jnp.eye(CHUNK, dtype=f32) + lower
    u = lax.linalg.triangular_solve(a_mat, v_beta, left_side=True, lower=True, unit_diagonal=True)
    w = lax.linalg.triangular_solve(a_mat, k_beta * jnp.exp(gc)[..., None], left_side=True,
                                    lower=True, unit_diagonal=True)
    attn_qk = jnp.einsum('bnhcd,bnhsd->bnhcs', qc, kc) * decay
    q_decay = qc * jnp.exp(gc)[..., None]
    k_decay = kc * jnp.exp(gc[..., -1:] - gc)[..., None]
    chunk_decay = jnp.exp(gc[..., -1])
    xs = tuple(jnp.moveaxis(a, 1, 0) for a in (q_decay, k_decay, u, w, attn_qk, chunk_decay))

    def step(state, inp):
        qd, kd, u_c, w_c, a_c, dc = inp
        v_new = u_c - jnp.einsum('bhcd,bhde->bhce', w_c, state)
        o = jnp.einsum('bhcd,bhde->bhce', qd, state) + jnp.einsum('bhcs,bhse->bhce', a_c, v_new)
        state = state * dc[..., None, None] + jnp.einsum('bhcd,bhce->bhde', kd, v_new)
        return state, o

    s0 = jnp.zeros((B, H, Dk, v.shape[-1]), f32)
    _, o = lax.scan(step, s0, xs)
    return o.transpose(1, 0, 3, 2, 4).reshape(B, T, H, v.shape[-1]).astype(q.dtype)


def differential_attention(q, k, v, lam):
    B, T, H, _, d = q.shape
    nb = T // Q_BLOCK
    qb = q.reshape(B, nb, Q_BLOCK, H, 2, d).swapaxes(0, 1)
    key_pos = jnp.arange(T)
    scale = d ** -0.5

    def block(args):
        q_blk, i = args
        s = jnp.einsum('bqhmd,bkhmd->bhmqk', q_blk, k).astype(jnp.float32) * scale
        q_pos = i * Q_BLOCK + jnp.arange(Q_BLOCK)
        s = jnp.where(key_pos[None, :] <= q_pos[:, None], s, -jnp.inf)
        p = jax.nn.softmax(s, axis=-1)
        a = p[:, :, 0] - lam * p[:, :, 1]
        return jnp.einsum('bhqk,bkhe->bqhe', a.astype(v.dtype), v)

    o = lax.map(block, (qb, jnp.arange(nb)))
    return o.swapaxes(0, 1).reshape(B, T, H, 2 * d)


def memory_attention(mq, mem_k, mem_v):
    B, T, _ = mq.shape
    q = mq.reshape(B, T, MEM_HEADS, MEM_HEAD_DIM)
    s = jnp.einsum('bthd,bmhd->bhtm', q, mem_k).astype(jnp.float32) * (MEM_HEAD_DIM ** -0.5)
    p = jax.nn.softmax(s, axis=-1)
    return jnp.einsum('bhtm,bmhd->bthd', p.astype(mem_v.dtype), mem_v).reshape(B, T, MEM_WIDTH)


def sqrelu_mlp(x, w1, w2):
    return jnp.square(jax.nn.relu(x @ w1)) @ w2


def lambda_init(layer_idx):
    return 0.8 - 0.6 * math.exp(-0.3 * layer_idx)


def gdn_mixer(x, mem_k, mem_v, w_in, conv_w, a_log, dt_bias, gate_norm_w, w_out):
    B, T, _ = x.shape
    proj = x @ w_in
    qkv = proj[..., :3 * GDN_WIDTH]
    z = proj[..., 3 * GDN_WIDTH:4 * GDN_WIDTH]
    a = proj[..., 4 * GDN_WIDTH:4 * GDN_WIDTH + GDN_HEADS]
    b = proj[..., 4 * GDN_WIDTH + GDN_HEADS:4 * GDN_WIDTH + 2 * GDN_HEADS]
    mq = proj[..., 4 * GDN_WIDTH + 2 * GDN_HEADS:]
    qkv = jax.nn.silu(causal_depthwise_conv(qkv, conv_w))
    heads = lambda t: t.reshape(B, T, GDN_HEADS, GDN_HEAD_DIM)
    q = l2_normalize(heads(qkv[..., :GDN_WIDTH]))
    k = l2_normalize(heads(qkv[..., GDN_WIDTH:2 * GDN_WIDTH]))
    v = heads(qkv[..., 2 * GDN_WIDTH:])
    beta = jax.nn.sigmoid(b.astype(jnp.float32))
    g = -jnp.exp(a_log) * jax.nn.softplus(a.astype(jnp.float32) + dt_bias)
    o = chunk_gated_delta_rule(q, k, v, g, beta)
    o = (rms_norm(o, gate_norm_w) * jax.nn.silu(heads(z))).reshape(B, T, GDN_WIDTH)
    m = memory_attention(mq, mem_k, mem_v)
    return jnp.concatenate([o, m], axis=-1) @ w_out


def diff_mixer(x, mem_k, mem_v, layer_idx, w_in, lq1, lk1, lq2, lk2, subln_w, w_out):
    B, T, _ = x.shape
    proj = x @ w_in
    q = proj[..., :DIFF_WIDTH].reshape(B, T, DIFF_HEADS, 2, DIFF_HEAD_DIM)
    k = proj[..., DIFF_WIDTH:2 * DIFF_WIDTH].reshape(B, T, DIFF_HEADS, 2, DIFF_HEAD_DIM)
    v = proj[..., 2 * DIFF_WIDTH:3 * DIFF_WIDTH].reshape(B, T, DIFF_HEADS, 2 * DIFF_HEAD_DIM)
    mq = proj[..., 3 * DIFF_WIDTH:]
    lam_init = lambda_init(layer_idx)
    f32 = jnp.float32
    lam = (jnp.exp(jnp.sum(lq1.astype(f32) * lk1.astype(f32)))
           - jnp.exp(jnp.sum(lq2.astype(f32) * lk2.astype(f32))) + lam_init)
    o = differential_attention(q, k, v, lam)
    o = (rms_norm(o, subln_w) * (1.0 - lam_init)).reshape(B, T, DIFF_WIDTH)
    m = memory_attention(mq, mem_k, mem_v)
    return jnp.concatenate([o, m], axis=-1) @ w_out


def setup_inputs(seed: int = 0) -> dict:
    key = jax.random.key(seed)
    ks = iter(jax.random.split(key, 40))
    nrm = lambda shape, s: jax.random.normal(next(ks), shape, jnp.float32) * s
    gain = lambda n: 1.0 + nrm((n,), 0.02)
    bias = lambda n: nrm((n,), 0.02)
    d = D_MODEL
    inp = {}
    inp['x'] = nrm((BATCH, SEQ, d), 1.0)
    inp['mem'] = nrm((BATCH, MEM_LEN, d), 1.0)
    inp['mem_ln_g'] = gain(d)
    inp['mem_ln_b'] = bias(d)
    inp['w_mem_kv'] = nrm((d, 2 * MEM_WIDTH), d ** -0.5)
    inp['l0_w_in'] = nrm((d, GDN_IN), d ** -0.5)
    inp['l0_conv_w'] = nrm((CONV_WIDTH, 3 * GDN_WIDTH), CONV_WIDTH ** -0.5)
    inp['l0_a_log'] = jnp.log(jax.random.uniform(next(ks), (GDN_HEADS,), jnp.float32, 1.0, 16.0))
    dt = jnp.exp(jax.random.uniform(next(ks), (GDN_HEADS,), jnp.float32, math.log(1e-3), math.log(1e-1)))
    inp['l0_dt_bias'] = dt + jnp.log(-jnp.expm1(-dt))
    inp['l0_gate_norm_w'] = gain(GDN_HEAD_DIM)
    inp['l0_w_out'] = nrm((GDN_CAT, d), GDN_CAT ** -0.5 * BETA_INIT)
    inp['l0_ln1_g'] = gain(d)
    inp['l0_ln1_b'] = bias(d)
    inp['l0_w_ff1'] = nrm((d, D_FF), d ** -0.5)
    inp['l0_w_ff2'] = nrm((D_FF, d), D_FF ** -0.5 * BETA_INIT)
    inp['l0_ln2_g'] = gain(d)
    inp['l0_ln2_b'] = bias(d)
    inp['l1_w_in'] = nrm((d, DIFF_IN), d ** -0.5)
    inp['l1_lambda_q1'] = nrm((DIFF_HEAD_DIM,), 0.1)
    inp['l1_lambda_k1'] = nrm((DIFF_HEAD_DIM,), 0.1)
    inp['l1_lambda_q2'] = nrm((DIFF_HEAD_DIM,), 0.1)
    inp['l1_lambda_k2'] = nrm((DIFF_HEAD_DIM,), 0.1)
    inp['l1_subln_w'] = gain(2 * DIFF_HEAD_DIM)
    inp['l1_w_out'] = nrm((DIFF_CAT, d), DIFF_CAT ** -0.5 * BETA_INIT)
    inp['l1_ln1_g'] = gain(d)
    inp['l1_ln1_b'] = bias(d)
    inp['l1_w_ff1'] = nrm((d, D_FF), d ** -0.5)
    inp['l1_w_ff2'] = nrm((D_FF, d), D_FF ** -0.5 * BETA_INIT)
    inp['l1_ln2_g'] = gain(d)
    inp['l1_ln2_b'] = bias(d)
    return inp


def reference(x, mem, mem_ln_g, mem_ln_b, w_mem_kv,
              l0_w_in, l0_conv_w, l0_a_log, l0_dt_bias, l0_gate_norm_w, l0_w_out,
              l0_ln1_g, l0_ln1_b, l0_w_ff1, l0_w_ff2, l0_ln2_g, l0_ln2_b,
              l1_w_in, l1_lambda_q1, l1_lambda_k1, l1_lambda_q2, l1_lambda_k2, l1_subln_w, l1_w_out,
              l1_ln1_g, l1_ln1_b, l1_w_ff1, l1_w_ff2, l1_ln2_g, l1_ln2_b):
    B = mem.shape[0]
    mem_kv = layer_norm(mem, mem_ln_g, mem_ln_b) @ w_mem_kv
    mem_k = mem_kv[..., :MEM_WIDTH].reshape(B, MEM_LEN, MEM_HEADS, MEM_HEAD_DIM)
    mem_v = mem_kv[..., MEM_WIDTH:].reshape(B, MEM_LEN, MEM_HEADS, MEM_HEAD_DIM)

    mixer_params = [
        (l0_w_in, l0_conv_w, l0_a_log, l0_dt_bias, l0_gate_norm_w, l0_w_out),
        (l1_w_in, l1_lambda_q1, l1_lambda_k1, l1_lambda_q2, l1_lambda_k2, l1_subln_w, l1_w_out),
    ]
    norm_ffn_params = [
        (l0_ln1_g, l0_ln1_b, l0_w_ff1, l0_w_ff2, l0_ln2_g, l0_ln2_b),
        (l1_ln1_g, l1_ln1_b, l1_w_ff1, l1_w_ff2, l1_ln2_g, l1_ln2_b),
    ]
    for i in range(DEPTH):
        if i % N_MIXERS == 0:
            mix = gdn_mixer(x, mem_k, mem_v, *mixer_params[i])
        else:
            mix = diff_mixer(x, mem_k, mem_v, i, *mixer_params[i])
        ln1_g, ln1_b, w_ff1, w_ff2, ln2_g, ln2_b = norm_ffn_params[i]
        x = layer_norm(ALPHA * x + mix, ln1_g, ln1_b)
        x = layer_norm(ALPHA * x + sqrelu_mlp(x, w_ff1, w_ff2), ln2_g, ln2_b)
    return x
```

```python
from contextlib import ExitStack
import numpy as np
import ml_dtypes
import concourse.bass as bass
import concourse.mybir as mybir
from concourse.bass_utils import run_bass_kernel_spmd

F32 = mybir.dt.float32
BF16 = mybir.dt.bfloat16
AF = mybir.ActivationFunctionType
ALU = mybir.AluOpType
NPBF = ml_dtypes.bfloat16

D = 1024
T = 4096
NB = 4
DFF = 4096
ALPHA = 4.0 ** 0.25
LN_EPS = 1e-5
RMS_EPS = 1e-6

SAME_ENGINE_SYNC = {"pe": False, "act": True, "dve": True, "pool": True, "sp": False}
N_DMA_SEMS = 64
N_HW_SEMS = 44


class Op:
    __slots__ = ("eng", "fn", "deps", "inc", "val", "sem", "is_dma")

    def __init__(self, eng, fn, deps, is_dma=False):
        self.eng = eng
        self.fn = fn
        self.deps = deps
        self.inc = False
        self.val = None
        self.sem = None
        self.is_dma = is_dma


class Prog:
    def __init__(self, nc):
        self.nc = nc
        self.engs = {"pe": nc.tensor, "act": nc.scalar, "dve": nc.vector,
                     "pool": nc.gpsimd, "sp": nc.sync}
        self.ops = []
        self.last_w = {}
        self.readers = {}
        self.bank_last = {}
        self.dma_rr = 0
        self.dma_rr_sw = 0
        self.cc_sems = []
        self.dma_last = [None] * N_DMA_SEMS
        self.uid = 0

    def _deps_for(self, eng, reads, writes, banks):
        deps = []
        for k in reads:
            w = self.last_w.get(k)
            if w is not None:
                deps.append(w)
        for k in writes:
            w = self.last_w.get(k)
            if w is not None:
                deps.append(w)
            deps.extend(self.readers.get(k, ()))
        for b in banks:
            for e, o in self.bank_last.get(b, {}).items():
                if e != eng:
                    deps.append(o)
        return deps

    def _record(self, op, reads, writes, banks):
        for k in reads:
            self.readers.setdefault(k, []).append(op)
        for k in writes:
            self.last_w[k] = op
            self.readers[k] = []
        for b in banks:
            self.bank_last.setdefault(b, {})[op.eng] = op
        self.ops.append(op)

    def op(self, eng, fn, reads=(), writes=(), banks=()):
        o = Op(eng, fn, self._deps_for(eng, reads, writes, banks))
        self._record(o, reads, writes, banks)
        return o

    def dma(self, eng, fn, reads=(), writes=()):
        deps = self._deps_for(eng, reads, writes, ())
        if eng == "pool":
            i = N_HW_SEMS + self.dma_rr_sw
            self.dma_rr_sw = (self.dma_rr_sw + 1) % (N_DMA_SEMS - N_HW_SEMS)
        else:
            i = self.dma_rr
            self.dma_rr = (i + 1) % N_HW_SEMS
        if self.dma_last[i] is not None:
            deps.append(self.dma_last[i])
        o = Op(eng, fn, deps, is_dma=True)
        o.sem = i
        o.inc = True
        self.dma_last[i] = o
        self._record(o, reads, writes, ())
        return o

    def barrier(self):
        last = {}
        for o in self.ops:
            if not o.is_dma:
                last[o.eng] = o
        dmas = [d for d in self.dma_last if d is not None]
        for e in self.engs:
            deps = [last[x] for x in last if x != e] + dmas
            self.ops.append(Op(e, lambda E: E.nop(), deps))

    def coll_wait(self):
        def f(E):
            if self.cc_sems:
                E.wait_ge(self.cc_sems[0], self.cc_count)
            return E.nop()
        return self.op("pool", f)

    def coll(self, kind, groups, in_ap, out_ap, reads=(), writes=(), wait=True):
        nc = self.nc
        def f(E):
            if not self.cc_sems:
                self.cc_sems.append(nc.alloc_semaphore(name="s_cc"))
                self.cc_count = 0
            sem = self.cc_sems[0]
            self.cc_count += 1
            E.collective_compute(kind, ALU.bypass, replica_groups=groups, ins=[in_ap.opt()], outs=[out_ap.opt()]).then_inc(sem)
            if wait:
                E.wait_ge(sem, self.cc_count)
            return E.nop()
        return self.op("pool", f, reads, writes)

    def dma_(self, eng, out, in_, reads=(), writes=()):
        return self.dma(eng, lambda E: E.dma_start(out=out, in_=in_), reads, writes)

    def mm(self, out, lhsT, rhs, start, stop, reads=(), banks=()):
        return self.op("pe", lambda E: E.matmul(out, lhsT=lhsT, rhs=rhs, start=start, stop=stop), reads, (), banks)

    def tr(self, out, in_, ident, reads=(), banks=()):
        return self.op("pe", lambda E: E.transpose(out, in_, ident), reads, (), banks)

    def act(self, out, in_, func, bias=0.0, scale=1.0, accum_out=None, reads=(), writes=(), banks=()):
        if accum_out is None:
            f = lambda E: E.activation(out=out, in_=in_, func=func, bias=bias, scale=scale)
        else:
            f = lambda E: E.activation(out=out, in_=in_, func=func, bias=bias, scale=scale, accum_out=accum_out)
        return self.op("act", f, reads, writes, banks)

    def copy(self, eng, out, in_, reads=(), writes=(), banks=()):
        if eng == "act":
            return self.op("act", lambda E: E.copy(out=out, in_=in_), reads, writes, banks)
        return self.op(eng, lambda E: E.tensor_copy(out=out, in_=in_), reads, writes, banks)

    def tt(self, eng, out, in0, in1, op, reads=(), writes=(), banks=()):
        return self.op(eng, lambda E: E.tensor_tensor(out=out, in0=in0, in1=in1, op=op), reads, writes, banks)

    def ts(self, eng, out, in0, s1, op0, s2=None, op1=None, reads=(), writes=(), banks=()):
        if op1 is None:
            f = lambda E: E.tensor_scalar(out=out, in0=in0, scalar1=s1, scalar2=None, op0=op0)
        else:
            f = lambda E: E.tensor_scalar(out=out, in0=in0, scalar1=s1, scalar2=s2, op0=op0, op1=op1)
        return self.op(eng, f, reads, writes, banks)

    def stt(self, eng, out, in0, scalar, in1, op0, op1, reads=(), writes=(), banks=()):
        return self.op(eng, lambda E: E.scalar_tensor_tensor(out=out, in0=in0, scalar=scalar, in1=in1, op0=op0, op1=op1),
                       reads, writes, banks)

    def emit(self, stack, final_wait_ops=()):
        nc = self.nc
        for o in self.ops:
            for d in o.deps:
                if d.is_dma:
                    continue
                if d.eng == o.eng and not SAME_ENGINE_SYNC[o.eng]:
                    continue
                d.inc = True
        for d in final_wait_ops:
            d.inc = True
        esem = {e: stack.enter_context(nc.semaphore(f"s_{e}")) for e in self.engs}
        dsem = [stack.enter_context(nc.semaphore(f"s_dma{i}")) for i in range(N_DMA_SEMS)]
        cnt = {e: 0 for e in self.engs}
        dcnt = [0] * N_DMA_SEMS
        for o in self.ops:
            if o.is_dma:
                dcnt[o.sem] += 16
                o.val = dcnt[o.sem]
                assert o.val <= 96, 'DMA semaphore value limit (device faults above ~100)'
            elif o.inc:
                cnt[o.eng] += 1
                o.val = cnt[o.eng]
        waited = {e: {} for e in self.engs}
        nwaits = 0
        for o in self.ops:
            E = self.engs[o.eng]
            need = {}
            for d in o.deps:
                if d.is_dma:
                    sk = ("d", d.sem)
                    sh = dsem[d.sem]
                else:
                    if d.eng == o.eng and not SAME_ENGINE_SYNC[o.eng]:
                        continue
                    sk = ("e", d.eng)
                    sh = esem[d.eng]
                if waited[o.eng].get(sk, 0) >= d.val:
                    continue
                if sk not in need or need[sk][1] < d.val:
                    need[sk] = (sh, d.val)
            for sk, (sh, v) in need.items():
                E.wait_ge(sh, v)
                waited[o.eng][sk] = v
                nwaits += 1
            ins = o.fn(E)
            if o.is_dma:
                ins.then_inc(dsem[o.sem], 16)
            elif o.inc:
                ins.then_inc(esem[o.eng], 1)
        for d in final_wait_ops:
            if d.is_dma:
                nc.sync.wait_ge(dsem[d.sem], d.val)
            else:
                nc.sync.wait_ge(esem[d.eng], d.val)
        self.stats = dict(n_ops=len(self.ops), n_waits=nwaits, cnt=dict(cnt))
        return self.stats


class Ctx:
    def __init__(self, nc):
        self.nc = nc
        self.P = Prog(nc)
        self.banks = [nc.alloc_psum_tensor(f"psb{i}", [128, 512], F32) for i in range(8)]
        self.rr = {}
        self.uid = 0
        self.pfx = ""

    def bank(self, pool, ids):
        i = self.rr.get(pool, 0)
        self.rr[pool] = i + 1
        return ids[i % len(ids)]

    def eng(self, pool, engs):
        i = self.rr.get(("e", pool), 0)
        self.rr[("e", pool)] = i + 1
        return engs[i % len(engs)]

    def key(self, name):
        self.uid += 1
        return (name, self.uid)


def layer_norm_tile(C, r, rkey, out, okey, g_rep, b_rep, gkey, scr, tag):
    P = C.P
    st, mv, sd, rstd, nmr, xn = scr["stats"], scr["mv"], scr["sd"], scr["rstd"], scr["nmr"], scr["xn"]
    k = lambda n: (tag, n)
    P.op("dve", lambda E: E.bn_stats(out=st[:, 0:6], in_=r[:, 0:512]), reads=[rkey], writes=[k("st0")])
    P.op("dve", lambda E: E.bn_stats(out=st[:, 6:12], in_=r[:, 512:1024]), reads=[rkey], writes=[k("st1")])
    P.op("dve", lambda E: E.bn_aggr(out=mv[:, 0:2], in_=st[:, 0:12]), reads=[k("st0"), k("st1")], writes=[k("mv")])
    P.op("act", lambda E: E.activation(out=sd[:], in_=mv[:, 1:2], func=AF.Sqrt, bias=LN_EPS, scale=1.0),
         reads=[k("mv")], writes=[k("sd")])
    P.op("dve", lambda E: E.reciprocal(out=rstd[:], in_=sd[:]), reads=[k("sd")], writes=[k("rstd")])
    P.op("dve", lambda E: E.scalar_tensor_tensor(out=nmr[:], in0=mv[:, 0:1], scalar=-1.0, in1=rstd[:],
                                                 op0=ALU.mult, op1=ALU.mult),
         reads=[k("mv"), k("rstd")], writes=[k("nmr")])
    P.op("act", lambda E: E.activation(out=xn[:], in_=r[:], func=AF.Identity, bias=nmr[:], scale=rstd[:]),
         reads=[rkey, k("rstd"), k("nmr")], writes=[k("xn")])
    P.op("pool", lambda E: E.tensor_tensor(out=xn[:], in0=xn[:], in1=g_rep[:], op=ALU.mult),
         reads=[k("xn"), gkey], writes=[k("xn")])
    P.op("pool", lambda E: E.tensor_tensor(out=out, in0=xn[:], in1=b_rep[:], op=ALU.add),
         reads=[k("xn"), gkey], writes=[okey])


def transpose_tile_bf16(C, src_bf, skey, dst3, dkey, ident_bf, nchunks, bank_ids, evac_engs=("dve", "act")):
    P = C.P
    b = C.bank("tr", bank_ids)
    pb = C.banks[b][:].bitcast(BF16)
    for kc in range(nchunks):
        P.op("pe", lambda E, kc=kc: E.transpose(pb[:, kc * 128:(kc + 1) * 128], src_bf[:, kc * 128:(kc + 1) * 128], ident_bf[:]),
             reads=[skey, "ident_bf"], banks=[b])
    e = C.eng("tr", evac_engs)
    src_v = pb[:, 0:nchunks * 128].rearrange("p (k t) -> p k t", k=nchunks)
    if e == "act":
        P.op("act", lambda E: E.copy(out=dst3, in_=src_v), writes=[dkey], banks=[b])
    else:
        P.op(e, lambda E: E.tensor_copy(out=dst3, in_=src_v), writes=[dkey], banks=[b])


def build_ffn_phase(C, catT, xres, w_out, w_ff1, w_ff2, ln1g, ln1b, ln2g, ln2b, ident,
                    x_out, xT_out, stack, dbg=None):
    nc, P = C.nc, C.P
    sb = lambda name, shape, dt: stack.enter_context(nc.sbuf_tensor(C.pfx + name, shape, dt))
    dbg = dbg or {}
    NPASS, TP = dbg.get('npass', 2), 1024
    NT = TP // 128
    wo = sb("wo", [128, 12, 1024], BF16)
    lnp = sb("lnp", [128, 4, 1024], F32)
    idb = sb("idb", [128, 128], BF16)
    idf = sb("idf", [128, 128], F32)
    cat_sb = [sb(f"cat{i}", [128, 12, 512], BF16) for i in range(1 if (dbg or {}).get("cat_loader") else 2)]
    xmT = sb("xmT", [128, 8, TP], BF16)
    yacc = sb("yacc", [128, NT, 1024], F32)
    hT = [sb(f"hT{i}", [128, 4, TP], BF16) for i in range(2)]
    w1g = [sb(f"w1g{i}", [128, 8, 512], BF16) for i in range(2)]
    w2g = [sb(f"w2g{i}", [128, 4, 1024], BF16) for i in range(2)]
    rt = [sb(f"rt{i}", [128, 1024], F32) for i in range(3)]
    xr = [sb(f"xr{i}", [128, 1024], F32) for i in range(3)]
    xb = [sb(f"xb{i}", [128, 1024], BF16) for i in range(3)]
    relu_t = [sb(f"relu{i}", [128, 512], F32) for i in range(2)]
    scr = [dict(stats=sb(f"lst{i}", [128, 12], F32), mv=sb(f"lmv{i}", [128, 2], F32), sd=sb(f"lsd{i}", [128, 1], F32),
                rstd=sb(f"lrs{i}", [128, 1], F32), nmr=sb(f"lnm{i}", [128, 1], F32))
           for i in range(3)]
    outs = []

    P.dma("sp", lambda E: E.dma_start(out=idf[:], in_=ident), writes=["ident_f"])
    P.op("dve", lambda E: E.tensor_copy(out=idb[:], in_=idf[:]), reads=["ident_f"], writes=["ident_bf"])
    for i, src in enumerate((ln1g, ln1b, ln2g, ln2b)):
        P.dma("act", lambda E, i=i, src=src: E.dma_start(out=lnp[:, i, :], in_=src), writes=[("lnp", i)])
    wov = w_out.rearrange("(kc p) n -> p kc n", p=128)
    for j in range(3):
        P.dma("pool", lambda E, j=j: E.dma_start(out=wo[:, 4 * j:4 * j + 4, :], in_=wov[:, 4 * j:4 * j + 4, :]),
              writes=[("wo", j)])
    cat_loader = dbg.get("cat_loader")
    xT_dst = dbg.get("xT_dst")
    catv = catT.rearrange("(kc p) t -> p kc t", p=128) if cat_loader is None else None
    w1v = w_ff1.rearrange("(kc p) n -> p kc n", p=128)
    w2v = w_ff2.rearrange("(fc p) n -> p fc n", p=128)

    ncat = 0
    nrt = 0
    gcount = 0
    for ps in range(NPASS):
        t0 = ps * TP
        def s1_gen(tl):
            nonlocal ncat, nrt
            tg = tl // 4
            if tl % 4 == 0:
                cb_ = ncat % len(cat_sb)
                ncat += 1
                s1_gen.cb = cb_
                if cat_loader is None:
                    P.dma_("sp", cat_sb[cb_][:], catv[:, :, t0 + tg * 512:t0 + (tg + 1) * 512], writes=[("cat", cb_)])
                else:
                    cat_loader(cat_sb[cb_], ("cat", cb_), ps * 2 + tg)
            cb = s1_gen.cb
            ri = nrt % 3
            nrt += 1
            tok = t0 + tl * 128
            P.dma_("act", xr[ri][:], xres[tok:tok + 128, :], writes=[("xr", ri)])
            for half in range(2):
                b = C.bank("mm", [0, 1, 2, 3])
                for kc in range(12):
                    P.mm(C.banks[b][:], cat_sb[cb][:, kc, (tl % 4) * 128:(tl % 4 + 1) * 128], wo[:, kc, half * 512:(half + 1) * 512],
                         kc == 0, kc == 11, reads=[("cat", cb), ("wo", kc // 4)], banks=[b])
                P.stt("dve", rt[ri][:, half * 512:(half + 1) * 512], xr[ri][:, half * 512:(half + 1) * 512], ALPHA, C.banks[b][:],
                      ALU.mult, ALU.add, reads=[("xr", ri)], writes=[("rt", ri, half)], banks=[b])
            yield
            yield from ln_gen(C, rt[ri][:], [("rt", ri, 0), ("rt", ri, 1)], xr[ri][:], ("xr", ri), lnp[:, 0, :], lnp[:, 1, :],
                              [("lnp", 0), ("lnp", 1)], scr[ri], ("ln", ri))
            P.act(yacc[:, tl, :], xr[ri][:], AF.Copy, scale=ALPHA, reads=[("xr", ri)], writes=[("yacc", tl, 0), ("yacc", tl, 1)])
            P.copy("dve", xb[ri][:], xr[ri][:], reads=[("xr", ri)], writes=[("xb", ri)])
            yield
            transpose_tile_bf16(C, xb[ri], ("xb", ri), xmT[:, :, tl * 128:(tl + 1) * 128], ("xmT", tl), idb, 8, [4, 5])

        run_window([s1_gen(tl) for tl in range(NT)], 3)

        NG = dbg.get('ngroups', 8)

        def ld_w1(g):
            P.dma_("pool", w1g[(gbase + g) % 2][:], w1v[:, :, g * 512:(g + 1) * 512], writes=[("w1g", (gbase + g) % 2)])

        def ld_w2(g):
            P.dma_("pool", w2g[(gbase + g) % 2][:], w2v[:, g * 4:(g + 1) * 4, :], writes=[("w2g", (gbase + g) % 2)])

        def ff1(g):
            w3, h2 = (gbase + g) % 2, (gbase + g) % 2
            for fc in range(4):
                for tg in range(2):
                    b = C.bank("mm", [0, 1, 2, 3])
                    for kc in range(8):
                        P.mm(C.banks[b][:], w1g[w3][:, kc, fc * 128:(fc + 1) * 128], xmT[:, kc, tg * 512:(tg + 1) * 512], kc == 0, kc == 7,
                             reads=[("w1g", w3)] + [("xmT", tg * 4 + j) for j in range(4)], banks=[b])
                    rb = C.bank("relu", [0, 1])
                    P.act(relu_t[rb][:], C.banks[b][:], AF.Relu, writes=[("relu", rb)], banks=[b])
                    P.tt("pool", hT[h2][:, fc, tg * 512:(tg + 1) * 512], relu_t[rb][:], relu_t[rb][:], ALU.mult,
                         reads=[("relu", rb)], writes=[("hT", h2, fc, tg)])

        def ff2(g):
            w3, h2 = (gbase + g) % 2, (gbase + g) % 2
            for tl in range(NT):
                for half in range(2):
                    b = C.bank("mm2", [4, 5, 6, 7])
                    for fc in range(4):
                        P.mm(C.banks[b][:], hT[h2][:, fc, tl * 128:(tl + 1) * 128], w2g[w3][:, fc, half * 512:(half + 1) * 512], fc == 0, fc == 3,
                             reads=[("w2g", w3), ("hT", h2, fc, tl // 4)], banks=[b])
                    P.tt("dve", yacc[:, tl, half * 512:(half + 1) * 512], yacc[:, tl, half * 512:(half + 1) * 512], C.banks[b][:], ALU.add,
                         reads=[("yacc", tl, half)], writes=[("yacc", tl, half)], banks=[b])

        gbase = gcount
        gcount += NG
        if NG > 0:
            ld_w1(0)
            ld_w2(0)
            if NG > 1:
                ld_w1(1)
                ld_w2(1)
            ff1(0)
            for g in range(NG):
                if g + 2 < NG:
                    ld_w1(g + 2)
                if g + 1 < NG:
                    ff1(g + 1)
                ff2(g)
                if g + 2 < NG:
                    ld_w2(g + 2)
        def s3_gen(tl):
            nonlocal nrt
            ri = nrt % 3
            nrt += 1
            tok = t0 + tl * 128
            yield from ln_gen(C, yacc[:, tl, :], [("yacc", tl, 0), ("yacc", tl, 1)], xr[ri][:], ("xr", ri), lnp[:, 2, :], lnp[:, 3, :],
                              [("lnp", 2), ("lnp", 3)], scr[ri], ("ln", ri))
            outs.append(P.dma_("sp", x_out[tok:tok + 128, :], xr[ri][:], reads=[("xr", ri)]))
            if xT_out is not None or xT_dst is not None:
                P.copy("dve", xb[ri][:], xr[ri][:], reads=[("xr", ri)], writes=[("xb", ri)])
                yield
                transpose_tile_bf16(C, xb[ri], ("xb", ri), xmT[:, :, tl * 128:(tl + 1) * 128], ("xmT", tl), idb, 8, [4, 5])
                if xT_dst is None:
                    dstT = xT_out.rearrange("(kc p) t -> p kc t", p=128)[:, :, tok:tok + 128]
                else:
                    dstT = xT_dst(tok)
                outs.append(P.dma_("act", dstT, xmT[:, :, tl * 128:(tl + 1) * 128], reads=[("xmT", tl)], writes=[("xsend", tok)]))

        run_window([s3_gen(tl) for tl in range(NT)], 3)
        if dbg.get('after_pass'):
            dbg['after_pass'](ps)
    return outs


def ln_gen(C, r, rkeys, out, okey, g_rep, b_rep, gkeys, scr, tag):
    P = C.P
    st, mv, sd, rstd, nmr = scr["stats"], scr["mv"], scr["sd"], scr["rstd"], scr["nmr"]
    k = lambda n: (tag, n)
    P.op("dve", lambda E: E.bn_stats(out=st[:, 0:6], in_=r[:, 0:512]), reads=rkeys, writes=[k("st0")])
    P.op("dve", lambda E: E.bn_stats(out=st[:, 6:12], in_=r[:, 512:1024]), reads=rkeys, writes=[k("st1")])
    yield
    P.op("dve", lambda E: E.bn_aggr(out=mv[:, 0:2], in_=st[:, 0:12]), reads=[k("st0"), k("st1")], writes=[k("mv")])
    yield
    P.op("act", lambda E: E.activation(out=sd[:], in_=mv[:, 1:2], func=AF.Sqrt, bias=LN_EPS, scale=1.0),
         reads=[k("mv")], writes=[k("sd")])
    yield
    P.op("dve", lambda E: E.reciprocal(out=rstd[:], in_=sd[:]), reads=[k("sd")], writes=[k("rstd")])
    yield
    P.op("dve", lambda E: E.scalar_tensor_tensor(out=nmr[:], in0=mv[:, 0:1], scalar=-1.0, in1=rstd[:],
                                                 op0=ALU.mult, op1=ALU.mult),
         reads=[k("mv"), k("rstd")], writes=[k("nmr")])
    yield
    P.op("act", lambda E: E.activation(out=out, in_=r, func=AF.Identity, bias=nmr[:], scale=rstd[:]),
         reads=list(rkeys) + [k("rstd"), k("nmr")], writes=[okey])
    yield
    P.op("dve", lambda E: E.tensor_tensor(out=out, in0=out, in1=g_rep, op=ALU.mult), reads=[okey, gkeys[0]], writes=[okey])
    yield
    P.op("dve", lambda E: E.tensor_tensor(out=out, in0=out, in1=b_rep, op=ALU.add), reads=[okey, gkeys[1]], writes=[okey])
    yield


def _ln_two_keys(C, r, rkeys, out, okey, g_rep, b_rep, gkeys, scr, tag):
    for _ in ln_gen(C, r, rkeys, out, okey, g_rep, b_rep, gkeys, scr, tag):
        pass


def run_window(gen_list, width):
    pending = list(gen_list)
    active = []
    while pending or active:
        while pending and len(active) < width:
            active.append(pending.pop(0))
        for g_ in list(active):
            try:
                next(g_)
            except StopIteration:
                active.remove(g_)


def make_ffn_program(dbg=None):
    nc = bass.Bass("TRN2", target_bir_lowering=False)
    dt = lambda n, s, d, k: nc.dram_tensor(n, s, d, kind=k).ap()
    catT = dt("catT", [1536, 2048], BF16, "ExternalInput")
    xres = dt("xres", [2048, 1024], F32, "ExternalInput")
    w_out = dt("w_out", [1536, 1024], F32, "ExternalInput")
    w_ff1 = dt("w_ff1", [1024, 4096], F32, "ExternalInput")
    w_ff2 = dt("w_ff2", [4096, 1024], F32, "ExternalInput")
    lns = [dt(n, [128, 1024], F32, "ExternalInput") for n in ("ln1g", "ln1b", "ln2g", "ln2b")]
    ident = dt("ident", [128, 128], F32, "ExternalInput")
    x_out = dt("x_out", [2048, 1024], F32, "ExternalOutput")
    xT_out = dt("xT_out", [1024, 2048], BF16, "ExternalOutput")
    C = Ctx(nc)
    with ExitStack() as st:
        outs = build_ffn_phase(C, catT, xres, w_out, w_ff1, w_ff2, *lns, ident, x_out, (None if (dbg or {}).get('noxt') else xT_out), st, dbg)
        with ExitStack() as st2:
            stats = C.P.emit(st2, final_wait_ops=outs)
    return nc, stats


def load_xT_group(C, xT_dram, xT_sb, is_f32, tg):
    v = xT_dram.rearrange("(kc p) t -> p kc t", p=128)
    sl = tg % 4
    C.P.dma_("pool" if is_f32 else ("sp" if tg % 2 == 0 else "act"),
             xT_sb[:, :, sl * 512:(sl + 1) * 512], v[:, :, tg * 512:(tg + 1) * 512], writes=[("xT", sl)])


def proj_fm(C, w_sb, wkey, col0, xT_sb, tg, banks_ids):
    P = C.P
    b = C.bank("pf", banks_ids)
    for kc in range(8):
        P.mm(C.banks[b][:], w_sb[:, kc, col0:col0 + 128], xT_sb[:, kc, (tg % 4) * 512:(tg % 4 + 1) * 512], kc == 0, kc == 7,
             reads=[wkey, ("xT", tg % 4)], banks=[b])
    return b


def build_mem_kv(C, mem, lng, lnb, wmk_d, wmv_d, idb, sbt, stack):
    nc, P = C.nc, C.P
    KmT = sbt("KmT", [128, 2, 256], BF16)
    Vm = sbt("Vm", [128, 2, 256], BF16)
    with ExitStack() as st:
        sb = lambda name, shape, dt: st.enter_context(nc.sbuf_tensor(C.pfx + name, shape, dt))
        mt_ = [sb(f"memt{i}", [128, 1024], F32) for i in range(2)]
        mo = [sb(f"memo{i}", [128, 1024], F32) for i in range(2)]
        mb = [sb(f"memb{i}", [128, 1024], BF16) for i in range(2)]
        lnp = sb("mlnp", [128, 2, 1024], F32)
        memT = sb("memT", [128, 8, 256], BF16)
        wmk = sb("wmk_sb", [128, 8, 256], BF16)
        wmv = sb("wmv_sb", [128, 8, 256], BF16)
        scr = [dict(stats=sb(f"mst{i}", [128, 12], F32), mv=sb(f"mmv{i}", [128, 2], F32), sd=sb(f"msd{i}", [128, 1], F32),
                    rstd=sb(f"mrs{i}", [128, 1], F32), nmr=sb(f"mnm{i}", [128, 1], F32), xn=sb(f"mxn{i}", [128, 1024], F32))
               for i in range(2)]
        P.dma_("sp", lnp[:, 0, :], lng, writes=[("mlnp", 0)])
        P.dma_("sp", lnp[:, 1, :], lnb, writes=[("mlnp", 1)])
        P.dma_("pool", wmk[:], wmk_d.rearrange("(kc p) n -> p kc n", p=128), writes=["wmk"])
        P.dma_("pool", wmv[:], wmv_d.rearrange("(kc p) n -> p kc n", p=128), writes=["wmv"])
        for i in range(2):
            P.dma_("act", mt_[i][:], mem[i * 128:(i + 1) * 128, :], writes=[("memt", i)])
            _ln_two_keys(C, mt_[i][:], [("memt", i)], mo[i][:], ("memo", i), lnp[:, 0, :], lnp[:, 1, :],
                         [("mlnp", 0), ("mlnp", 1)], scr[i], ("mln", i))
            P.copy("dve", mb[i][:], mo[i][:], reads=[("memo", i)], writes=[("memb", i)])
            transpose_tile_bf16(C, mb[i], ("memb", i), memT[:, :, i * 128:(i + 1) * 128], ("memT", i), idb, 8, [6, 7])
        for h in range(2):
            b = C.bank("mkv", [4, 5])
            for kc in range(8):
                P.mm(C.banks[b][:, 0:256], wmk[:, kc, h * 128:(h + 1) * 128], memT[:, kc, :], kc == 0, kc == 7,
                     reads=["wmk", ("memT", 0), ("memT", 1)], banks=[b])
            P.copy("dve", KmT[:, h, :], C.banks[b][:, 0:256], writes=[("KmT", h)], banks=[b])
        for mt in range(2):
            b = C.bank("mkv", [4, 5])
            for kc in range(8):
                P.mm(C.banks[b][:, 0:256], memT[:, kc, mt * 128:(mt + 1) * 128], wmv[:, kc, :], kc == 0, kc == 7,
                     reads=["wmv", ("memT", mt)], banks=[b])
            P.copy("act", Vm[:, mt, :], C.banks[b][:, 0:256], writes=[("Vm", mt)], banks=[b])
        P.barrier()
    return KmT, Vm


def _cat_dst(catT_out, k, tg):
    if callable(catT_out):
        return catT_out(k, tg)
    return catT_out[k * 128:(k + 1) * 128, tg * 512:(tg + 1) * 512]


def mem_attention_group(C, KmT, Vm, ones_bf, mqT, mqkey, out_tile, okey, pT_bufs, rden, tagi):
    P = C.P
    h = tagi
    pk = []
    for mt in range(2):
        b = C.bank("ms", [4, 5])
        P.mm(C.banks[b][:], KmT[:, h, mt * 128:(mt + 1) * 128], mqT, True, True, reads=[("KmT", h), mqkey], banks=[b])
        pi = C.bank("mpT", list(range(len(pT_bufs))))
        P.act(pT_bufs[pi][:], C.banks[b][:], AF.Exp, writes=[("mpT", pi)], banks=[b])
        pk.append(pi)
    bo = C.bank("mo", [6])
    bd = C.bank("md", [7])
    for mt in range(2):
        P.mm(C.banks[bo][:], Vm[:, mt, h * 128:(h + 1) * 128], pT_bufs[pk[mt]][:], mt == 0, mt == 1,
             reads=[("Vm", mt), ("mpT", pk[mt])], banks=[bo])
    for mt in range(2):
        P.mm(C.banks[bd][:], ones_bf[:], pT_bufs[pk[mt]][:], mt == 0, mt == 1,
             reads=["ones_bf", ("mpT", pk[mt])], banks=[bd])
    P.act(rden[:], C.banks[bd][:], AF.Ln, writes=["mrden0"], banks=[bd])
    P.act(rden[:], rden[:], AF.Exp, scale=-1.0, reads=["mrden0"], writes=["mrden"])
    P.tt("dve", out_tile, C.banks[bo][:], rden[:], ALU.mult, reads=["mrden"], writes=[okey], banks=[bo])


LAM_INIT1 = 0.8 - 0.6 * float(np.exp(-0.3))


def build_diff_phase(C, xT_d, x_is_f32, mem, mlng, mlnb, wmk_d, wmv_d, wq_d, wk_d, wv_d, wmq_d, lamv, subw, masks_d,
                     ident, catT_out, stack, dbg=None):
    nc, P = C.nc, C.P
    dbg = dbg or {}
    sbt = lambda name, shape, dt: stack.enter_context(nc.sbuf_tensor(C.pfx + name, shape, dt))
    outs = []
    idf = sbt("idf", [128, 128], F32)
    idb = sbt("idb", [128, 128], BF16)
    ones_bf = sbt("ones_bf", [128, 128], BF16)
    ones_f = sbt("ones_f", [128, 128], F32)
    P.dma_("sp", idf[:], ident, writes=["ident_f"])
    P.copy("dve", idb[:], idf[:], reads=["ident_f"], writes=["ident_bf"])
    P.op("dve", lambda E: E.memset(ones_bf[:], 1.0), writes=["ones_bf"])
    P.op("dve", lambda E: E.memset(ones_f[:], 1.0), writes=["ones_f"])
    KmT, Vm = build_mem_kv(C, mem, mlng, mlnb, wmk_d, wmv_d, idb, sbt, stack)

    qT = sbt("qT", [128, 4, T], BF16)
    kT = sbt("kT", [128, 4, T], BF16)
    vtok = sbt("vtok", [128, 32, 512], BF16)
    mqT = sbt("mqT", [128, 2, T], BF16)
    masks = sbt("masks_sb", [128, 4, 512], BF16)
    P.dma_("pool", masks[:], masks_d.rearrange("r p q -> p r q"), writes=["masks"])
    lv = sbt("lv", [128, 4, 64], F32)
    lprod = sbt("lprod", [128, 2, 64], F32)
    lsum = sbt("lsum", [128, 2], F32)
    lexp = sbt("lexp", [128, 2], F32)
    nlam = sbt("nlam", [128, 1], F32)
    swc = sbt("swc", [128, 2], F32)
    P.dma_("sp", lv[:], lamv, writes=["lv"])
    P.dma_("sp", swc[:, 0:1], subw, writes=["swc0"])
    P.tt("dve", lprod[:, 0, :], lv[:, 0, :], lv[:, 1, :], ALU.mult, reads=["lv"], writes=["lprod0"])
    P.tt("dve", lprod[:, 1, :], lv[:, 2, :], lv[:, 3, :], ALU.mult, reads=["lv"], writes=["lprod1"])
    P.op("dve", lambda E: E.reduce_sum(out=lsum[:, 0:1], in_=lprod[:, 0, :], axis=mybir.AxisListType.X), reads=["lprod0"], writes=["lsum0"])
    P.op("dve", lambda E: E.reduce_sum(out=lsum[:, 1:2], in_=lprod[:, 1, :], axis=mybir.AxisListType.X), reads=["lprod1"], writes=["lsum1"])
    P.act(lexp[:], lsum[:], AF.Exp, reads=["lsum0", "lsum1"], writes=["lexp"])
    P.tt("dve", nlam[:], lexp[:, 1:2], lexp[:, 0:1], ALU.subtract, reads=["lexp"], writes=["nlam0"])
    P.ts("dve", nlam[:], nlam[:], -LAM_INIT1, ALU.add, reads=["nlam0"], writes=["nlam"])
    P.ts("dve", swc[:, 1:2], swc[:, 0:1], 1.0 - LAM_INIT1, ALU.mult, reads=["swc0"], writes=["swc"])

    with ExitStack() as st:
        sb = lambda name, shape, dt: st.enter_context(nc.sbuf_tensor(C.pfx + name, shape, dt))
        xT_sb = sb("xT_sb", [128, 8, 2048], BF16)
        wq = sb("wq_sb", [128, 8, 512], BF16)
        wk = sb("wk_sb", [128, 8, 512], BF16)
        wv = sb("wv_sb", [128, 8, 512], BF16)
        wmq = sb("wmq_sb", [128, 8, 256], BF16)
        for wsb, wd, key in ((wq, wq_d, "wq"), (wk, wk_d, "wk"), (wv, wv_d, "wv"), (wmq, wmq_d, "wmq")):
            P.dma_("pool", wsb[:], wd.rearrange("(kc p) n -> p kc n", p=128), writes=[key])
        for tg in range(4):
            load_xT_group(C, xT_d, xT_sb, x_is_f32, tg)
        for tg in range(8):
            if tg >= 4:
                load_xT_group(C, xT_d, xT_sb, x_is_f32, tg)
            for h in range(4):
                b = proj_fm(C, wq, "wq", h * 128, xT_sb, tg, [0, 1, 2, 3])
                P.act(qT[:, h, tg * 512:(tg + 1) * 512], C.banks[b][:], AF.Copy, scale=0.125, writes=[("qT", h, tg)], banks=[b])
                b = proj_fm(C, wk, "wk", h * 128, xT_sb, tg, [0, 1, 2, 3])
                P.copy("dve", kT[:, h, tg * 512:(tg + 1) * 512], C.banks[b][:], writes=[("kT", h, tg)], banks=[b])
            for h in range(2):
                b = proj_fm(C, wmq, "wmq", h * 128, xT_sb, tg, [0, 1, 2, 3])
                P.act(mqT[:, h, tg * 512:(tg + 1) * 512], C.banks[b][:], AF.Copy, scale=128.0 ** -0.5, writes=[("mqT", h, tg)], banks=[b])
            for tl in range(4):
                t = tg * 4 + tl
                b = C.bank("pf", [0, 1, 2, 3])
                for kc in range(8):
                    P.mm(C.banks[b][:], xT_sb[:, kc, (t % 16) * 128:(t % 16 + 1) * 128], wv[:, kc, :], kc == 0, kc == 7,
                         reads=["wv", ("xT", tg % 4)], banks=[b])
                P.copy("dve" if tl % 2 else "act", vtok[:, t, :], C.banks[b][:], writes=[("vtok", t)], banks=[b])
        P.op("dve", lambda E: E.memset(idf[:, 0:1], 1.0) if False else E.memset(ones_f[:, 0:1], 1.0),
             reads=[], writes=[("xT", g) for g in range(4)] + ["wq", "wk", "wv", "wmq", "ones_f"])
        C.free_key = [("xT", g) for g in range(4)] + ["wq", "wk", "wv", "wmq"]
        P.barrier()

    pT = [sbt(f"pT{i}", [128, 512], BF16) for i in range(6)]
    mpT = [sbt(f"mpT{i}", [128, 512], BF16) for i in range(4)]
    rden = sbt("rden", [128, 512], F32)
    r12 = [sbt(f"r12_{i}", [128, 512], F32) for i in range(2)]
    o12 = [sbt(f"o12_{i}", [128, 512], F32) for i in range(2)]
    od = sbt("od", [128, 512], F32)
    sq = sbt("sq", [128, 512], F32)
    rs = sbt("rs", [128, 512], F32)
    cat_t = [sbt(f"cat_t{i}", [128, 512], BF16) for i in range(3)]
    fk = C.free_key
    catv = catT_out
    nq = dbg.get("nqg", 8)
    for qg in range(nq):
        qs = slice(qg * 512, (qg + 1) * 512)
        for h in range(4):
            nkt = 4 * (qg + 1)

            def score(kt):
                pis = []
                for half in range(2):
                    b = C.bank("sT", [4, 5, 6, 7])
                    hp = slice(half * 64, (half + 1) * 64)
                    P.mm(C.banks[b][:], kT[hp, h, kt * 128:(kt + 1) * 128], qT[hp, h, qs], True, True,
                         reads=[("kT", h, kt // 4), ("qT", h, qg)], banks=[b])
                    pi = C.bank("pT", list(range(6)))
                    P.act(pT[pi][:], C.banks[b][:], AF.Exp, reads=fk, writes=[("pT", pi)], banks=[b])
                    if kt >= 4 * qg:
                        r = kt - 4 * qg
                        P.tt("pool", pT[pi][:], pT[pi][:], masks[:, r, :], ALU.mult, reads=["masks", ("pT", pi)], writes=[("pT", pi)])
                    pis.append(pi)
                return pis

            nxt = score(0)
            for kt in range(nkt):
                pis = nxt
                if kt + 1 < nkt:
                    nxt = score(kt + 1)
                for half in range(2):
                    P.mm(C.banks[half][:], vtok[:, kt, h * 128:(h + 1) * 128], pT[pis[half]][:], kt == 0, kt == nkt - 1,
                         reads=[("vtok", kt), ("pT", pis[half])], banks=[half])
                    P.mm(C.banks[2 + half][:], ones_bf[:], pT[pis[half]][:], kt == 0, kt == nkt - 1,
                         reads=["ones_bf", ("pT", pis[half])], banks=[2 + half])
            for half in range(2):
                P.act(r12[half][:], C.banks[2 + half][:], AF.Ln, reads=fk, writes=[("r12", half)], banks=[2 + half])
                P.act(r12[half][:], r12[half][:], AF.Exp, scale=-1.0, reads=[("r12", half)], writes=[("r12", half)])
                P.tt("dve", o12[half][:], C.banks[half][:], r12[half][:], ALU.mult, reads=[("r12", half)] + fk,
                     writes=[("o12", half)], banks=[half])
            P.stt("dve", od[:], o12[1][:], nlam[:], o12[0][:], ALU.mult, ALU.add, reads=[("o12", 0), ("o12", 1), "nlam"] + fk, writes=["od"])
            P.act(sq[:], od[:], AF.Square, reads=["od"] + fk, writes=["sq"])
            b = C.bank("sT", [4, 5, 6, 7])
            P.mm(C.banks[b][:], ones_f[:], sq[:], True, True, reads=["ones_f", "sq"], banks=[b])
            P.act(rs[:], C.banks[b][:], AF.Ln, bias=RMS_EPS, scale=1.0 / 128.0, reads=fk, writes=["rs0"], banks=[b])
            P.act(rs[:], rs[:], AF.Exp, scale=-0.5, reads=["rs0"], writes=["rs"])
            ci = C.bank("cat_t", [0, 1, 2])
            P.stt("dve", cat_t[ci][:], od[:], swc[:, 1:2], rs[:], ALU.mult, ALU.mult, reads=["od", "swc", "rs"] + fk, writes=[("cat_t", ci)])
            outs.append(P.dma_("sp", _cat_dst(catv, h, qg), cat_t[ci][:], reads=[("cat_t", ci)], writes=[("catdst", h, qg)]))
        for h in range(2):
            ci = C.bank("cat_t", [0, 1, 2])
            mem_attention_group(C, KmT, Vm, ones_bf, mqT[:, h, qs], ("mqT", h, qg), cat_t[ci][:], ("cat_t", ci), mpT, rden, h)
            outs.append(P.dma_("act", _cat_dst(catv, 4 + h, qg), cat_t[ci][:], reads=[("cat_t", ci)], writes=[("catdst", 4 + h, qg)]))
        if dbg.get("after_seg"):
            dbg["after_seg"](qg)
    return outs


def make_diff_program(x_is_f32=False, dbg=None):
    nc = bass.Bass("TRN2", target_bir_lowering=False)
    dt = lambda n, s, d, k="ExternalInput": nc.dram_tensor(n, s, d, kind=k).ap()
    xT_d = dt("xT", [1024, T], F32 if x_is_f32 else BF16)
    mem = dt("mem", [256, 1024], F32)
    mlng = dt("mlng", [128, 1024], F32)
    mlnb = dt("mlnb", [128, 1024], F32)
    wmk = dt("wmk", [1024, 256], F32)
    wmv = dt("wmv", [1024, 256], F32)
    wq = dt("wq", [1024, 512], F32)
    wk = dt("wk", [1024, 512], F32)
    wv = dt("wv", [1024, 512], F32)
    wmq = dt("wmq", [1024, 256], F32)
    lamv = dt("lamv", [128, 4, 64], F32)
    subw = dt("subw", [128, 1], F32)
    masks = dt("masks", [4, 128, 512], F32)
    ident = dt("ident", [128, 128], F32)
    catT_out = dt("catT_out", [768, T], BF16, "ExternalOutput")
    C = Ctx(nc)
    with ExitStack() as st:
        outs = build_diff_phase(C, xT_d, x_is_f32, mem, mlng, mlnb, wmk, wmv, wq, wk, wv, wmq, lamv, subw, masks, ident,
                                catT_out, st, dbg)
        with ExitStack() as st2:
            stats = C.P.emit(st2, final_wait_ops=outs)
    return nc, stats


def build_gdn_phase(C, xT_d, x_is_f32, mem, mlng, mlnb, wmk_d, wmv_d, wq_d, wk_d, wv_d, wz_d, wab_d, wmq_d,
                    convw_d, alog_d, dtb_d, gw_d, cm_d, ident, catT_out, stack, dbg=None):
    nc, P = C.nc, C.P
    dbg = dbg or {}
    sbt = lambda name, shape, dt: stack.enter_context(nc.sbuf_tensor(C.pfx + name, shape, dt))
    outs = []
    idf = sbt("idf", [128, 128], F32)
    idb = sbt("idb", [128, 128], BF16)
    ones_bf = sbt("ones_bf", [128, 128], BF16)
    ones_f = sbt("ones_f", [128, 128], F32)
    P.dma_("sp", idf[:], ident, writes=["ident_f"])
    P.copy("dve", idb[:], idf[:], reads=["ident_f"], writes=["ident_bf"])
    P.op("dve", lambda E: E.memset(ones_bf[:], 1.0), writes=["ones_bf"])
    P.op("dve", lambda E: E.memset(ones_f[:], 1.0), writes=["ones_f"])
    KmT, Vm = build_mem_kv(C, mem, mlng, mlnb, wmk_d, wmv_d, idb, sbt, stack)

    cm = sbt("cm_sb", [128, 6, 128], F32)
    P.dma_("sp", cm[:], cm_d.rearrange("r p q -> p r q"), writes=["cm"])
    TRI, SUT, MSL, MUI, ONA, ONB = [cm[:, i, :] for i in range(6)]
    convw = sbt("convw_sb", [128, 12, 4], F32)
    P.dma_("sp", convw[:], convw_d, writes=["convw"])
    alog = sbt("alog_sb", [128, 4], F32)
    dtb = sbt("dtb_sb", [128, 4], F32)
    negA = sbt("negA", [128, 4], F32)
    gw = sbt("gw_sb", [128, 128], F32)
    P.dma_("act", alog[:], alog_d, writes=["alog"])
    P.dma_("act", dtb[:], dtb_d, writes=["dtb"])
    P.dma_("act", gw[:], gw_d, writes=["gw"])
    P.act(negA[:], alog[:], AF.Exp, reads=["alog"], writes=["negA0"])
    P.ts("dve", negA[:], negA[:], -1.0, ALU.mult, reads=["negA0"], writes=["negA"])

    xT_sb = sbt("xT_sb", [128, 8, 2048], BF16)
    wsb = {}
    for n, wd, cols in (("wq", wq_d, 512), ("wk", wk_d, 512), ("wv", wv_d, 512), ("wz", wz_d, 512), ("wab", wab_d, 8), ("wmq", wmq_d, 256)):
        wsb[n] = sbt(n + "_sb", [128, 8, cols], BF16)
        P.dma_("pool", wsb[n][:], wd.rearrange("(kc p) n -> p kc n", p=128), writes=[n])
    pre = [sbt(f"pre{c}", [128, 515], F32) for c in range(12)]
    cv = [sbt(f"cv{i}", [128, 512], F32) for i in range(4)]
    sl = [sbt(f"sl{i}", [128, 512], F32) for i in range(4)]
    sqb = [sbt(f"sqb{i}", [128, 512], F32) for i in range(4)]
    rnb = [sbt(f"rnb{i}", [128, 512], F32) for i in range(4)]
    qT = sbt("gqT", [128, 4, 512], BF16)
    kT = sbt("gkT", [128, 4, 512], BF16)
    vT = sbt("gvT", [128, 4, 512], BF16)
    sz = sbt("sz", [128, 4, 512], BF16)
    mqT = sbt("gmqT", [128, 2, 512], BF16)
    catseg = sbt("catseg", [128, 6, 512], BF16)
    S = [sbt(f"S{h}", [128, 128], F32) for h in range(4)]
    Sb = [sbt(f"Sb{h}", [128, 128], BF16) for h in range(4)]
    for h in range(4):
        P.op("dve", lambda E, h=h: E.memset(S[h][:], 0.0), writes=[("S", h)])
        P.op("dve", lambda E, h=h: E.memset(Sb[h][:], 0.0), writes=[("Sb", h)])
    for c in range(12):
        P.op("pool", lambda E, c=c: E.memset(pre[c][:, 0:3], 0.0), writes=[("pre", c)])
    sc_sets = [{n: sbt(f"sc{j}_" + n, [128, 4], F32) for n in ("beta", "y", "ey", "sp", "g", "gc", "gam", "dca", "dcb", "tmp", "kdec", "bg")}
               for j in range(4)]
    NB2 = 4
    wt = {}
    for n, dt_ in (("gtri", F32), ("Ds", F32), ("DmT", F32), ("L0", F32), ("L1", F32), ("U0", F32), ("U1", F32),
                   ("R0", F32), ("R1", F32), ("Av", F32), ("o", F32), ("og", F32)):
        wt[n] = [sbt(f"wt_{n}{i}", [128, 128], dt_) for i in range(NB2)]
    for n in ("AT", "Rb", "kbg", "kd", "vb", "nwT", "vn", "og2"):
        wt[n] = [sbt(f"wt_{n}{i}", [128, 128], BF16) for i in range(8 if n in ("AT", "Rb", "kd", "vb", "nwT") else NB2)]
    gss = [sbt(f"gss{i}", [128, 1], F32) for i in range(NB2)]
    grs = [sbt(f"grs{i}", [128, 1], F32) for i in range(NB2)]
    junk = [sbt(f"junk{i}", [128, 128], F32) for i in range(NB2)]
    mpT = [sbt(f"mpT{i}", [128, 512], BF16) for i in range(4)]
    rden = sbt("rden", [128, 512], F32)

    def small_bank(pool="sm"):
        return C.bank(pool, [0, 1, 2, 3, 4, 5, 6, 7])

    nseg = dbg.get("nseg", 8)
    for tg in range(nseg):
        load_xT_group(C, xT_d, xT_sb, x_is_f32, tg)
        slot = tg % 4
        def s2_chain(c):
            qkv, h = c // 4, c % 4
            wn = ("wq", "wk", "wv")[qkv]
            if tg > 0:
                P.copy("pool", pre[c][:, 0:3], pre[c][:, 512:515], reads=[("pre", c)], writes=[("pre", c)])
                yield
            b = proj_fm(C, wsb[wn], wn, h * 128, xT_sb, tg, [0, 1, 2, 3, 4, 5, 6, 7])
            P.copy("act", pre[c][:, 3:515], C.banks[b][:], reads=[("pre", c)], writes=[("pre", c)], banks=[b])
            yield
            ci = c % 4
            ce = "dve"
            P.ts(ce, cv[ci][:], pre[c][:, 3:515], convw[:, c, 3:4], ALU.mult, reads=[("pre", c), "convw"], writes=[("cv", ci)])
            yield
            for j in (2, 1, 0):
                P.stt(ce, cv[ci][:], pre[c][:, j:j + 512], convw[:, c, j:j + 1], cv[ci][:], ALU.mult, ALU.add,
                      reads=[("pre", c), ("cv", ci), "convw"], writes=[("cv", ci)])
                yield
            if qkv == 2:
                P.act(vT[:, h, :], cv[ci][:], AF.Silu, reads=[("cv", ci)], writes=[("vT", h)])
                yield
            else:
                si = c % 4
                P.act(sl[si][:], cv[ci][:], AF.Silu, reads=[("cv", ci)], writes=[("sl", si)])
                yield
                P.act(sqb[si][:], sl[si][:], AF.Square, reads=[("sl", si)], writes=[("sqb", si)])
                yield
                b2 = C.bank("pf", [0, 1, 2, 3, 4, 5, 6, 7])
                P.mm(C.banks[b2][:], ones_f[:], sqb[si][:], True, True, reads=["ones_f", ("sqb", si)], banks=[b2])
                yield
                if qkv == 0:
                    P.act(rnb[si][:], C.banks[b2][:], AF.Ln, bias=128.0 * RMS_EPS, scale=128.0, writes=[("rnb", si)], banks=[b2])
                else:
                    P.act(rnb[si][:], C.banks[b2][:], AF.Ln, bias=RMS_EPS, scale=1.0, writes=[("rnb", si)], banks=[b2])
                P.act(rnb[si][:], rnb[si][:], AF.Exp, scale=-0.5, reads=[("rnb", si)], writes=[("rnb", si)])
                yield
                dst = qT if qkv == 0 else kT
                P.tt("pool", dst[:, h, :], sl[si][:], rnb[si][:], ALU.mult, reads=[("sl", si), ("rnb", si)],
                     writes=[(("qT", "kT")[qkv], h)])
                yield
        for w_ in range(3):
            gens = [s2_chain(c) for c in range(w_ * 4, w_ * 4 + 4)]
            while gens:
                for g_ in list(gens):
                    try:
                        next(g_)
                    except StopIteration:
                        gens.remove(g_)
        for h in range(2):
            b = proj_fm(C, wsb["wmq"], "wmq", h * 128, xT_sb, tg, [0, 1])
            P.act(mqT[:, h, :], C.banks[b][:], AF.Copy, scale=128.0 ** -0.5, writes=[("mqT", h)], banks=[b])
        def scal(tl):
            ts_ = slice(tl * 128, (tl + 1) * 128)
            xs = slice(slot * 512 + tl * 128, slot * 512 + (tl + 1) * 128)
            sc = sc_sets[tl]
            sk = lambda n, _p=tl: (n, _p)
            b = C.bank("pf", [0, 1])
            for kc in range(8):
                P.mm(C.banks[b][:], xT_sb[:, kc, xs], wsb["wz"][:, kc, :], kc == 0, kc == 7, reads=["wz", ("xT", slot)], banks=[b])
            P.act(sz[:, tl, :], C.banks[b][:], AF.Silu, writes=[("sz", tl)], banks=[b])
            yield
            b = small_bank()
            pab = C.banks[b][:, 0:8]
            for kc in range(8):
                P.mm(pab, xT_sb[:, kc, xs], wsb["wab"][:, kc, :], kc == 0, kc == 7, reads=["wab", ("xT", slot)], banks=[b])
            P.act(sc["beta"][:], C.banks[b][:, 4:8], AF.Sigmoid, writes=[sk("beta")], banks=[b])
            yield
            P.tt("dve", sc["y"][:], C.banks[b][:, 0:4], dtb[:], ALU.add, reads=["dtb"], writes=[sk("y")], banks=[b])
            yield
            P.act(sc["ey"][:], sc["y"][:], AF.Exp, reads=[sk("y")], writes=[sk("ey")])
            yield
            P.act(sc["sp"][:], sc["ey"][:], AF.Ln, bias=1.0, reads=[sk("ey")], writes=[sk("sp")])
            yield
            P.tt("dve", sc["g"][:], sc["sp"][:], negA[:], ALU.mult, reads=[sk("sp"), "negA"], writes=[sk("g")])
            yield
            b = small_bank()
            pb = C.banks[b]
            P.mm(pb[:, 0:4], TRI, sc["g"][:], True, True, reads=["cm", sk("g")], banks=[b])
            yield
            P.mm(pb[:, 8:12], ONA, sc["g"][:], True, True, reads=["cm", sk("g")], banks=[b])
            yield
            P.mm(pb[:, 16:20], ONB, sc["g"][:], True, True, reads=["cm", sk("g")], banks=[b])
            yield
            P.copy("dve", sc["gc"][:], pb[:, 0:4], writes=[sk("gc")], banks=[b])
            yield
            P.act(sc["gam"][:], pb[:, 0:4], AF.Exp, writes=[sk("gam")], banks=[b])
            yield
            P.act(sc["dca"][:], pb[:, 8:12], AF.Exp, writes=[sk("dca")], banks=[b])
            yield
            P.act(sc["dcb"][:], pb[:, 16:20], AF.Exp, writes=[sk("dcb")], banks=[b])
            yield
            P.tt("dve", sc["tmp"][0:64, :], pb[0:64, 8:12], sc["gc"][0:64, :], ALU.subtract, reads=[sk("gc")], writes=[sk("tmpa")], banks=[b])
            yield
            P.tt("dve", sc["tmp"][64:128, :], pb[64:128, 16:20], sc["gc"][64:128, :], ALU.subtract, reads=[sk("gc")], writes=[sk("tmpb")], banks=[b])
            yield
            P.act(sc["kdec"][:], sc["tmp"][:], AF.Exp, reads=[sk("tmpa"), sk("tmpb")], writes=[sk("kdec")])
            yield
            P.tt("dve", sc["bg"][:], sc["beta"][:], sc["gam"][:], ALU.mult, reads=[sk("beta"), sk("gam")], writes=[sk("bg")])
            yield
        gens = [scal(tl) for tl in range(4)]
        while gens:
            for g_ in list(gens):
                try:
                    next(g_)
                except StopIteration:
                    gens.remove(g_)
        HAND = ("AT", "Rb", "kd", "vb", "nwT")

        def pre_gen(tl, h):
                ts_ = slice(tl * 128, (tl + 1) * 128)
                sc = sc_sets[tl]
                sk = lambda n, _p=tl: (n, _p)
                i = h
                ih = (tl % 2) * 4 + h
                W = {n: (wt[n][ih] if n in HAND else wt[n][h]) for n in wt}
                k_ = lambda n: (n, ih if n in HAND else h)
                kt_ap = kT[:, h, ts_]
                qt_ap = qT[:, h, ts_]
                P.ts("dve", W["gtri"][:], TRI, sc["g"][:, h:h + 1], ALU.mult, reads=["cm", sk("g")], writes=[k_("gtri")])
                bE = small_bank()
                P.mm(C.banks[bE][:, 0:128], W["gtri"][:], SUT, True, True, reads=[k_("gtri"), "cm"], banks=[bE])
                P.mm(C.banks[bE][:, 128:256], SUT, W["gtri"][:], True, True, reads=[k_("gtri"), "cm"], banks=[bE])
                P.act(W["Ds"][:], C.banks[bE][:, 0:128], AF.Exp, writes=[k_("Ds")], banks=[bE])
                P.act(W["DmT"][:], C.banks[bE][:, 128:256], AF.Exp, writes=[k_("DmT")], banks=[bE])
                P.tt("pool", W["Ds"][:], W["Ds"][:], MSL, ALU.mult, reads=[k_("Ds"), "cm"], writes=[k_("Ds")])
                P.tt("pool", W["DmT"][:], W["DmT"][:], MUI, ALU.mult, reads=[k_("DmT"), "cm"], writes=[k_("DmT")])
                yield
                bK = small_bank()
                P.mm(C.banks[bK][:, 0:128], kt_ap, kt_ap, True, True, reads=[("kT", h)], banks=[bK])
                P.mm(C.banks[bK][:, 128:256], kt_ap, qt_ap, True, True, reads=[("kT", h), ("qT", h)], banks=[bK])
                P.stt("dve", W["L0"][:], C.banks[bK][:, 0:128], sc["beta"][:, h:h + 1], W["Ds"][:], ALU.mult, ALU.mult,
                      reads=[sk("beta"), k_("Ds")], writes=[k_("L0")], banks=[bK])
                P.tt("dve", W["AT"][:], C.banks[bK][:, 128:256], W["DmT"][:], ALU.mult, reads=[k_("DmT")], writes=[k_("AT")], banks=[bK])
                yield
                bU = small_bank()
                P.tr(C.banks[bU][:, 0:128], W["L0"][:], idf[:], reads=[k_("L0"), "ident_f"], banks=[bU])
                P.copy("act", W["U0"][:], C.banks[bU][:, 0:128], writes=[k_("U0")], banks=[bU])
                P.tt("dve", W["R0"][:], idf[:], W["U0"][:], ALU.subtract, reads=["ident_f", k_("U0")], writes=[k_("R0")])
                yield
                Lc, Uc, Rc = "L0", "U0", "R0"
                for it in range(5):
                    Ln_, Un_, Rn_ = ("L1", "U1", "R1") if Lc == "L0" else ("L0", "U0", "R0")
                    bN = small_bank()
                    P.mm(C.banks[bN][:, 0:128], W[Uc][:], W[Lc][:], True, True, reads=[k_(Uc), k_(Lc)], banks=[bN])
                    if it < 4:
                        P.mm(C.banks[bN][:, 128:256], W[Lc][:], W[Uc][:], True, True, reads=[k_(Uc), k_(Lc)], banks=[bN])
                    P.copy("act", W[Ln_][:], C.banks[bN][:, 0:128], writes=[k_(Ln_)], banks=[bN])
                    if it < 4:
                        P.copy("dve", W[Un_][:], C.banks[bN][:, 128:256], writes=[k_(Un_)], banks=[bN])
                    bR = small_bank()
                    P.mm(C.banks[bR][:, 0:128], W[Ln_][:], W[Rc][:], True, True, reads=[k_(Ln_), k_(Rc)], banks=[bR])
                    P.tt("dve", W[Rn_][:], C.banks[bR][:, 0:128], W[Rc][:], ALU.add, reads=[k_(Rc)], writes=[k_(Rn_)], banks=[bR])
                    yield
                    Lc, Uc, Rc = Ln_, Un_, Rn_
                P.copy("act", W["Rb"][:], W[Rc][:], reads=[k_(Rc)], writes=[k_("Rb")])
                bT = small_bank()
                pbt = C.banks[bT][:].bitcast(BF16)
                P.tr(pbt[:, 0:128], kt_ap, idb[:], reads=[("kT", h), "ident_bf"], banks=[bT])
                P.tr(pbt[:, 128:256], vT[:, h, ts_], idb[:], reads=[("vT", h), "ident_bf"], banks=[bT])
                P.ts("dve", W["kbg"][:], pbt[:, 0:128], sc["bg"][:, h:h + 1], ALU.mult, reads=[sk("bg")], writes=[k_("kbg")], banks=[bT])
                P.act(W["kd"][:], pbt[:, 0:128], AF.Identity, scale=sc["kdec"][:, h:h + 1], reads=[sk("kdec")], writes=[k_("kd")], banks=[bT])
                P.ts("dve", W["vb"][:], pbt[:, 128:256], sc["beta"][:, h:h + 1], ALU.mult, reads=[sk("beta")], writes=[k_("vb")], banks=[bT])
                yield
                bW = small_bank()
                P.mm(C.banks[bW][:, 0:128], W["kbg"][:], W["Rb"][:], True, True, reads=[k_("kbg"), k_("Rb")], banks=[bW])
                P.act(W["nwT"][:], C.banks[bW][:, 0:128], AF.Copy, scale=-1.0, writes=[k_("nwT")], banks=[bW])
                yield

        def run_gen(tl, h):
                ts_ = slice(tl * 128, (tl + 1) * 128)
                sc = sc_sets[tl]
                sk = lambda n, _p=tl: (n, _p)
                i = h
                ih = (tl % 2) * 4 + h
                W = {n: (wt[n][ih] if n in HAND else wt[n][h]) for n in wt}
                k_ = lambda n: (n, ih if n in HAND else h)
                kt_ap = kT[:, h, ts_]
                qt_ap = qT[:, h, ts_]
                for half in range(2):
                    rows = slice(half * 64, (half + 1) * 64)
                    M = 64 if half == 0 else 128
                    dc = sc["dca"] if half == 0 else sc["dcb"]
                    bV = small_bank()
                    P.mm(C.banks[bV][0:M, 0:128], W["Rb"][rows, 0:M], W["vb"][rows, :], True, False,
                         reads=[k_("Rb"), k_("vb")], banks=[bV])
                    P.mm(C.banks[bV][0:M, 0:128], W["nwT"][:, 0:M], Sb[h][:], False, True,
                         reads=[k_("nwT"), ("Sb", h)], banks=[bV])
                    P.copy("act", W["vn"][rows, :], C.banks[bV][rows, 0:128], writes=[(k_("vn"), half)], banks=[bV])
                    yield
                    bO = small_bank()
                    P.mm(C.banks[bO][0:M, 0:128], qt_ap[:, 0:M], Sb[h][:], True, True, reads=[("qT", h), ("Sb", h)], banks=[bO])
                    P.mm(C.banks[bO][0:M, 128:256], W["AT"][rows, 0:M], W["vn"][rows, :], True, True,
                         reads=[k_("AT"), (k_("vn"), half)], banks=[bO])
                    P.copy("act", W["Av"][rows, :], C.banks[bO][rows, 128:256], writes=[(k_("Av"), half)], banks=[bO])
                    P.stt("dve", W["o"][rows, :], C.banks[bO][rows, 0:128], sc["gam"][rows, h:h + 1], W["Av"][rows, :], ALU.mult, ALU.add,
                          reads=[sk("gam"), (k_("Av"), half)], writes=[(k_("o"), half)], banks=[bO])
                    yield
                    bS = small_bank()
                    P.mm(C.banks[bS][:, 0:128], W["kd"][rows, :], W["vn"][rows, :], True, True,
                         reads=[k_("kd"), (k_("vn"), half)], banks=[bS])
                    P.stt("dve", S[h][:], S[h][:], dc[:, h:h + 1], C.banks[bS][:, 0:128], ALU.mult, ALU.add,
                          reads=[("S", h), sk("dca"), sk("dcb")], writes=[("S", h)], banks=[bS])
                    P.copy("act", Sb[h][:], S[h][:], reads=[("S", h)], writes=[("Sb", h)])
                    yield
                okeys = [(k_("o"), 0), (k_("o"), 1)]
                P.op("pool", lambda E, i=i: E.memset(gss[i][:], 0.0), writes=[("gss", i)])
                P.act(junk[i][:], W["o"][:], AF.Square, accum_out=gss[i][:], reads=okeys + [("gss", i)], writes=[("junk", i), ("gss", i)])
                P.act(grs[i][:], gss[i][:], AF.Sqrt, bias=RMS_EPS, scale=1.0 / 128.0, reads=[("gss", i)], writes=[("grs", i)])
                P.op("dve", lambda E, i=i: E.reciprocal(out=grs[i][:], in_=grs[i][:]), reads=[("grs", i)], writes=[("grs", i)])
                P.stt("dve", W["og"][:], W["o"][:], grs[i][:], gw[:], ALU.mult, ALU.mult, reads=okeys + [("grs", i), "gw"], writes=[k_("og")])
                yield
                P.tt("pool", W["og2"][:], W["og"][:], sz[:, tl, h * 128:(h + 1) * 128], ALU.mult, reads=[k_("og"), ("sz", tl)], writes=[k_("og2")])
                bG = small_bank()
                pbg = C.banks[bG][:].bitcast(BF16)
                P.tr(pbg[:, 0:128], W["og2"][:], idb[:], reads=[k_("og2"), "ident_bf"], banks=[bG])
                P.copy("act", catseg[:, h, ts_], pbg[:, 0:128], writes=[("catseg", h, tl)], banks=[bG])

        for step in range(5):
            gens = []
            if step >= 1:
                gens += [run_gen(step - 1, h) for h in range(4)]
            if step < 4:
                gens += [pre_gen(step, h) for h in range(4)]
            while gens:
                for g_ in list(gens):
                    try:
                        next(g_)
                    except StopIteration:
                        gens.remove(g_)
        qs = slice(tg * 512, (tg + 1) * 512)
        for h in range(4):
            outs.append(P.dma_("sp", _cat_dst(catT_out, h, tg), catseg[:, h, :], reads=[("catseg", h, tl) for tl in range(4)], writes=[("catdst", h, tg)]))
        for h in range(2):
            mem_attention_group(C, KmT, Vm, ones_bf, mqT[:, h, :], ("mqT", h), catseg[:, 4 + h, :], ("catseg", 4 + h), mpT, rden, h)
            outs.append(P.dma_("act", _cat_dst(catT_out, 4 + h, tg), catseg[:, 4 + h, :], reads=[("catseg", 4 + h)], writes=[("catdst", 4 + h, tg)]))
        if dbg.get("after_seg"):
            dbg["after_seg"](tg)
    return outs


def make_gdn_program(x_is_f32=True, dbg=None):
    nc = bass.Bass("TRN2", target_bir_lowering=False)
    dt = lambda n, s, d, k="ExternalInput": nc.dram_tensor(n, s, d, kind=k).ap()
    xT_d = dt("xT", [1024, T], F32 if x_is_f32 else BF16)
    mem = dt("mem", [256, 1024], F32)
    mlng = dt("mlng", [128, 1024], F32)
    mlnb = dt("mlnb", [128, 1024], F32)
    wmk = dt("wmk", [1024, 256], F32)
    wmv = dt("wmv", [1024, 256], F32)
    wq = dt("wq", [1024, 512], F32)
    wk = dt("wk", [1024, 512], F32)
    wv = dt("wv", [1024, 512], F32)
    wz = dt("wz", [1024, 512], F32)
    wab = dt("wab", [1024, 8], F32)
    wmq = dt("wmq", [1024, 256], F32)
    convw = dt("convw", [128, 12, 4], F32)
    alog = dt("alog", [128, 4], F32)
    dtb = dt("dtb", [128, 4], F32)
    gw = dt("gw", [128, 128], F32)
    cm = dt("cm", [6, 128, 128], F32)
    ident = dt("ident", [128, 128], F32)
    catT_out = dt("catT_out", [768, T], BF16, "ExternalOutput")
    C = Ctx(nc)
    with ExitStack() as st:
        outs = build_gdn_phase(C, xT_d, x_is_f32, mem, mlng, mlnb, wmk, wmv, wq, wk, wv, wz, wab, wmq, convw, alog, dtb, gw,
                               cm, ident, catT_out, st, dbg)
        with ExitStack() as st2:
            stats = C.P.emit(st2, final_wait_ops=outs)
    return nc, stats


def gdn_const_masks():
    i = np.arange(128)
    same = (i[:, None] // 64) == (i[None, :] // 64)
    tri = ((i[:, None] <= i[None, :]) & same)
    sut = (i[:, None] > i[None, :])
    msl = ((i[:, None] > i[None, :]) & same)
    mui = ((i[:, None] <= i[None, :]) & same)
    ona = np.broadcast_to((i[:, None] < 64), (128, 128))
    onb = np.broadcast_to((i[:, None] >= 64), (128, 128))
    return np.stack([tri, sut, msl, mui, ona, onb]).astype(np.float32)


def _rep(v, n=128):
    v = np.asarray(v, np.float32)
    return np.ascontiguousarray(np.broadcast_to(v[None, :], (n, v.shape[0])))


def _c(a):
    return np.ascontiguousarray(a)


def _diff_masks():
    masks = np.zeros((4, 128, 512), np.float32)
    for r in range(4):
        masks[r] = (128 * r + np.arange(128)[:, None] <= np.arange(512)[None, :]).astype(np.float32)
    return masks


def _mem_inputs(inp, b, hh):
    w = inp["w_mem_kv"]
    return dict(mem=_c(inp["mem"][b]), mlng=_rep(inp["mem_ln_g"]), mlnb=_rep(inp["mem_ln_b"]),
                wmk=_c(w[:, hh * 256:hh * 256 + 256]), wmv=_c(w[:, 512 + hh * 256:512 + hh * 256 + 256]),
                ident=np.eye(128, dtype=np.float32))


def _gdn_inputs(inp, b, hh):
    w_in = inp["l0_w_in"]
    conv_w = inp["l0_conv_w"]
    convw = np.zeros((128, 12, 4), np.float32)
    for qkv in range(3):
        for h in range(4):
            ch0 = qkv * 1024 + (hh * 4 + h) * 128
            convw[:, qkv * 4 + h, :] = conv_w[:, ch0:ch0 + 128].T
    ab = np.concatenate([w_in[:, 4096 + hh * 4:4096 + hh * 4 + 4], w_in[:, 4104 + hh * 4:4104 + hh * 4 + 4]], axis=1)
    d = _mem_inputs(inp, b, hh)
    d.update(xT=_c(inp["x"][b].T),
             wq=_c(w_in[:, hh * 512:hh * 512 + 512]), wk=_c(w_in[:, 1024 + hh * 512:1024 + hh * 512 + 512]),
             wv=_c(w_in[:, 2048 + hh * 512:2048 + hh * 512 + 512]), wz=_c(w_in[:, 3072 + hh * 512:3072 + hh * 512 + 512]),
             wab=_c(ab), wmq=_c(w_in[:, 4112 + hh * 256:4112 + hh * 256 + 256]),
             convw=convw, alog=_rep(inp["l0_a_log"][hh * 4:hh * 4 + 4]), dtb=_rep(inp["l0_dt_bias"][hh * 4:hh * 4 + 4]),
             gw=_rep(inp["l0_gate_norm_w"]), cm=gdn_const_masks())
    return d


def _diff_inputs(inp, b, hh, xT_bf):
    w_in = inp["l1_w_in"]
    d = _mem_inputs(inp, b, hh)
    lam = np.stack([inp["l1_lambda_q1"], inp["l1_lambda_k1"], inp["l1_lambda_q2"], inp["l1_lambda_k2"]]).astype(np.float32)
    d.update(xT=xT_bf,
             wq=_c(w_in[:, hh * 512:hh * 512 + 512]), wk=_c(w_in[:, 1024 + hh * 512:1024 + hh * 512 + 512]),
             wv=_c(w_in[:, 2048 + hh * 512:2048 + hh * 512 + 512]), wmq=_c(w_in[:, 3072 + hh * 256:3072 + hh * 256 + 256]),
             lamv=_c(np.broadcast_to(lam[None], (128, 4, 64))), subw=_c(np.asarray(inp["l1_subln_w"], np.float32)[:, None]),
             masks=_diff_masks())
    return d


def _ffn_inputs(inp, layer, catT_pair, th, xres):
    p = f"l{layer}_"
    c0, c1 = catT_pair
    ts = slice(th * 2048, (th + 1) * 2048)
    catT = np.concatenate([c0[0:512, ts], c1[0:512, ts], c0[512:768, ts], c1[512:768, ts]], axis=0)
    return dict(catT=_c(catT), xres=_c(xres), w_out=_c(inp[p + "w_out"]), w_ff1=_c(inp[p + "w_ff1"]), w_ff2=_c(inp[p + "w_ff2"]),
                ln1g=_rep(inp[p + "ln1_g"]), ln1b=_rep(inp[p + "ln1_b"]), ln2g=_rep(inp[p + "ln2_g"]), ln2b=_rep(inp[p + "ln2_b"]),
                ident=np.eye(128, dtype=np.float32))


def kernel(**inputs):
    inp = {k: np.asarray(v) for k, v in inputs.items()}
    cores = list(range(8))
    ncA, _ = make_gdn_program(x_is_f32=True)
    resA = run_bass_kernel_spmd(ncA, [_gdn_inputs(inp, c // 2, c % 2) for c in cores], core_ids=cores)
    catA = [np.asarray(r["catT_out"]) for r in resA.results]
    ncB, _ = make_ffn_program()
    inB = [_ffn_inputs(inp, 0, (catA[2 * (c // 2)], catA[2 * (c // 2) + 1]), c % 2,
                       inp["x"][c // 2, (c % 2) * 2048:(c % 2 + 1) * 2048, :]) for c in cores]
    resB = run_bass_kernel_spmd(ncB, inB, core_ids=cores)
    x1 = [np.asarray(r["x_out"]) for r in resB.results]
    x1T = [np.asarray(r["xT_out"]) for r in resB.results]
    ncC, _ = make_diff_program(x_is_f32=False)
    inC = []
    for c in cores:
        b = c // 2
        xT_bf = _c(np.concatenate([x1T[2 * b], x1T[2 * b + 1]], axis=1))
        inC.append(_diff_inputs(inp, b, c % 2, xT_bf))
    resC = run_bass_kernel_spmd(ncC, inC, core_ids=cores)
    catC = [np.asarray(r["catT_out"]) for r in resC.results]
    ncD, _ = make_ffn_program()
    inD = [_ffn_inputs(inp, 1, (catC[2 * (c // 2)], catC[2 * (c // 2) + 1]), c % 2, x1[c]) for c in cores]
    resD = run_bass_kernel_spmd(ncD, inD, core_ids=cores)
    out = np.zeros((NB, T, D), np.float32)
    for c in cores:
        out[c // 2, (c % 2) * 2048:(c % 2 + 1) * 2048, :] = np.asarray(resD.results[c]["x_out"])
    return out


def make_fused_program(n_cores=8, dbg=None):
    dbg = dbg or {}
    nc = bass.Bass("TRN2", target_bir_lowering=False)
    dt = lambda n, s, d, k="ExternalInput": nc.dram_tensor(n, s, d, kind=k).ap()
    groups = [[2 * i, 2 * i + 1] for i in range(n_cores // 2)]
    ident = dt("ident", [128, 128], F32)
    sel_d = dt("sel", [128, 2], F32)
    mem = dt("mem", [256, 1024], F32)
    mlng = dt("mlng", [128, 1024], F32)
    mlnb = dt("mlnb", [128, 1024], F32)
    wmk = dt("wmk", [1024, 256], F32)
    wmv = dt("wmv", [1024, 256], F32)
    a_xT = dt("a_xT", [1024, T], F32)
    a_w = {n: dt("a_" + n, [1024, c], F32) for n, c in (("wq", 512), ("wk", 512), ("wv", 512), ("wz", 512), ("wab", 8), ("wmq", 256))}
    a_convw = dt("a_convw", [128, 12, 4], F32)
    a_alog = dt("a_alog", [128, 4], F32)
    a_dtb = dt("a_dtb", [128, 4], F32)
    a_gw = dt("a_gw", [128, 128], F32)
    a_cm = dt("a_cm", [6, 128, 128], F32)
    c_w = {n: dt("c_" + n, [1024, c], F32) for n, c in (("wq", 512), ("wk", 512), ("wv", 512), ("wmq", 256))}
    c_lamv = dt("c_lamv", [128, 4, 64], F32)
    c_subw = dt("c_subw", [128, 1], F32)
    c_masks = dt("c_masks", [4, 128, 512], F32)
    f_in = {}
    for L in ("b", "d"):
        f_in[L] = dict(w_out=dt(L + "_w_out", [1536, 1024], F32), w_ff1=dt(L + "_w_ff1", [1024, 4096], F32),
                       w_ff2=dt(L + "_w_ff2", [4096, 1024], F32),
                       lns=[dt(L + "_" + n, [128, 1024], F32) for n in ("ln1g", "ln1b", "ln2g", "ln2b")])
    b_xres = dt("b_xres", [2048, 1024], F32)
    out_d = dt("out", [2048, 1024], F32, "ExternalOutput")
    it = lambda n, s, d: nc.dram_tensor(n, s, d).ap()
    cat_send = [it(f"cat_send{i}", [24 * 128, 1024], BF16) for i in range(2)]
    cat_recv = [it(f"cat_recv{i}", [24 * 256, 1024], BF16) for i in range(2)]
    x_send = it("x_send", [16 * 128, 1024], BF16)
    x_recv = it("x_recv", [16 * 256, 1024], BF16)
    x1_d = it("x1_d", [2048, 1024], F32)

    C = Ctx(nc)
    P = C.P

    def cat_dst_fn(i):
        v = cat_send[i].rearrange("(k q p) t -> k q p t", q=4, p=128)
        return lambda k, tg: v[k, tg // 2, :, (tg % 2) * 512:(tg % 2 + 1) * 512]

    def finish_exchange():
        P.coll_wait()
        P.barrier()

    def cat_after_seg(i):
        def f(tg):
            if tg % 2 == 1:
                q = tg // 2
                for k in range(6):
                    c = k * 4 + q
                    P.coll("AllGather", groups, cat_send[i][c * 128:(c + 1) * 128, :], cat_recv[i][c * 256:(c + 1) * 256, :],
                           reads=[("catdst", k, tg - 1), ("catdst", k, tg)], wait=False)
        return f

    def x_after_pass(ps):
        for kc in range(8):
            c = kc * 2 + ps
            P.coll("AllGather", groups, x_send[c * 128:(c + 1) * 128, :], x_recv[c * 256:(c + 1) * 256, :],
                   reads=[("xsend", ps * 1024 + j * 128) for j in range(8)], wait=False)

    def make_cat_loader(i, selt, tmpA, tmpB):
        rv = cat_recv[i].rearrange("(k q r p) t -> p r k q t", q=4, r=2, p=128)

        def loader(dst, key, g):
            off = (g % 2) * 512
            for r in range(2):
                P.dma_("sp", tmpA[:], rv[:, r, :, g // 2, off:off + 512], writes=["catA"])
                P.dma_("act", tmpB[:], rv[:, r, :, 2 + g // 2, off:off + 512], writes=["catB"])
                P.ts("dve", tmpB[:], tmpB[:], selt[:, 1:2], ALU.mult, reads=["catB", "sel"], writes=["catB"])
                P.stt("dve", dst[:, r * 6:(r + 1) * 6, :], tmpA[:], selt[:, 0:1], tmpB[:], ALU.mult, ALU.add,
                      reads=["catA", "catB", "sel"], writes=[key])
        return loader

    xs_v = x_send.rearrange("(kc q p) t -> p kc q t", q=2, p=128)

    def xT_dst(tok):
        return xs_v[:, :, tok // 1024, tok % 1024:tok % 1024 + 128]

    C.pfx = "A_"
    with ExitStack() as st:
        build_gdn_phase(C, a_xT, True, mem, mlng, mlnb, wmk, wmv, a_w["wq"], a_w["wk"], a_w["wv"], a_w["wz"], a_w["wab"], a_w["wmq"],
                        a_convw, a_alog, a_dtb, a_gw, a_cm, ident, cat_dst_fn(0), st, dict(dbg, after_seg=cat_after_seg(0)))
        finish_exchange()
    C.pfx = "B_"
    with ExitStack() as st:
        selt = st.enter_context(nc.sbuf_tensor("B_sel", [128, 2], F32))
        tmpA = st.enter_context(nc.sbuf_tensor("B_tmpA", [128, 6, 512], BF16))
        tmpB = st.enter_context(nc.sbuf_tensor("B_tmpB", [128, 6, 512], BF16))
        P.dma_("sp", selt[:], sel_d, writes=["sel"])
        d2 = dict(dbg)
        d2.update(cat_loader=make_cat_loader(0, selt, tmpA, tmpB), xT_dst=xT_dst, after_pass=x_after_pass)
        fi = f_in["b"]
        build_ffn_phase(C, None, b_xres, fi["w_out"], fi["w_ff1"], fi["w_ff2"], *fi["lns"], ident, x1_d, None, st, d2)
        finish_exchange()
    C.pfx = "C_"
    xr_v = x_recv.rearrange("(kc q r p) t -> p kc q r t", q=2, r=2, p=128)

    class _XT:
        def rearrange(self, *_a, **_k):
            return self

        def __getitem__(self, idx):
            tsl = idx[2]
            tg = tsl.start // 512
            r, l = tg // 4, (tg % 4)
            return xr_v[:, :, l // 2, r, (l % 2) * 512:(l % 2 + 1) * 512]

    with ExitStack() as st:
        build_diff_phase(C, _XT(), False, mem, mlng, mlnb, wmk, wmv, c_w["wq"], c_w["wk"], c_w["wv"], c_w["wmq"], c_lamv, c_subw,
                         c_masks, ident, cat_dst_fn(1), st, dict(dbg, after_seg=cat_after_seg(1)))
        finish_exchange()
    C.pfx = "D_"
    with ExitStack() as st:
        selt = st.enter_context(nc.sbuf_tensor("D_sel", [128, 2], F32))
        tmpA = st.enter_context(nc.sbuf_tensor("D_tmpA", [128, 6, 512], BF16))
        tmpB = st.enter_context(nc.sbuf_tensor("D_tmpB", [128, 6, 512], BF16))
        P.dma_("sp", selt[:], sel_d, writes=["sel"])
        d2 = dict(dbg)
        d2.update(cat_loader=make_cat_loader(1, selt, tmpA, tmpB), xT_dst=None)
        fi = f_in["d"]
        outs = build_ffn_phase(C, None, x1_d, fi["w_out"], fi["w_ff1"], fi["w_ff2"], *fi["lns"], ident, out_d, None, st, d2)
        with ExitStack() as st2:
            stats = P.emit(st2, final_wait_ops=outs)
    return nc, stats


def _perm_w_out(w):
    idx = []
    for r in range(2):
        idx += list(range(r * 512, r * 512 + 512))
        idx += list(range(1024 + r * 256, 1024 + r * 256 + 256))
    return np.ascontiguousarray(w[np.asarray(idx)])


def _fused_inputs(inp, c):
    b, hh = c // 2, c % 2
    g = _gdn_inputs(inp, b, hh)
    d = dict(ident=g["ident"], mem=g["mem"], mlng=g["mlng"], mlnb=g["mlnb"], wmk=g["wmk"], wmv=g["wmv"])
    sel = np.zeros((128, 2), np.float32)
    sel[:, hh] = 1.0
    d["sel"] = sel
    d["a_xT"] = g["xT"]
    for n in ("wq", "wk", "wv", "wz", "wab", "wmq"):
        d["a_" + n] = g[n]
    d.update(a_convw=g["convw"], a_alog=g["alog"], a_dtb=g["dtb"], a_gw=g["gw"], a_cm=g["cm"])
    w1 = inp["l1_w_in"]
    d.update(c_wq=_c(w1[:, hh * 512:hh * 512 + 512]), c_wk=_c(w1[:, 1024 + hh * 512:1024 + hh * 512 + 512]),
             c_wv=_c(w1[:, 2048 + hh * 512:2048 + hh * 512 + 512]), c_wmq=_c(w1[:, 3072 + hh * 256:3072 + hh * 256 + 256]))
    lam = np.stack([inp["l1_lambda_q1"], inp["l1_lambda_k1"], inp["l1_lambda_q2"], inp["l1_lambda_k2"]]).astype(np.float32)
    d.update(c_lamv=_c(np.broadcast_to(lam[None], (128, 4, 64))), c_subw=_c(np.asarray(inp["l1_subln_w"], np.float32)[:, None]),
             c_masks=_diff_masks())
    for L, layer in (("b", 0), ("d", 1)):
        p = f"l{layer}_"
        d[L + "_w_out"] = _perm_w_out(inp[p + "w_out"])
        d[L + "_w_ff1"] = _c(inp[p + "w_ff1"])
        d[L + "_w_ff2"] = _c(inp[p + "w_ff2"])
        for n, k in (("ln1g", "ln1_g"), ("ln1b", "ln1_b"), ("ln2g", "ln2_g"), ("ln2b", "ln2_b")):
            d[L + "_" + n] = _rep(inp[p + k])
    d["b_xres"] = _c(inp["x"][b, hh * 2048:(hh + 1) * 2048, :])
    return d


def kernel_unfused(**inputs):
    return _kernel_unfused(**inputs)


_kernel_unfused = kernel


def kernel(**inputs):
    inp = {k: np.asarray(v) for k, v in inputs.items()}
    cores = list(range(8))
    nc, _ = make_fused_program(8)
    res = run_bass_kernel_spmd(nc, [_fused_inputs(inp, c) for c in cores], core_ids=cores)
    out = np.zeros((NB, T, D), np.float32)
    for c in cores:
        out[c // 2, (c % 2) * 2048:(c % 2 + 1) * 2048, :] = np.asarray(res.results[c]["out"])
    return out
```

```python
from contextlib import ExitStack
import numpy as np
import ml_dtypes
import concourse.bass as bass
import concourse.mybir as mybir
from concourse.bass_utils import run_bass_kernel_spmd

F32 = mybir.dt.float32
BF16 = mybir.dt.bfloat16
F32R = mybir.dt.float32r
AF = mybir.ActivationFunctionType
ALU = mybir.AluOpType
NPBF = ml_dtypes.bfloat16

D = 1024
T = 4096
NB = 4
DFF = 4096
ALPHA = 4.0 ** 0.25
LN_EPS = 1e-5
RMS_EPS = 1e-6

SAME_ENGINE_SYNC = {"pe": False, "act": True, "dve": True, "pool": True, "sp": False}
N_DMA_SEMS = 64
N_HW_SEMS = 44


class Op:
    __slots__ = ("eng", "fn", "deps", "inc", "val", "sem", "is_dma")

    def __init__(self, eng, fn, deps, is_dma=False):
        self.eng = eng
        self.fn = fn
        self.deps = deps
        self.inc = False
        self.val = None
        self.sem = None
        self.is_dma = is_dma


class Prog:
    def __init__(self, nc):
        self.nc = nc
        self.engs = {"pe": nc.tensor, "act": nc.scalar, "dve": nc.vector,
                     "pool": nc.gpsimd, "sp": nc.sync}
        self.ops = []
        self.last_w = {}
        self.readers = {}
        self.bank_last = {}
        self.dma_rr = 0
        self.dma_rr_sw = 0
        self.cc_sems = []
        self.dma_last = [None] * N_DMA_SEMS
        self.uid = 0

    def _deps_for(self, eng, reads, writes, banks):
        deps = []
        for k in reads:
            w = self.last_w.get(k)
            if w is not None:
                deps.append(w)
        for k in writes:
            w = self.last_w.get(k)
            if w is not None:
                deps.append(w)
            deps.extend(self.readers.get(k, ()))
        for b in banks:
            for e, o in self.bank_last.get(b, {}).items():
                if e != eng:
                    deps.append(o)
        return deps

    def _record(self, op, reads, writes, banks):
        for k in reads:
            self.readers.setdefault(k, []).append(op)
        for k in writes:
            self.last_w[k] = op
            self.readers[k] = []
        for b in banks:
            self.bank_last.setdefault(b, {})[op.eng] = op
        self.ops.append(op)

    def op(self, eng, fn, reads=(), writes=(), banks=()):
        o = Op(eng, fn, self._deps_for(eng, reads, writes, banks))
        self._record(o, reads, writes, banks)
        return o

    def dma(self, eng, fn, reads=(), writes=()):
        deps = self._deps_for(eng, reads, writes, ())
        if eng == "pool":
            i = N_HW_SEMS + self.dma_rr_sw
            self.dma_rr_sw = (self.dma_rr_sw + 1) % (N_DMA_SEMS - N_HW_SEMS)
        else:
            i = self.dma_rr
            self.dma_rr = (i + 1) % N_HW_SEMS
        if self.dma_last[i] is not None:
            deps.append(self.dma_last[i])
        o = Op(eng, fn, deps, is_dma=True)
        o.sem = i
        o.inc = True
        self.dma_last[i] = o
        self._record(o, reads, writes, ())
        return o

    def barrier(self):
        last = {}
        for o in self.ops:
            if not o.is_dma:
                last[o.eng] = o
        dmas = [d for d in self.dma_last if d is not None]
        for e in self.engs:
            deps = [last[x] for x in last if x != e] + dmas
            self.ops.append(Op(e, lambda E: E.nop(), deps))

    def coll_wait(self):
        def f(E):
            if self.cc_sems:
                E.wait_ge(self.cc_sems[0], self.cc_count)
            return E.nop()
        return self.op("pool", f)

    def coll(self, kind, groups, in_ap, out_ap, reads=(), writes=(), wait=True):
        nc = self.nc
        def f(E):
            if not self.cc_sems:
                self.cc_sems.append(nc.alloc_semaphore(name="s_cc"))
                self.cc_count = 0
            sem = self.cc_sems[0]
            self.cc_count += 1
            E.collective_compute(kind, ALU.bypass, replica_groups=groups, ins=[in_ap.opt()], outs=[out_ap.opt()]).then_inc(sem)
            if wait:
                E.wait_ge(sem, self.cc_count)
            return E.nop()
        return self.op("pool", f, reads, writes)

    def dma_(self, eng, out, in_, reads=(), writes=()):
        return self.dma(eng, lambda E: E.dma_start(out=out, in_=in_), reads, writes)

    def mm(self, out, lhsT, rhs, start, stop, reads=(), banks=()):
        return self.op("pe", lambda E: E.matmul(out, lhsT=lhsT, rhs=rhs, start=start, stop=stop), reads, (), banks)

    def tr(self, out, in_, ident, reads=(), banks=()):
        return self.op("pe", lambda E: E.transpose(out, in_, ident), reads, (), banks)

    def act(self, out, in_, func, bias=0.0, scale=1.0, accum_out=None, reads=(), writes=(), banks=()):
        if accum_out is None:
            f = lambda E: E.activation(out=out, in_=in_, func=func, bias=bias, scale=scale)
        else:
            f = lambda E: E.activation(out=out, in_=in_, func=func, bias=bias, scale=scale, accum_out=accum_out)
        return self.op("act", f, reads, writes, banks)

    def copy(self, eng, out, in_, reads=(), writes=(), banks=()):
        if eng == "act":
            return self.op("act", lambda E: E.copy(out=out, in_=in_), reads, writes, banks)
        return self.op(eng, lambda E: E.tensor_copy(out=out, in_=in_), reads, writes, banks)

    def tt(self, eng, out, in0, in1, op, reads=(), writes=(), banks=()):
        return self.op(eng, lambda E: E.tensor_tensor(out=out, in0=in0, in1=in1, op=op), reads, writes, banks)

    def ts(self, eng, out, in0, s1, op0, s2=None, op1=None, reads=(), writes=(), banks=()):
        if op1 is None:
            f = lambda E: E.tensor_scalar(out=out, in0=in0, scalar1=s1, scalar2=None, op0=op0)
        else:
            f = lambda E: E.tensor_scalar(out=out, in0=in0, scalar1=s1, scalar2=s2, op0=op0, op1=op1)
        return self.op(eng, f, reads, writes, banks)

    def stt(self, eng, out, in0, scalar, in1, op0, op1, reads=(), writes=(), banks=()):
        return self.op(eng, lambda E: E.scalar_tensor_tensor(out=out, in0=in0, scalar=scalar, in1=in1, op0=op0, op1=op1),
                       reads, writes, banks)

    def emit(self, stack, final_wait_ops=()):
        nc = self.nc
        for o in self.ops:
            for d in o.deps:
                if d.is_dma:
                    continue
                if d.eng == o.eng and not SAME_ENGINE_SYNC[o.eng]:
                    continue
                d.inc = True
        for d in final_wait_ops:
            d.inc = True
        esem = {e: stack.enter_context(nc.semaphore(f"s_{e}")) for e in self.engs}
        dsem = [stack.enter_context(nc.semaphore(f"s_dma{i}")) for i in range(N_DMA_SEMS)]
        cnt = {e: 0 for e in self.engs}
        dcnt = [0] * N_DMA_SEMS
        for o in self.ops:
            if o.is_dma:
                dcnt[o.sem] += 16
                o.val = dcnt[o.sem]
                assert o.val <= 96, 'DMA semaphore value limit (device faults above ~100)'
            elif o.inc:
                cnt[o.eng] += 1
                o.val = cnt[o.eng]
        waited = {e: {} for e in self.engs}
        nwaits = 0
        for o in self.ops:
            E = self.engs[o.eng]
            need = {}
            for d in o.deps:
                if d.is_dma:
                    sk = ("d", d.sem)
                    sh = dsem[d.sem]
                else:
                    if d.eng == o.eng and not SAME_ENGINE_SYNC[o.eng]:
                        continue
                    sk = ("e", d.eng)
                    sh = esem[d.eng]
                if waited[o.eng].get(sk, 0) >= d.val:
                    continue
                if sk not in need or need[sk][1] < d.val:
                    need[sk] = (sh, d.val)
            for sk, (sh, v) in need.items():
                E.wait_ge(sh, v)
                waited[o.eng][sk] = v
                nwaits += 1
            ins = o.fn(E)
            if o.is_dma:
                ins.then_inc(dsem[o.sem], 16)
            elif o.inc:
                ins.then_inc(esem[o.eng], 1)
        for d in final_wait_ops:
            if d.is_dma:
                nc.sync.wait_ge(dsem[d.sem], d.val)
            else:
                nc.sync.wait_ge(esem[d.eng], d.val)
        self.stats = dict(n_ops=len(self.ops), n_waits=nwaits, cnt=dict(cnt))
        return self.stats


class Ctx:
    def __init__(self, nc):
        self.nc = nc
        self.P = Prog(nc)
        self.banks = [nc.alloc_psum_tensor(f"psb{i}", [128, 512], F32) for i in range(8)]
        self.rr = {}
        self.uid = 0
        self.pfx = ""

    def bank(self, pool, ids):
        i = self.rr.get(pool, 0)
        self.rr[pool] = i + 1
        return ids[i % len(ids)]

    def eng(self, pool, engs):
        i = self.rr.get(("e", pool), 0)
        self.rr[("e", pool)] = i + 1
        return engs[i % len(engs)]

    def key(self, name):
        self.uid += 1
        return (name, self.uid)


def layer_norm_tile(C, r, rkey, out, okey, g_rep, b_rep, gkey, scr, tag):
    P = C.P
    st, mv, sd, rstd, nmr, xn = scr["stats"], scr["mv"], scr["sd"], scr["rstd"], scr["nmr"], scr["xn"]
    k = lambda n: (tag, n)
    P.op("dve", lambda E: E.bn_stats(out=st[:, 0:6], in_=r[:, 0:512]), reads=[rkey], writes=[k("st0")])
    P.op("dve", lambda E: E.bn_stats(out=st[:, 6:12], in_=r[:, 512:1024]), reads=[rkey], writes=[k("st1")])
    P.op("dve", lambda E: E.bn_aggr(out=mv[:, 0:2], in_=st[:, 0:12]), reads=[k("st0"), k("st1")], writes=[k("mv")])
    P.op("act", lambda E: E.activation(out=sd[:], in_=mv[:, 1:2], func=AF.Sqrt, bias=LN_EPS, scale=1.0),
         reads=[k("mv")], writes=[k("sd")])
    P.op("dve", lambda E: E.reciprocal(out=rstd[:], in_=sd[:]), reads=[k("sd")], writes=[k("rstd")])
    P.op("dve", lambda E: E.scalar_tensor_tensor(out=nmr[:], in0=mv[:, 0:1], scalar=-1.0, in1=rstd[:],
                                                 op0=ALU.mult, op1=ALU.mult),
         reads=[k("mv"), k("rstd")], writes=[k("nmr")])
    P.op("act", lambda E: E.activation(out=xn[:], in_=r[:], func=AF.Identity, bias=nmr[:], scale=rstd[:]),
         reads=[rkey, k("rstd"), k("nmr")], writes=[k("xn")])
    P.op("pool", lambda E: E.tensor_tensor(out=xn[:], in0=xn[:], in1=g_rep[:], op=ALU.mult),
         reads=[k("xn"), gkey], writes=[k("xn")])
    P.op("pool", lambda E: E.tensor_tensor(out=out, in0=xn[:], in1=b_rep[:], op=ALU.add),
         reads=[k("xn"), gkey], writes=[okey])


def transpose_tile_bf16(C, src_bf, skey, dst3, dkey, ident_bf, nchunks, bank_ids, evac_engs=("dve", "act")):
    P = C.P
    b = C.bank("tr", bank_ids)
    pb = C.banks[b][:].bitcast(BF16)
    for kc in range(nchunks):
        P.op("pe", lambda E, kc=kc: E.transpose(pb[:, kc * 128:(kc + 1) * 128], src_bf[:, kc * 128:(kc + 1) * 128], ident_bf[:]),
             reads=[skey, "ident_bf"], banks=[b])
    e = C.eng("tr", evac_engs)
    src_v = pb[:, 0:nchunks * 128].rearrange("p (k t) -> p k t", k=nchunks)
    if e == "act":
        P.op("act", lambda E: E.copy(out=dst3, in_=src_v), writes=[dkey], banks=[b])
    else:
        P.op(e, lambda E: E.tensor_copy(out=dst3, in_=src_v), writes=[dkey], banks=[b])


def build_ffn_phase(C, catT, xres, w_out, w_ff1, w_ff2, ln1g, ln1b, ln2g, ln2b, ident,
                    x_out, xT_out, stack, dbg=None):
    nc, P = C.nc, C.P
    sb = lambda name, shape, dt: stack.enter_context(nc.sbuf_tensor(C.pfx + name, shape, dt))
    dbg = dbg or {}
    NPASS, TP = dbg.get('npass', 2), 1024
    NT = TP // 128
    wo = sb("wo", [128, 12, 1024], BF16)
    lnp = sb("lnp", [128, 4, 1024], F32)
    idb = sb("idb", [128, 128], BF16)
    idf = sb("idf", [128, 128], F32)
    cat_sb = [sb(f"cat{i}", [128, 12, 512], BF16) for i in range(1 if (dbg or {}).get("cat_loader") else 2)]
    xmT = sb("xmT", [128, 8, TP], BF16)
    yacc = sb("yacc", [128, NT, 1024], F32)
    hT = [sb(f"hT{i}", [128, 4, TP], BF16) for i in range(2)]
    w1g = [sb(f"w1g{i}", [128, 8, 512], BF16) for i in range(2)]
    w2g = [sb(f"w2g{i}", [128, 4, 1024], BF16) for i in range(2)]
    rt = [sb(f"rt{i}", [128, 1024], F32) for i in range(3)]
    xr = [sb(f"xr{i}", [128, 1024], F32) for i in range(3)]
    xb = [sb(f"xb{i}", [128, 1024], BF16) for i in range(3)]
    relu_t = [sb(f"relu{i}", [128, 512], F32) for i in range(2)]
    scr = [dict(stats=sb(f"lst{i}", [128, 12], F32), mv=sb(f"lmv{i}", [128, 2], F32), sd=sb(f"lsd{i}", [128, 1], F32),
                rstd=sb(f"lrs{i}", [128, 1], F32), nmr=sb(f"lnm{i}", [128, 1], F32))
           for i in range(3)]
    outs = []

    P.dma("sp", lambda E: E.dma_start(out=idf[:], in_=ident), writes=["ident_f"])
    P.op("dve", lambda E: E.tensor_copy(out=idb[:], in_=idf[:]), reads=["ident_f"], writes=["ident_bf"])
    for i, src in enumerate((ln1g, ln1b, ln2g, ln2b)):
        P.dma("act", lambda E, i=i, src=src: E.dma_start(out=lnp[:, i, :], in_=src), writes=[("lnp", i)])
    wov = w_out.rearrange("(kc p) n -> p kc n", p=128)
    for j in range(3):
        P.dma("pool", lambda E, j=j: E.dma_start(out=wo[:, 4 * j:4 * j + 4, :], in_=wov[:, 4 * j:4 * j + 4, :]),
              writes=[("wo", j)])
    cat_loader = dbg.get("cat_loader")
    xT_dst = dbg.get("xT_dst")
    catv = catT.rearrange("(kc p) t -> p kc t", p=128) if cat_loader is None else None
    w1v = w_ff1.rearrange("(kc p) n -> p kc n", p=128)
    w2v = w_ff2.rearrange("(fc p) n -> p fc n", p=128)

    ncat = 0
    nrt = 0
    gcount = 0
    for ps in range(NPASS):
        t0 = ps * TP
        def s1_gen(tl):
            nonlocal ncat, nrt
            tg = tl // 4
            if tl % 4 == 0:
                cb_ = ncat % len(cat_sb)
                ncat += 1
                s1_gen.cb = cb_
                if cat_loader is None:
                    P.dma_("sp", cat_sb[cb_][:], catv[:, :, t0 + tg * 512:t0 + (tg + 1) * 512], writes=[("cat", cb_)])
                else:
                    cat_loader(cat_sb[cb_], ("cat", cb_), ps * 2 + tg)
            cb = s1_gen.cb
            ri = nrt % 3
            nrt += 1
            tok = t0 + tl * 128
            P.dma_("act", xr[ri][:], xres[tok:tok + 128, :], writes=[("xr", ri)])
            for half in range(2):
                b = C.bank("mm", [0, 1, 2, 3])
                for kc in range(12):
                    P.mm(C.banks[b][:], cat_sb[cb][:, kc, (tl % 4) * 128:(tl % 4 + 1) * 128], wo[:, kc, half * 512:(half + 1) * 512],
                         kc == 0, kc == 11, reads=[("cat", cb), ("wo", kc // 4)], banks=[b])
                P.stt("dve", rt[ri][:, half * 512:(half + 1) * 512], xr[ri][:, half * 512:(half + 1) * 512], ALPHA, C.banks[b][:],
                      ALU.mult, ALU.add, reads=[("xr", ri)], writes=[("rt", ri, half)], banks=[b])
            yield
            yield from ln_gen(C, rt[ri][:], [("rt", ri, 0), ("rt", ri, 1)], xr[ri][:], ("xr", ri), lnp[:, 0, :], lnp[:, 1, :],
                              [("lnp", 0), ("lnp", 1)], scr[ri], ("ln", ri))
            P.act(yacc[:, tl, :], xr[ri][:], AF.Copy, scale=ALPHA, reads=[("xr", ri)], writes=[("yacc", tl, 0), ("yacc", tl, 1)])
            P.copy("dve", xb[ri][:], xr[ri][:], reads=[("xr", ri)], writes=[("xb", ri)])
            yield
            transpose_tile_bf16(C, xb[ri], ("xb", ri), xmT[:, :, tl * 128:(tl + 1) * 128], ("xmT", tl), idb, 8, [4, 5])

        run_window([s1_gen(tl) for tl in range(NT)], 3)

        NG = dbg.get('ngroups', 8)

        def ld_w1(g):
            P.dma_("pool", w1g[(gbase + g) % 2][:], w1v[:, :, g * 512:(g + 1) * 512], writes=[("w1g", (gbase + g) % 2)])

        def ld_w2(g):
            P.dma_("pool", w2g[(gbase + g) % 2][:], w2v[:, g * 4:(g + 1) * 4, :], writes=[("w2g", (gbase + g) % 2)])

        def ff1(g):
            w3, h2 = (gbase + g) % 2, (gbase + g) % 2
            for fc in range(4):
                for tg in range(2):
                    b = C.bank("mm", [0, 1, 2, 3])
                    for kc in range(8):
                        P.mm(C.banks[b][:], w1g[w3][:, kc, fc * 128:(fc + 1) * 128], xmT[:, kc, tg * 512:(tg + 1) * 512], kc == 0, kc == 7,
                             reads=[("w1g", w3)] + [("xmT", tg * 4 + j) for j in range(4)], banks=[b])
                    rb = C.bank("relu", [0, 1])
                    P.act(relu_t[rb][:], C.banks[b][:], AF.Relu, writes=[("relu", rb)], banks=[b])
                    P.tt("pool", hT[h2][:, fc, tg * 512:(tg + 1) * 512], relu_t[rb][:], relu_t[rb][:], ALU.mult,
                         reads=[("relu", rb)], writes=[("hT", h2, fc, tg)])

        def ff2(g):
            w3, h2 = (gbase + g) % 2, (gbase + g) % 2
            for tl in range(NT):
                for half in range(2):
                    b = C.bank("mm2", [4, 5, 6, 7])
                    for fc in range(4):
                        P.mm(C.banks[b][:], hT[h2][:, fc, tl * 128:(tl + 1) * 128], w2g[w3][:, fc, half * 512:(half + 1) * 512], fc == 0, fc == 3,
                             reads=[("w2g", w3), ("hT", h2, fc, tl // 4)], banks=[b])
                    P.tt("dve", yacc[:, tl, half * 512:(half + 1) * 512], yacc[:, tl, half * 512:(half + 1) * 512], C.banks[b][:], ALU.add,
                         reads=[("yacc", tl, half)], writes=[("yacc", tl, half)], banks=[b])

        gbase = gcount
        gcount += NG
        if NG > 0:
            ld_w1(0)
            ld_w2(0)
            if NG > 1:
                ld_w1(1)
                ld_w2(1)
            ff1(0)
            for g in range(NG):
                if g + 2 < NG:
                    ld_w1(g + 2)
                if g + 1 < NG:
                    ff1(g + 1)
                ff2(g)
                if g + 2 < NG:
                    ld_w2(g + 2)
        def s3_gen(tl):
            nonlocal nrt
            ri = nrt % 3
            nrt += 1
            tok = t0 + tl * 128
            yield from ln_gen(C, yacc[:, tl, :], [("yacc", tl, 0), ("yacc", tl, 1)], xr[ri][:], ("xr", ri), lnp[:, 2, :], lnp[:, 3, :],
                              [("lnp", 2), ("lnp", 3)], scr[ri], ("ln", ri))
            outs.append(P.dma_("sp", x_out[tok:tok + 128, :], xr[ri][:], reads=[("xr", ri)]))
            if xT_out is not None or xT_dst is not None:
                P.copy("dve", xb[ri][:], xr[ri][:], reads=[("xr", ri)], writes=[("xb", ri)])
                yield
                transpose_tile_bf16(C, xb[ri], ("xb", ri), xmT[:, :, tl * 128:(tl + 1) * 128], ("xmT", tl), idb, 8, [4, 5])
                if xT_dst is None:
                    dstT = xT_out.rearrange("(kc p) t -> p kc t", p=128)[:, :, tok:tok + 128]
                else:
                    dstT = xT_dst(tok)
                outs.append(P.dma_("act", dstT, xmT[:, :, tl * 128:(tl + 1) * 128], reads=[("xmT", tl)], writes=[("xsend", tok)]))

        run_window([s3_gen(tl) for tl in range(NT)], 3)
        if dbg.get('after_pass'):
            dbg['after_pass'](ps)
    return outs


def ln_gen(C, r, rkeys, out, okey, g_rep, b_rep, gkeys, scr, tag):
    P = C.P
    st, mv, sd, rstd, nmr = scr["stats"], scr["mv"], scr["sd"], scr["rstd"], scr["nmr"]
    k = lambda n: (tag, n)
    P.op("dve", lambda E: E.bn_stats(out=st[:, 0:6], in_=r[:, 0:512]), reads=rkeys, writes=[k("st0")])
    P.op("dve", lambda E: E.bn_stats(out=st[:, 6:12], in_=r[:, 512:1024]), reads=rkeys, writes=[k("st1")])
    yield
    P.op("dve", lambda E: E.bn_aggr(out=mv[:, 0:2], in_=st[:, 0:12]), reads=[k("st0"), k("st1")], writes=[k("mv")])
    yield
    P.op("act", lambda E: E.activation(out=sd[:], in_=mv[:, 1:2], func=AF.Sqrt, bias=LN_EPS, scale=1.0),
         reads=[k("mv")], writes=[k("sd")])
    yield
    P.op("dve", lambda E: E.reciprocal(out=rstd[:], in_=sd[:]), reads=[k("sd")], writes=[k("rstd")])
    yield
    P.op("dve", lambda E: E.scalar_tensor_tensor(out=nmr[:], in0=mv[:, 0:1], scalar=-1.0, in1=rstd[:],
                                                 op0=ALU.mult, op1=ALU.mult),
         reads=[k("mv"), k("rstd")], writes=[k("nmr")])
    yield
    P.op("act", lambda E: E.activation(out=out, in_=r, func=AF.Identity, bias=nmr[:], scale=rstd[:]),
         reads=list(rkeys) + [k("rstd"), k("nmr")], writes=[okey])
    yield
    P.op("dve", lambda E: E.tensor_tensor(out=out, in0=out, in1=g_rep, op=ALU.mult), reads=[okey, gkeys[0]], writes=[okey])
    yield
    P.op("dve", lambda E: E.tensor_tensor(out=out, in0=out, in1=b_rep, op=ALU.add), reads=[okey, gkeys[1]], writes=[okey])
    yield


def _ln_two_keys(C, r, rkeys, out, okey, g_rep, b_rep, gkeys, scr, tag):
    for _ in ln_gen(C, r, rkeys, out, okey, g_rep, b_rep, gkeys, scr, tag):
        pass


def run_window(gen_list, width):
    pending = list(gen_list)
    active = []
    while pending or active:
        while pending and len(active) < width:
            active.append(pending.pop(0))
        for g_ in list(active):
            try:
                next(g_)
            except StopIteration:
                active.remove(g_)


def make_ffn_program(dbg=None):
    nc = bass.Bass("TRN2", target_bir_lowering=False)
    dt = lambda n, s, d, k: nc.dram_tensor(n, s, d, kind=k).ap()
    catT = dt("catT", [1536, 2048], BF16, "ExternalInput")
    xres = dt("xres", [2048, 1024], F32, "ExternalInput")
    w_out = dt("w_out", [1536, 1024], F32, "ExternalInput")
    w_ff1 = dt("w_ff1", [1024, 4096], F32, "ExternalInput")
    w_ff2 = dt("w_ff2", [4096, 1024], F32, "ExternalInput")
    lns = [dt(n, [128, 1024], F32, "ExternalInput") for n in ("ln1g", "ln1b", "ln2g", "ln2b")]
    ident = dt("ident", [128, 128], F32, "ExternalInput")
    x_out = dt("x_out", [2048, 1024], F32, "ExternalOutput")
    xT_out = dt("xT_out", [1024, 2048], BF16, "ExternalOutput")
    C = Ctx(nc)
    with ExitStack() as st:
        outs = build_ffn_phase(C, catT, xres, w_out, w_ff1, w_ff2, *lns, ident, x_out, (None if (dbg or {}).get('noxt') else xT_out), st, dbg)
        with ExitStack() as st2:
            stats = C.P.emit(st2, final_wait_ops=outs)
    return nc, stats


def load_xT_group(C, xT_dram, xT_sb, is_f32, tg):
    v = xT_dram.rearrange("(kc p) t -> p kc t", p=128)
    sl = tg % 4
    C.P.dma_("pool" if is_f32 else ("sp" if tg % 2 == 0 else "act"),
             xT_sb[:, :, sl * 512:(sl + 1) * 512], v[:, :, tg * 512:(tg + 1) * 512], writes=[("xT", sl)])


def proj_fm(C, w_sb, wkey, col0, xT_sb, tg, banks_ids):
    P = C.P
    b = C.bank("pf", banks_ids)
    for kc in range(8):
        P.mm(C.banks[b][:], w_sb[:, kc, col0:col0 + 128], xT_sb[:, kc, (tg % 4) * 512:(tg % 4 + 1) * 512], kc == 0, kc == 7,
             reads=[wkey, ("xT", tg % 4)], banks=[b])
    return b


def build_mem_kv(C, mem, lng, lnb, wmk_d, wmv_d, idb, sbt, stack):
    nc, P = C.nc, C.P
    KmT = sbt("KmT", [128, 2, 256], BF16)
    Vm = sbt("Vm", [128, 2, 256], BF16)
    with ExitStack() as st:
        sb = lambda name, shape, dt: st.enter_context(nc.sbuf_tensor(C.pfx + name, shape, dt))
        mt_ = [sb(f"memt{i}", [128, 1024], F32) for i in range(2)]
        mo = [sb(f"memo{i}", [128, 1024], F32) for i in range(2)]
        mb = [sb(f"memb{i}", [128, 1024], BF16) for i in range(2)]
        lnp = sb("mlnp", [128, 2, 1024], F32)
        memT = sb("memT", [128, 8, 256], BF16)
        wmk = sb("wmk_sb", [128, 8, 256], BF16)
        wmv = sb("wmv_sb", [128, 8, 256], BF16)
        scr = [dict(stats=sb(f"mst{i}", [128, 12], F32), mv=sb(f"mmv{i}", [128, 2], F32), sd=sb(f"msd{i}", [128, 1], F32),
                    rstd=sb(f"mrs{i}", [128, 1], F32), nmr=sb(f"mnm{i}", [128, 1], F32), xn=sb(f"mxn{i}", [128, 1024], F32))
               for i in range(2)]
        P.dma_("sp", lnp[:, 0, :], lng, writes=[("mlnp", 0)])
        P.dma_("sp", lnp[:, 1, :], lnb, writes=[("mlnp", 1)])
        P.dma_("pool", wmk[:], wmk_d.rearrange("(kc p) n -> p kc n", p=128), writes=["wmk"])
        P.dma_("pool", wmv[:], wmv_d.rearrange("(kc p) n -> p kc n", p=128), writes=["wmv"])
        for i in range(2):
            P.dma_("act", mt_[i][:], mem[i * 128:(i + 1) * 128, :], writes=[("memt", i)])
            _ln_two_keys(C, mt_[i][:], [("memt", i)], mo[i][:], ("memo", i), lnp[:, 0, :], lnp[:, 1, :],
                         [("mlnp", 0), ("mlnp", 1)], scr[i], ("mln", i))
            P.copy("dve", mb[i][:], mo[i][:], reads=[("memo", i)], writes=[("memb", i)])
            transpose_tile_bf16(C, mb[i], ("memb", i), memT[:, :, i * 128:(i + 1) * 128], ("memT", i), idb, 8, [6, 7])
        for h in range(2):
            b = C.bank("mkv", [4, 5])
            for kc in range(8):
                P.mm(C.banks[b][:, 0:256], wmk[:, kc, h * 128:(h + 1) * 128], memT[:, kc, :], kc == 0, kc == 7,
                     reads=["wmk", ("memT", 0), ("memT", 1)], banks=[b])
            P.copy("dve", KmT[:, h, :], C.banks[b][:, 0:256], writes=[("KmT", h)], banks=[b])
        for mt in range(2):
            b = C.bank("mkv", [4, 5])
            for kc in range(8):
                P.mm(C.banks[b][:, 0:256], memT[:, kc, mt * 128:(mt + 1) * 128], wmv[:, kc, :], kc == 0, kc == 7,
                     reads=["wmv", ("memT", mt)], banks=[b])
            P.copy("act", Vm[:, mt, :], C.banks[b][:, 0:256], writes=[("Vm", mt)], banks=[b])
        P.barrier()
    return KmT, Vm


def _cat_dst(catT_out, k, tg):
    if callable(catT_out):
        return catT_out(k, tg)
    return catT_out[k * 128:(k + 1) * 128, tg * 512:(tg + 1) * 512]


def mem_attention_group(C, KmT, Vm, ones_bf, mqT, mqkey, out_tile, okey, pT_bufs, rden, tagi):
    P = C.P
    h = tagi
    pk = []
    for mt in range(2):
        b = C.bank("ms", [4, 5])
        P.mm(C.banks[b][:], KmT[:, h, mt * 128:(mt + 1) * 128], mqT, True, True, reads=[("KmT", h), mqkey], banks=[b])
        pi = C.bank("mpT", list(range(len(pT_bufs))))
        P.act(pT_bufs[pi][:], C.banks[b][:], AF.Exp, writes=[("mpT", pi)], banks=[b])
        pk.append(pi)
    bo = C.bank("mo", [6])
    bd = C.bank("md", [7])
    for mt in range(2):
        P.mm(C.banks[bo][:], Vm[:, mt, h * 128:(h + 1) * 128], pT_bufs[pk[mt]][:], mt == 0, mt == 1,
             reads=[("Vm", mt), ("mpT", pk[mt])], banks=[bo])
    for mt in range(2):
        P.mm(C.banks[bd][:], ones_bf[:], pT_bufs[pk[mt]][:], mt == 0, mt == 1,
             reads=["ones_bf", ("mpT", pk[mt])], banks=[bd])
    P.act(rden[:], C.banks[bd][:], AF.Ln, writes=["mrden0"], banks=[bd])
    P.act(rden[:], rden[:], AF.Exp, scale=-1.0, reads=["mrden0"], writes=["mrden"])
    P.tt("dve", out_tile, C.banks[bo][:], rden[:], ALU.mult, reads=["mrden"], writes=[okey], banks=[bo])


LAM_INIT1 = 0.8 - 0.6 * float(np.exp(-0.3))


def build_diff_phase(C, xT_d, x_is_f32, mem, mlng, mlnb, wmk_d, wmv_d, wq_d, wk_d, wv_d, wmq_d, lamv, subw, masks_d,
                     ident, catT_out, stack, dbg=None):
    nc, P = C.nc, C.P
    dbg = dbg or {}
    sbt = lambda name, shape, dt: stack.enter_context(nc.sbuf_tensor(C.pfx + name, shape, dt))
    outs = []
    idf = sbt("idf", [128, 128], F32)
    idb = sbt("idb", [128, 128], BF16)
    ones_bf = sbt("ones_bf", [128, 128], BF16)
    ones_f = sbt("ones_f", [128, 128], F32)
    P.dma_("sp", idf[:], ident, writes=["ident_f"])
    P.copy("dve", idb[:], idf[:], reads=["ident_f"], writes=["ident_bf"])
    P.op("dve", lambda E: E.memset(ones_bf[:], 1.0), writes=["ones_bf"])
    P.op("dve", lambda E: E.memset(ones_f[:], 1.0), writes=["ones_f"])
    KmT, Vm = build_mem_kv(C, mem, mlng, mlnb, wmk_d, wmv_d, idb, sbt, stack)

    qT = sbt("qT", [128, 4, T], BF16)
    kT = sbt("kT", [128, 4, T], BF16)
    vtok = sbt("vtok", [128, 32, 512], BF16)
    mqT = sbt("mqT", [128, 2, T], BF16)
    masks = sbt("masks_sb", [128, 4, 512], BF16)
    P.dma_("pool", masks[:], masks_d.rearrange("r p q -> p r q"), writes=["masks"])
    lv = sbt("lv", [128, 4, 64], F32)
    lprod = sbt("lprod", [128, 2, 64], F32)
    lsum = sbt("lsum", [128, 2], F32)
    lexp = sbt("lexp", [128, 2], F32)
    nlam = sbt("nlam", [128, 1], F32)
    swc = sbt("swc", [128, 2], F32)
    P.dma_("sp", lv[:], lamv, writes=["lv"])
    P.dma_("sp", swc[:, 0:1], subw, writes=["swc0"])
    P.tt("dve", lprod[:, 0, :], lv[:, 0, :], lv[:, 1, :], ALU.mult, reads=["lv"], writes=["lprod0"])
    P.tt("dve", lprod[:, 1, :], lv[:, 2, :], lv[:, 3, :], ALU.mult, reads=["lv"], writes=["lprod1"])
    P.op("dve", lambda E: E.reduce_sum(out=lsum[:, 0:1], in_=lprod[:, 0, :], axis=mybir.AxisListType.X), reads=["lprod0"], writes=["lsum0"])
    P.op("dve", lambda E: E.reduce_sum(out=lsum[:, 1:2], in_=lprod[:, 1, :], axis=mybir.AxisListType.X), reads=["lprod1"], writes=["lsum1"])
    P.act(lexp[:], lsum[:], AF.Exp, reads=["lsum0", "lsum1"], writes=["lexp"])
    P.tt("dve", nlam[:], lexp[:, 1:2], lexp[:, 0:1], ALU.subtract, reads=["lexp"], writes=["nlam0"])
    P.ts("dve", nlam[:], nlam[:], -LAM_INIT1, ALU.add, reads=["nlam0"], writes=["nlam"])
    P.ts("dve", swc[:, 1:2], swc[:, 0:1], 1.0 - LAM_INIT1, ALU.mult, reads=["swc0"], writes=["swc"])

    with ExitStack() as st:
        sb = lambda name, shape, dt: st.enter_context(nc.sbuf_tensor(C.pfx + name, shape, dt))
        xT_sb = sb("xT_sb", [128, 8, 2048], BF16)
        wq = sb("wq_sb", [128, 8, 512], BF16)
        wk = sb("wk_sb", [128, 8, 512], BF16)
        wv = sb("wv_sb", [128, 8, 512], BF16)
        wmq = sb("wmq_sb", [128, 8, 256], BF16)
        for wsb, wd, key in ((wq, wq_d, "wq"), (wk, wk_d, "wk"), (wv, wv_d, "wv"), (wmq, wmq_d, "wmq")):
            P.dma_("pool", wsb[:], wd.rearrange("(kc p) n -> p kc n", p=128), writes=[key])
        for tg in range(4):
            load_xT_group(C, xT_d, xT_sb, x_is_f32, tg)
        for tg in range(8):
            if tg >= 4:
                load_xT_group(C, xT_d, xT_sb, x_is_f32, tg)
            for h in range(4):
                b = proj_fm(C, wq, "wq", h * 128, xT_sb, tg, [0, 1, 2, 3])
                P.act(qT[:, h, tg * 512:(tg + 1) * 512], C.banks[b][:], AF.Copy, scale=0.125, writes=[("qT", h, tg)], banks=[b])
                b = proj_fm(C, wk, "wk", h * 128, xT_sb, tg, [0, 1, 2, 3])
                P.copy("dve", kT[:, h, tg * 512:(tg + 1) * 512], C.banks[b][:], writes=[("kT", h, tg)], banks=[b])
            for h in range(2):
                b = proj_fm(C, wmq, "wmq", h * 128, xT_sb, tg, [0, 1, 2, 3])
                P.act(mqT[:, h, tg * 512:(tg + 1) * 512], C.banks[b][:], AF.Copy, scale=128.0 ** -0.5, writes=[("mqT", h, tg)], banks=[b])
            for tl in range(4):
                t = tg * 4 + tl
                b = C.bank("pf", [0, 1, 2, 3])
                for kc in range(8):
                    P.mm(C.banks[b][:], xT_sb[:, kc, (t % 16) * 128:(t % 16 + 1) * 128], wv[:, kc, :], kc == 0, kc == 7,
                         reads=["wv", ("xT", tg % 4)], banks=[b])
                P.copy("dve" if tl % 2 else "act", vtok[:, t, :], C.banks[b][:], writes=[("vtok", t)], banks=[b])
        P.op("dve", lambda E: E.memset(idf[:, 0:1], 1.0) if False else E.memset(ones_f[:, 0:1], 1.0),
             reads=[], writes=[("xT", g) for g in range(4)] + ["wq", "wk", "wv", "wmq", "ones_f"])
        C.free_key = [("xT", g) for g in range(4)] + ["wq", "wk", "wv", "wmq"]
        P.barrier()

    pT = [sbt(f"pT{i}", [128, 512], BF16) for i in range(6)]
    mpT = [sbt(f"mpT{i}", [128, 512], BF16) for i in range(4)]
    rden = sbt("rden", [128, 512], F32)
    r12 = [sbt(f"r12_{i}", [128, 512], F32) for i in range(2)]
    o12 = [sbt(f"o12_{i}", [128, 512], F32) for i in range(2)]
    od = sbt("od", [128, 512], F32)
    sq = sbt("sq", [128, 512], F32)
    rs = sbt("rs", [128, 512], F32)
    cat_t = [sbt(f"cat_t{i}", [128, 512], BF16) for i in range(3)]
    fk = C.free_key
    catv = catT_out
    nq = dbg.get("nqg", 8)
    for qg in range(nq):
        qs = slice(qg * 512, (qg + 1) * 512)
        for h in range(4):
            nkt = 4 * (qg + 1)

            def score(kt):
                pis = []
                for half in range(2):
                    b = C.bank("sT", [4, 5, 6, 7])
                    hp = slice(half * 64, (half + 1) * 64)
                    P.mm(C.banks[b][:], kT[hp, h, kt * 128:(kt + 1) * 128], qT[hp, h, qs], True, True,
                         reads=[("kT", h, kt // 4), ("qT", h, qg)], banks=[b])
                    pi = C.bank("pT", list(range(6)))
                    P.act(pT[pi][:], C.banks[b][:], AF.Exp, reads=fk, writes=[("pT", pi)], banks=[b])
                    if kt >= 4 * qg:
                        r = kt - 4 * qg
                        P.tt("pool", pT[pi][:], pT[pi][:], masks[:, r, :], ALU.mult, reads=["masks", ("pT", pi)], writes=[("pT", pi)])
                    pis.append(pi)
                return pis

            nxt = score(0)
            for kt in range(nkt):
                pis = nxt
                if kt + 1 < nkt:
                    nxt = score(kt + 1)
                for half in range(2):
                    P.mm(C.banks[half][:], vtok[:, kt, h * 128:(h + 1) * 128], pT[pis[half]][:], kt == 0, kt == nkt - 1,
                         reads=[("vtok", kt), ("pT", pis[half])], banks=[half])
                    P.mm(C.banks[2 + half][:], ones_bf[:], pT[pis[half]][:], kt == 0, kt == nkt - 1,
                         reads=["ones_bf", ("pT", pis[half])], banks=[2 + half])
            for half in range(2):
                P.act(r12[half][:], C.banks[2 + half][:], AF.Ln, reads=fk, writes=[("r12", half)], banks=[2 + half])
                P.act(r12[half][:], r12[half][:], AF.Exp, scale=-1.0, reads=[("r12", half)], writes=[("r12", half)])
                P.tt("dve", o12[half][:], C.banks[half][:], r12[half][:], ALU.mult, reads=[("r12", half)] + fk,
                     writes=[("o12", half)], banks=[half])
            P.stt("dve", od[:], o12[1][:], nlam[:], o12[0][:], ALU.mult, ALU.add, reads=[("o12", 0), ("o12", 1), "nlam"] + fk, writes=["od"])
            P.act(sq[:], od[:], AF.Square, reads=["od"] + fk, writes=["sq"])
            b = C.bank("sT", [4, 5, 6, 7])
            P.mm(C.banks[b][:], ones_f[:], sq[:], True, True, reads=["ones_f", "sq"], banks=[b])
            P.act(rs[:], C.banks[b][:], AF.Ln, bias=RMS_EPS, scale=1.0 / 128.0, reads=fk, writes=["rs0"], banks=[b])
            P.act(rs[:], rs[:], AF.Exp, scale=-0.5, reads=["rs0"], writes=["rs"])
            ci = C.bank("cat_t", [0, 1, 2])
            P.stt("dve", cat_t[ci][:], od[:], swc[:, 1:2], rs[:], ALU.mult, ALU.mult, reads=["od", "swc", "rs"] + fk, writes=[("cat_t", ci)])
            outs.append(P.dma_("sp", _cat_dst(catv, h, qg), cat_t[ci][:], reads=[("cat_t", ci)], writes=[("catdst", h, qg)]))
        for h in range(2):
            ci = C.bank("cat_t", [0, 1, 2])
            mem_attention_group(C, KmT, Vm, ones_bf, mqT[:, h, qs], ("mqT", h, qg), cat_t[ci][:], ("cat_t", ci), mpT, rden, h)
            outs.append(P.dma_("act", _cat_dst(catv, 4 + h, qg), cat_t[ci][:], reads=[("cat_t", ci)], writes=[("catdst", 4 + h, qg)]))
        if dbg.get("after_seg"):
            dbg["after_seg"](qg)
    return outs


def make_diff_program(x_is_f32=False, dbg=None):
    nc = bass.Bass("TRN2", target_bir_lowering=False)
    dt = lambda n, s, d, k="ExternalInput": nc.dram_tensor(n, s, d, kind=k).ap()
    xT_d = dt("xT", [1024, T], F32 if x_is_f32 else BF16)
    mem = dt("mem", [256, 1024], F32)
    mlng = dt("mlng", [128, 1024], F32)
    mlnb = dt("mlnb", [128, 1024], F32)
    wmk = dt("wmk", [1024, 256], F32)
    wmv = dt("wmv", [1024, 256], F32)
    wq = dt("wq", [1024, 512], F32)
    wk = dt("wk", [1024, 512], F32)
    wv = dt("wv", [1024, 512], F32)
    wmq = dt("wmq", [1024, 256], F32)
    lamv = dt("lamv", [128, 4, 64], F32)
    subw = dt("subw", [128, 1], F32)
    masks = dt("masks", [4, 128, 512], F32)
    ident = dt("ident", [128, 128], F32)
    catT_out = dt("catT_out", [768, T], BF16, "ExternalOutput")
    C = Ctx(nc)
    with ExitStack() as st:
        outs = build_diff_phase(C, xT_d, x_is_f32, mem, mlng, mlnb, wmk, wmv, wq, wk, wv, wmq, lamv, subw, masks, ident,
                                catT_out, st, dbg)
        with ExitStack() as st2:
            stats = C.P.emit(st2, final_wait_ops=outs)
    return nc, stats


def build_gdn_phase(C, xT_d, x_is_f32, mem, mlng, mlnb, wmk_d, wmv_d, wq_d, wk_d, wv_d, wz_d, wab_d, wmq_d,
                    convw_d, alog_d, dtb_d, gw_d, cm_d, ident, catT_out, stack, dbg=None):
    nc, P = C.nc, C.P
    dbg = dbg or {}
    sbt = lambda name, shape, dt: stack.enter_context(nc.sbuf_tensor(C.pfx + name, shape, dt))
    outs = []
    idf = sbt("idf", [128, 128], F32)
    idb = sbt("idb", [128, 128], BF16)
    ones_bf = sbt("ones_bf", [128, 128], BF16)
    ones_f = sbt("ones_f", [128, 128], F32)
    P.dma_("sp", idf[:], ident, writes=["ident_f"])
    P.copy("dve", idb[:], idf[:], reads=["ident_f"], writes=["ident_bf"])
    P.op("dve", lambda E: E.memset(ones_bf[:], 1.0), writes=["ones_bf"])
    P.op("dve", lambda E: E.memset(ones_f[:], 1.0), writes=["ones_f"])
    KmT, Vm = build_mem_kv(C, mem, mlng, mlnb, wmk_d, wmv_d, idb, sbt, stack)

    cm = sbt("cm_sb", [128, 6, 128], F32)
    P.dma_("sp", cm[:], cm_d.rearrange("r p q -> p r q"), writes=["cm"])
    TRI, SUT, MSL, MUI, ONA, ONB = [cm[:, i, :] for i in range(6)]
    convw = sbt("convw_sb", [128, 12, 4], F32)
    P.dma_("sp", convw[:], convw_d, writes=["convw"])
    alog = sbt("alog_sb", [128, 4], F32)
    dtb = sbt("dtb_sb", [128, 4], F32)
    negA = sbt("negA", [128, 4], F32)
    gw = sbt("gw_sb", [128, 128], F32)
    P.dma_("act", alog[:], alog_d, writes=["alog"])
    P.dma_("act", dtb[:], dtb_d, writes=["dtb"])
    P.dma_("act", gw[:], gw_d, writes=["gw"])
    P.act(negA[:], alog[:], AF.Exp, reads=["alog"], writes=["negA0"])
    P.ts("dve", negA[:], negA[:], -1.0, ALU.mult, reads=["negA0"], writes=["negA"])

    xT_sb = sbt("xT_sb", [128, 8, 2048], BF16)
    wsb = {}
    for n, wd, cols in (("wq", wq_d, 512), ("wk", wk_d, 512), ("wv", wv_d, 512), ("wz", wz_d, 512), ("wab", wab_d, 8), ("wmq", wmq_d, 256)):
        wsb[n] = sbt(n + "_sb", [128, 8, cols], BF16)
        P.dma_("pool", wsb[n][:], wd.rearrange("(kc p) n -> p kc n", p=128), writes=[n])
    pre = [sbt(f"pre{c}", [128, 515], F32) for c in range(12)]
    cv = [sbt(f"cv{i}", [128, 512], F32) for i in range(4)]
    sl = [sbt(f"sl{i}", [128, 512], F32) for i in range(4)]
    sqb = [sbt(f"sqb{i}", [128, 512], F32) for i in range(4)]
    rnb = [sbt(f"rnb{i}", [128, 512], F32) for i in range(4)]
    qT = sbt("gqT", [128, 4, 512], BF16)
    kT = sbt("gkT", [128, 4, 512], BF16)
    vT = sbt("gvT", [128, 4, 512], BF16)
    sz = sbt("sz", [128, 4, 512], BF16)
    mqT = sbt("gmqT", [128, 2, 512], BF16)
    catseg = sbt("catseg", [128, 6, 512], BF16)
    S = [sbt(f"S{h}", [128, 128], F32) for h in range(4)]
    Sb = [sbt(f"Sb{h}", [128, 128], BF16) for h in range(4)]
    for h in range(4):
        P.op("dve", lambda E, h=h: E.memset(S[h][:], 0.0), writes=[("S", h)])
        P.op("dve", lambda E, h=h: E.memset(Sb[h][:], 0.0), writes=[("Sb", h)])
    for c in range(12):
        P.op("pool", lambda E, c=c: E.memset(pre[c][:, 0:3], 0.0), writes=[("pre", c)])
    sc_sets = [{n: sbt(f"sc{j}_" + n, [128, 4], F32) for n in ("beta", "y", "ey", "sp", "g", "gc", "gam", "dca", "dcb", "tmp", "kdec", "bg")}
               for j in range(4)]
    NB2 = 4
    wt = {}
    for n, dt_ in (("gtri", F32), ("Ds", F32), ("DmT", F32), ("L0", F32R), ("L1", F32R), ("U0", F32R), ("U1", F32R),
                   ("R0", F32R), ("R1", F32R), ("Av", F32), ("o", F32), ("og", F32)):
        wt[n] = [sbt(f"wt_{n}{i}", [128, 128], dt_) for i in range(NB2)]
    for n in ("AT", "Rb", "kbg", "kd", "vb", "nwT", "vn", "og2"):
        wt[n] = [sbt(f"wt_{n}{i}", [128, 128], BF16) for i in range(8 if n in ("AT", "Rb", "kd", "vb", "nwT") else NB2)]
    gss = [sbt(f"gss{i}", [128, 1], F32) for i in range(NB2)]
    grs = [sbt(f"grs{i}", [128, 1], F32) for i in range(NB2)]
    junk = [sbt(f"junk{i}", [128, 128], F32) for i in range(NB2)]
    mpT = [sbt(f"mpT{i}", [128, 512], BF16) for i in range(4)]
    rden = sbt("rden", [128, 512], F32)

    def small_bank(pool="sm"):
        return C.bank(pool, [2, 3, 4, 5, 6, 7])

    nseg = dbg.get("nseg", 8)
    for tg in range(nseg):
        load_xT_group(C, xT_d, xT_sb, x_is_f32, tg)
        slot = tg % 4
        def s2_chain(c):
            qkv, h = c // 4, c % 4
            wn = ("wq", "wk", "wv")[qkv]
            if tg > 0:
                P.copy("pool", pre[c][:, 0:3], pre[c][:, 512:515], reads=[("pre", c)], writes=[("pre", c)])
                yield
            b = proj_fm(C, wsb[wn], wn, h * 128, xT_sb, tg, [0, 1, 2, 3, 4, 5, 6, 7])
            P.copy("act", pre[c][:, 3:515], C.banks[b][:], reads=[("pre", c)], writes=[("pre", c)], banks=[b])
            yield
            ci = c % 4
            ce = "dve"
            P.ts(ce, cv[ci][:], pre[c][:, 3:515], convw[:, c, 3:4], ALU.mult, reads=[("pre", c), "convw"], writes=[("cv", ci)])
            yield
            for j in (2, 1, 0):
                P.stt(ce, cv[ci][:], pre[c][:, j:j + 512], convw[:, c, j:j + 1], cv[ci][:], ALU.mult, ALU.add,
                      reads=[("pre", c), ("cv", ci), "convw"], writes=[("cv", ci)])
                yield
            if qkv == 2:
                P.act(vT[:, h, :], cv[ci][:], AF.Silu, reads=[("cv", ci)], writes=[("vT", h)])
                yield
            else:
                si = c % 4
                P.act(sl[si][:], cv[ci][:], AF.Silu, reads=[("cv", ci)], writes=[("sl", si)])
                yield
                P.act(sqb[si][:], sl[si][:], AF.Square, reads=[("sl", si)], writes=[("sqb", si)])
                yield
                b2 = C.bank("pf", [0, 1, 2, 3, 4, 5, 6, 7])
                P.mm(C.banks[b2][:], ones_f[:], sqb[si][:], True, True, reads=["ones_f", ("sqb", si)], banks=[b2])
                yield
                if qkv == 0:
                    P.act(rnb[si][:], C.banks[b2][:], AF.Ln, bias=128.0 * RMS_EPS, scale=128.0, writes=[("rnb", si)], banks=[b2])
                else:
                    P.act(rnb[si][:], C.banks[b2][:], AF.Ln, bias=RMS_EPS, scale=1.0, writes=[("rnb", si)], banks=[b2])
                P.act(rnb[si][:], rnb[si][:], AF.Exp, scale=-0.5, reads=[("rnb", si)], writes=[("rnb", si)])
                yield
                dst = qT if qkv == 0 else kT
                P.tt("pool", dst[:, h, :], sl[si][:], rnb[si][:], ALU.mult, reads=[("sl", si), ("rnb", si)],
                     writes=[(("qT", "kT")[qkv], h)])
                yield
        for w_ in range(3):
            gens = [s2_chain(c) for c in range(w_ * 4, w_ * 4 + 4)]
            while gens:
                for g_ in list(gens):
                    try:
                        next(g_)
                    except StopIteration:
                        gens.remove(g_)
        for h in range(2):
            b = proj_fm(C, wsb["wmq"], "wmq", h * 128, xT_sb, tg, [0, 1])
            P.act(mqT[:, h, :], C.banks[b][:], AF.Copy, scale=128.0 ** -0.5, writes=[("mqT", h)], banks=[b])
        def scal(tl):
            ts_ = slice(tl * 128, (tl + 1) * 128)
            xs = slice(slot * 512 + tl * 128, slot * 512 + (tl + 1) * 128)
            sc = sc_sets[tl]
            sk = lambda n, _p=tl: (n, _p)
            b = C.bank("pf", [0, 1])
            for kc in range(8):
                P.mm(C.banks[b][:], xT_sb[:, kc, xs], wsb["wz"][:, kc, :], kc == 0, kc == 7, reads=["wz", ("xT", slot)], banks=[b])
            P.act(sz[:, tl, :], C.banks[b][:], AF.Silu, writes=[("sz", tl)], banks=[b])
            yield
            b = small_bank()
            pab = C.banks[b][:, 0:8]
            for kc in range(8):
                P.mm(pab, xT_sb[:, kc, xs], wsb["wab"][:, kc, :], kc == 0, kc == 7, reads=["wab", ("xT", slot)], banks=[b])
            P.act(sc["beta"][:], C.banks[b][:, 4:8], AF.Sigmoid, writes=[sk("beta")], banks=[b])
            yield
            P.tt("dve", sc["y"][:], C.banks[b][:, 0:4], dtb[:], ALU.add, reads=["dtb"], writes=[sk("y")], banks=[b])
            yield
            P.act(sc["ey"][:], sc["y"][:], AF.Exp, reads=[sk("y")], writes=[sk("ey")])
            yield
            P.act(sc["sp"][:], sc["ey"][:], AF.Ln, bias=1.0, reads=[sk("ey")], writes=[sk("sp")])
            yield
            P.tt("dve", sc["g"][:], sc["sp"][:], negA[:], ALU.mult, reads=[sk("sp"), "negA"], writes=[sk("g")])
            yield
            b = small_bank()
            pb = C.banks[b]
            P.mm(pb[:, 0:4], TRI, sc["g"][:], True, True, reads=["cm", sk("g")], banks=[b])
            yield
            P.mm(pb[:, 8:12], ONA, sc["g"][:], True, True, reads=["cm", sk("g")], banks=[b])
            yield
            P.mm(pb[:, 16:20], ONB, sc["g"][:], True, True, reads=["cm", sk("g")], banks=[b])
            yield
            P.copy("dve", sc["gc"][:], pb[:, 0:4], writes=[sk("gc")], banks=[b])
            yield
            P.act(sc["gam"][:], pb[:, 0:4], AF.Exp, writes=[sk("gam")], banks=[b])
            yield
            P.act(sc["dca"][:], pb[:, 8:12], AF.Exp, writes=[sk("dca")], banks=[b])
            yield
            P.act(sc["dcb"][:], pb[:, 16:20], AF.Exp, writes=[sk("dcb")], banks=[b])
            yield
            P.tt("dve", sc["tmp"][0:64, :], pb[0:64, 8:12], sc["gc"][0:64, :], ALU.subtract, reads=[sk("gc")], writes=[sk("tmpa")], banks=[b])
            yield
            P.tt("dve", sc["tmp"][64:128, :], pb[64:128, 16:20], sc["gc"][64:128, :], ALU.subtract, reads=[sk("gc")], writes=[sk("tmpb")], banks=[b])
            yield
            P.act(sc["kdec"][:], sc["tmp"][:], AF.Exp, reads=[sk("tmpa"), sk("tmpb")], writes=[sk("kdec")])
            yield
            P.tt("dve", sc["bg"][:], sc["beta"][:], sc["gam"][:], ALU.mult, reads=[sk("beta"), sk("gam")], writes=[sk("bg")])
            yield
        gens = [scal(tl) for tl in range(4)]
        while gens:
            for g_ in list(gens):
                try:
                    next(g_)
                except StopIteration:
                    gens.remove(g_)
        HAND = ("AT", "Rb", "kd", "vb", "nwT")

        def pre_gen(tl, h):
                ts_ = slice(tl * 128, (tl + 1) * 128)
                sc = sc_sets[tl]
                sk = lambda n, _p=tl: (n, _p)
                i = h
                ih = (tl % 2) * 4 + h
                W = {n: (wt[n][ih] if n in HAND else wt[n][h]) for n in wt}
                k_ = lambda n: (n, ih if n in HAND else h)
                kt_ap = kT[:, h, ts_]
                qt_ap = qT[:, h, ts_]
                P.ts("dve", W["gtri"][:], TRI, sc["g"][:, h:h + 1], ALU.mult, reads=["cm", sk("g")], writes=[k_("gtri")])
                bE = small_bank()
                P.mm(C.banks[bE][:, 0:128], W["gtri"][:], SUT, True, True, reads=[k_("gtri"), "cm"], banks=[bE])
                P.mm(C.banks[bE][:, 128:256], SUT, W["gtri"][:], True, True, reads=[k_("gtri"), "cm"], banks=[bE])
                P.act(W["Ds"][:], C.banks[bE][:, 0:128], AF.Exp, writes=[k_("Ds")], banks=[bE])
                P.act(W["DmT"][:], C.banks[bE][:, 128:256], AF.Exp, writes=[k_("DmT")], banks=[bE])
                P.tt("pool", W["Ds"][:], W["Ds"][:], MSL, ALU.mult, reads=[k_("Ds"), "cm"], writes=[k_("Ds")])
                P.tt("pool", W["DmT"][:], W["DmT"][:], MUI, ALU.mult, reads=[k_("DmT"), "cm"], writes=[k_("DmT")])
                yield
                bK = small_bank()
                P.mm(C.banks[bK][:, 0:128], kt_ap, kt_ap, True, True, reads=[("kT", h)], banks=[bK])
                P.mm(C.banks[bK][:, 128:256], kt_ap, qt_ap, True, True, reads=[("kT", h), ("qT", h)], banks=[bK])
                P.stt("dve", W["L0"][:], C.banks[bK][:, 0:128], sc["beta"][:, h:h + 1], W["Ds"][:], ALU.mult, ALU.mult,
                      reads=[sk("beta"), k_("Ds")], writes=[k_("L0")], banks=[bK])
                P.tt("dve", W["AT"][:], C.banks[bK][:, 128:256], W["DmT"][:], ALU.mult, reads=[k_("DmT")], writes=[k_("AT")], banks=[bK])
                yield
                bU = small_bank()
                P.tr(C.banks[bU][:, 0:128], W["L0"][:].bitcast(F32), idf[:], reads=[k_("L0"), "ident_f"], banks=[bU])
                P.copy("act", W["U0"][:], C.banks[bU][:, 0:128], writes=[k_("U0")], banks=[bU])
                P.tt("dve", W["R0"][:], idf[:], W["U0"][:], ALU.subtract, reads=["ident_f", k_("U0")], writes=[k_("R0")])
                yield
                Lc, Uc, Rc = "L0", "U0", "R0"
                for it in range(5):
                    Ln_, Un_, Rn_ = ("L1", "U1", "R1") if Lc == "L0" else ("L0", "U0", "R0")
                    bN = small_bank()
                    P.mm(C.banks[bN][:, 0:128], W[Uc][:], W[Lc][:], True, True, reads=[k_(Uc), k_(Lc)], banks=[bN])
                    if it < 4:
                        P.mm(C.banks[bN][:, 128:256], W[Lc][:], W[Uc][:], True, True, reads=[k_(Uc), k_(Lc)], banks=[bN])
                    P.copy("act", W[Ln_][:], C.banks[bN][:, 0:128], writes=[k_(Ln_)], banks=[bN])
                    if it < 4:
                        P.copy("dve", W[Un_][:], C.banks[bN][:, 128:256], writes=[k_(Un_)], banks=[bN])
                    bR = small_bank()
                    P.mm(C.banks[bR][:, 0:128], W[Ln_][:], W[Rc][:], True, True, reads=[k_(Ln_), k_(Rc)], banks=[bR])
                    P.tt("dve", W[Rn_][:], C.banks[bR][:, 0:128], W[Rc][:], ALU.add, reads=[k_(Rc)], writes=[k_(Rn_)], banks=[bR])
                    yield
                    Lc, Uc, Rc = Ln_, Un_, Rn_
                P.copy("act", W["Rb"][:], W[Rc][:], reads=[k_(Rc)], writes=[k_("Rb")])
                bT = small_bank()
                pbt = C.banks[bT][:].bitcast(BF16)
                P.tr(pbt[:, 0:128], kt_ap, idb[:], reads=[("kT", h), "ident_bf"], banks=[bT])
                P.tr(pbt[:, 128:256], vT[:, h, ts_], idb[:], reads=[("vT", h), "ident_bf"], banks=[bT])
                P.ts("dve", W["kbg"][:], pbt[:, 0:128], sc["bg"][:, h:h + 1], ALU.mult, reads=[sk("bg")], writes=[k_("kbg")], banks=[bT])
                P.act(W["kd"][:], pbt[:, 0:128], AF.Identity, scale=sc["kdec"][:, h:h + 1], reads=[sk("kdec")], writes=[k_("kd")], banks=[bT])
                P.ts("dve", W["vb"][:], pbt[:, 128:256], sc["beta"][:, h:h + 1], ALU.mult, reads=[sk("beta")], writes=[k_("vb")], banks=[bT])
                yield
                bW = small_bank()
                P.mm(C.banks[bW][:, 0:128], W["kbg"][:], W["Rb"][:], True, True, reads=[k_("kbg"), k_("Rb")], banks=[bW])
                P.act(W["nwT"][:], C.banks[bW][:, 0:128], AF.Copy, scale=-1.0, writes=[k_("nwT")], banks=[bW])
                yield

        def run_gen(tl, h):
                ts_ = slice(tl * 128, (tl + 1) * 128)
                sc = sc_sets[tl]
                sk = lambda n, _p=tl: (n, _p)
                i = h
                ih = (tl % 2) * 4 + h
                W = {n: (wt[n][ih] if n in HAND else wt[n][h]) for n in wt}
                k_ = lambda n: (n, ih if n in HAND else h)
                kt_ap = kT[:, h, ts_]
                qt_ap = qT[:, h, ts_]
                for half in range(2):
                    rows = slice(half * 64, (half + 1) * 64)
                    M = 64 if half == 0 else 128
                    dc = sc["dca"] if half == 0 else sc["dcb"]
                    bV = small_bank()
                    P.mm(C.banks[bV][0:M, 0:128], W["Rb"][rows, 0:M], W["vb"][rows, :], True, False,
                         reads=[k_("Rb"), k_("vb")], banks=[bV])
                    P.mm(C.banks[bV][0:M, 0:128], W["nwT"][:, 0:M], Sb[h][:], False, True,
                         reads=[k_("nwT"), ("Sb", h)], banks=[bV])
                    P.copy("act", W["vn"][rows, :], C.banks[bV][rows, 0:128], writes=[(k_("vn"), half)], banks=[bV])
                    yield
                    bO = small_bank()
                    P.mm(C.banks[bO][0:M, 0:128], qt_ap[:, 0:M], Sb[h][:], True, True, reads=[("qT", h), ("Sb", h)], banks=[bO])
                    P.mm(C.banks[bO][0:M, 128:256], W["AT"][rows, 0:M], W["vn"][rows, :], True, True,
                         reads=[k_("AT"), (k_("vn"), half)], banks=[bO])
                    P.copy("act", W["Av"][rows, :], C.banks[bO][rows, 128:256], writes=[(k_("Av"), half)], banks=[bO])
                    P.stt("dve", W["o"][rows, :], C.banks[bO][rows, 0:128], sc["gam"][rows, h:h + 1], W["Av"][rows, :], ALU.mult, ALU.add,
                          reads=[sk("gam"), (k_("Av"), half)], writes=[(k_("o"), half)], banks=[bO])
                    yield
                    bS = small_bank()
                    P.mm(C.banks[bS][:, 0:128], W["kd"][rows, :], W["vn"][rows, :], True, True,
                         reads=[k_("kd"), (k_("vn"), half)], banks=[bS])
                    P.stt("dve", S[h][:], S[h][:], dc[:, h:h + 1], C.banks[bS][:, 0:128], ALU.mult, ALU.add,
                          reads=[("S", h), sk("dca"), sk("dcb")], writes=[("S", h)], banks=[bS])
                    P.copy("act", Sb[h][:], S[h][:], reads=[("S", h)], writes=[("Sb", h)])
                    yield
                okeys = [(k_("o"), 0), (k_("o"), 1)]
                P.op("pool", lambda E, i=i: E.memset(gss[i][:], 0.0), writes=[("gss", i)])
                P.act(junk[i][:], W["o"][:], AF.Square, accum_out=gss[i][:], reads=okeys + [("gss", i)], writes=[("junk", i), ("gss", i)])
                P.act(grs[i][:], gss[i][:], AF.Sqrt, bias=RMS_EPS, scale=1.0 / 128.0, reads=[("gss", i)], writes=[("grs", i)])
                P.op("dve", lambda E, i=i: E.reciprocal(out=grs[i][:], in_=grs[i][:]), reads=[("grs", i)], writes=[("grs", i)])
                P.stt("dve", W["og"][:], W["o"][:], grs[i][:], gw[:], ALU.mult, ALU.mult, reads=okeys + [("grs", i), "gw"], writes=[k_("og")])
                yield
                P.tt("pool", W["og2"][:], W["og"][:], sz[:, tl, h * 128:(h + 1) * 128], ALU.mult, reads=[k_("og"), ("sz", tl)], writes=[k_("og2")])
                bG = small_bank()
                pbg = C.banks[bG][:].bitcast(BF16)
                P.tr(pbg[:, 0:128], W["og2"][:], idb[:], reads=[k_("og2"), "ident_bf"], banks=[bG])
                P.copy("act", catseg[:, h, ts_], pbg[:, 0:128], writes=[("catseg", h, tl)], banks=[bG])

        for step in range(5):
            gens = []
            if step >= 1:
                gens += [run_gen(step - 1, h) for h in range(4)]
            if step < 4:
                gens += [pre_gen(step, h) for h in range(4)]
            while gens:
                for g_ in list(gens):
                    try:
                        next(g_)
                    except StopIteration:
                        gens.remove(g_)
        qs = slice(tg * 512, (tg + 1) * 512)
        for h in range(4):
            outs.append(P.dma_("sp", _cat_dst(catT_out, h, tg), catseg[:, h, :], reads=[("catseg", h, tl) for tl in range(4)], writes=[("catdst", h, tg)]))
        for h in range(2):
            mem_attention_group(C, KmT, Vm, ones_bf, mqT[:, h, :], ("mqT", h), catseg[:, 4 + h, :], ("catseg", 4 + h), mpT, rden, h)
            outs.append(P.dma_("act", _cat_dst(catT_out, 4 + h, tg), catseg[:, 4 + h, :], reads=[("catseg", 4 + h)], writes=[("catdst", 4 + h, tg)]))
        if dbg.get("after_seg"):
            dbg["after_seg"](tg)
    return outs


def make_gdn_program(x_is_f32=True, dbg=None):
    nc = bass.Bass("TRN2", target_bir_lowering=False)
    dt = lambda n, s, d, k="ExternalInput": nc.dram_tensor(n, s, d, kind=k).ap()
    xT_d = dt("xT", [1024, T], F32 if x_is_f32 else BF16)
    mem = dt("mem", [256, 1024], F32)
    mlng = dt("mlng", [128, 1024], F32)
    mlnb = dt("mlnb", [128, 1024], F32)
    wmk = dt("wmk", [1024, 256], F32)
    wmv = dt("wmv", [1024, 256], F32)
    wq = dt("wq", [1024, 512], F32)
    wk = dt("wk", [1024, 512], F32)
    wv = dt("wv", [1024, 512], F32)
    wz = dt("wz", [1024, 512], F32)
    wab = dt("wab", [1024, 8], F32)
    wmq = dt("wmq", [1024, 256], F32)
    convw = dt("convw", [128, 12, 4], F32)
    alog = dt("alog", [128, 4], F32)
    dtb = dt("dtb", [128, 4], F32)
    gw = dt("gw", [128, 128], F32)
    cm = dt("cm", [6, 128, 128], F32)
    ident = dt("ident", [128, 128], F32)
    catT_out = dt("catT_out", [768, T], BF16, "ExternalOutput")
    C = Ctx(nc)
    with ExitStack() as st:
        outs = build_gdn_phase(C, xT_d, x_is_f32, mem, mlng, mlnb, wmk, wmv, wq, wk, wv, wz, wab, wmq, convw, alog, dtb, gw,
                               cm, ident, catT_out, st, dbg)
        with ExitStack() as st2:
            stats = C.P.emit(st2, final_wait_ops=outs)
    return nc, stats


def gdn_const_masks():
    i = np.arange(128)
    same = (i[:, None] // 64) == (i[None, :] // 64)
    tri = ((i[:, None] <= i[None, :]) & same)
    sut = (i[:, None] > i[None, :])
    msl = ((i[:, None] > i[None, :]) & same)
    mui = ((i[:, None] <= i[None, :]) & same)
    ona = np.broadcast_to((i[:, None] < 64), (128, 128))
    onb = np.broadcast_to((i[:, None] >= 64), (128, 128))
    return np.stack([tri, sut, msl, mui, ona, onb]).astype(np.float32)


def _rep(v, n=128):
    v = np.asarray(v, np.float32)
    return np.ascontiguousarray(np.broadcast_to(v[None, :], (n, v.shape[0])))


def _c(a):
    return np.ascontiguousarray(a)


def _diff_masks():
    masks = np.zeros((4, 128, 512), np.float32)
    for r in range(4):
        masks[r] = (128 * r + np.arange(128)[:, None] <= np.arange(512)[None, :]).astype(np.float32)
    return masks


def _mem_inputs(inp, b, hh):
    w = inp["w_mem_kv"]
    return dict(mem=_c(inp["mem"][b]), mlng=_rep(inp["mem_ln_g"]), mlnb=_rep(inp["mem_ln_b"]),
                wmk=_c(w[:, hh * 256:hh * 256 + 256]), wmv=_c(w[:, 512 + hh * 256:512 + hh * 256 + 256]),
                ident=np.eye(128, dtype=np.float32))


def _gdn_inputs(inp, b, hh):
    w_in = inp["l0_w_in"]
    conv_w = inp["l0_conv_w"]
    convw = np.zeros((128, 12, 4), np.float32)
    for qkv in range(3):
        for h in range(4):
            ch0 = qkv * 1024 + (hh * 4 + h) * 128
            convw[:, qkv * 4 + h, :] = conv_w[:, ch0:ch0 + 128].T
    ab = np.concatenate([w_in[:, 4096 + hh * 4:4096 + hh * 4 + 4], w_in[:, 4104 + hh * 4:4104 + hh * 4 + 4]], axis=1)
    d = _mem_inputs(inp, b, hh)
    d.update(xT=_c(inp["x"][b].T),
             wq=_c(w_in[:, hh * 512:hh * 512 + 512]), wk=_c(w_in[:, 1024 + hh * 512:1024 + hh * 512 + 512]),
             wv=_c(w_in[:, 2048 + hh * 512:2048 + hh * 512 + 512]), wz=_c(w_in[:, 3072 + hh * 512:3072 + hh * 512 + 512]),
             wab=_c(ab), wmq=_c(w_in[:, 4112 + hh * 256:4112 + hh * 256 + 256]),
             convw=convw, alog=_rep(inp["l0_a_log"][hh * 4:hh * 4 + 4]), dtb=_rep(inp["l0_dt_bias"][hh * 4:hh * 4 + 4]),
             gw=_rep(inp["l0_gate_norm_w"]), cm=gdn_const_masks())
    return d


def _diff_inputs(inp, b, hh, xT_bf):
    w_in = inp["l1_w_in"]
    d = _mem_inputs(inp, b, hh)
    lam = np.stack([inp["l1_lambda_q1"], inp["l1_lambda_k1"], inp["l1_lambda_q2"], inp["l1_lambda_k2"]]).astype(np.float32)
    d.update(xT=xT_bf,
             wq=_c(w_in[:, hh * 512:hh * 512 + 512]), wk=_c(w_in[:, 1024 + hh * 512:1024 + hh * 512 + 512]),
             wv=_c(w_in[:, 2048 + hh * 512:2048 + hh * 512 + 512]), wmq=_c(w_in[:, 3072 + hh * 256:3072 + hh * 256 + 256]),
             lamv=_c(np.broadcast_to(lam[None], (128, 4, 64))), subw=_c(np.asarray(inp["l1_subln_w"], np.float32)[:, None]),
             masks=_diff_masks())
    return d


def _ffn_inputs(inp, layer, catT_pair, th, xres):
    p = f"l{layer}_"
    c0, c1 = catT_pair
    ts = slice(th * 2048, (th + 1) * 2048)
    catT = np.concatenate([c0[0:512, ts], c1[0:512, ts], c0[512:768, ts], c1[512:768, ts]], axis=0)
    return dict(catT=_c(catT), xres=_c(xres), w_out=_c(inp[p + "w_out"]), w_ff1=_c(inp[p + "w_ff1"]), w_ff2=_c(inp[p + "w_ff2"]),
                ln1g=_rep(inp[p + "ln1_g"]), ln1b=_rep(inp[p + "ln1_b"]), ln2g=_rep(inp[p + "ln2_g"]), ln2b=_rep(inp[p + "ln2_b"]),
                ident=np.eye(128, dtype=np.float32))


def kernel(**inputs):
    inp = {k: np.asarray(v) for k, v in inputs.items()}
    cores = list(range(8))
    ncA, _ = make_gdn_program(x_is_f32=True)
    resA = run_bass_kernel_spmd(ncA, [_gdn_inputs(inp, c // 2, c % 2) for c in cores], core_ids=cores)
    catA = [np.asarray(r["catT_out"]) for r in resA.results]
    ncB, _ = make_ffn_program()
    inB = [_ffn_inputs(inp, 0, (catA[2 * (c // 2)], catA[2 * (c // 2) + 1]), c % 2,
                       inp["x"][c // 2, (c % 2) * 2048:(c % 2 + 1) * 2048, :]) for c in cores]
    resB = run_bass_kernel_spmd(ncB, inB, core_ids=cores)
    x1 = [np.asarray(r["x_out"]) for r in resB.results]
    x1T = [np.asarray(r["xT_out"]) for r in resB.results]
    ncC, _ = make_diff_program(x_is_f32=False)
    inC = []
    for c in cores:
        b = c // 2
        xT_bf = _c(np.concatenate([x1T[2 * b], x1T[2 * b + 1]], axis=1))
        inC.append(_diff_inputs(inp, b, c % 2, xT_bf))
    resC = run_bass_kernel_spmd(ncC, inC, core_ids=cores)
    catC = [np.asarray(r["catT_out"]) for r in resC.results]
    ncD, _ = make_ffn_program()
    inD = [_ffn_inputs(inp, 1, (catC[2 * (c // 2)], catC[2 * (c // 2) + 1]), c % 2, x1[c]) for c in cores]
    resD = run_bass_kernel_spmd(ncD, inD, core_ids=cores)
    out = np.zeros((NB, T, D), np.float32)
    for c in cores:
        out[c // 2, (c % 2) * 2048:(c % 2 + 1) * 2048, :] = np.asarray(resD.results[c]["x_out"])
    return out


def make_fused_program(n_cores=8, dbg=None):
    dbg = dbg or {}
    nc = bass.Bass("TRN2", target_bir_lowering=False)
    dt = lambda n, s, d, k="ExternalInput": nc.dram_tensor(n, s, d, kind=k).ap()
    groups = [[2 * i, 2 * i + 1] for i in range(n_cores // 2)]
    ident = dt("ident", [128, 128], F32)
    sel_d = dt("sel", [128, 2], F32)
    mem = dt("mem", [256, 1024], F32)
    mlng = dt("mlng", [128, 1024], F32)
    mlnb = dt("mlnb", [128, 1024], F32)
    wmk = dt("wmk", [1024, 256], F32)
    wmv = dt("wmv", [1024, 256], F32)
    a_xT = dt("a_xT", [1024, T], F32)
    a_w = {n: dt("a_" + n, [1024, c], F32) for n, c in (("wq", 512), ("wk", 512), ("wv", 512), ("wz", 512), ("wab", 8), ("wmq", 256))}
    a_convw = dt("a_convw", [128, 12, 4], F32)
    a_alog = dt("a_alog", [128, 4], F32)
    a_dtb = dt("a_dtb", [128, 4], F32)
    a_gw = dt("a_gw", [128, 128], F32)
    a_cm = dt("a_cm", [6, 128, 128], F32)
    c_w = {n: dt("c_" + n, [1024, c], F32) for n, c in (("wq", 512), ("wk", 512), ("wv", 512), ("wmq", 256))}
    c_lamv = dt("c_lamv", [128, 4, 64], F32)
    c_subw = dt("c_subw", [128, 1], F32)
    c_masks = dt("c_masks", [4, 128, 512], F32)
    f_in = {}
    for L in ("b", "d"):
        f_in[L] = dict(w_out=dt(L + "_w_out", [1536, 1024], F32), w_ff1=dt(L + "_w_ff1", [1024, 4096], F32),
                       w_ff2=dt(L + "_w_ff2", [4096, 1024], F32),
                       lns=[dt(L + "_" + n, [128, 1024], F32) for n in ("ln1g", "ln1b", "ln2g", "ln2b")])
    b_xres = dt("b_xres", [2048, 1024], F32)
    out_d = dt("out", [2048, 1024], F32, "ExternalOutput")
    it = lambda n, s, d: nc.dram_tensor(n, s, d).ap()
    cat_send = [it(f"cat_send{i}", [24 * 128, 1024], BF16) for i in range(2)]
    cat_recv = [it(f"cat_recv{i}", [24 * 256, 1024], BF16) for i in range(2)]
    x_send = it("x_send", [16 * 128, 1024], BF16)
    x_recv = it("x_recv", [16 * 256, 1024], BF16)
    x1_d = it("x1_d", [2048, 1024], F32)

    C = Ctx(nc)
    P = C.P

    def cat_dst_fn(i):
        v = cat_send[i].rearrange("(k q p) t -> k q p t", q=4, p=128)
        return lambda k, tg: v[k, tg // 2, :, (tg % 2) * 512:(tg % 2 + 1) * 512]

    def finish_exchange():
        P.coll_wait()
        P.barrier()

    def cat_after_seg(i):
        def f(tg):
            if tg % 2 == 1:
                q = tg // 2
                for k in range(6):
                    c = k * 4 + q
                    P.coll("AllGather", groups, cat_send[i][c * 128:(c + 1) * 128, :], cat_recv[i][c * 256:(c + 1) * 256, :],
                           reads=[("catdst", k, tg - 1), ("catdst", k, tg)], wait=False)
        return f

    def x_after_pass(ps):
        for kc in range(8):
            c = kc * 2 + ps
            P.coll("AllGather", groups, x_send[c * 128:(c + 1) * 128, :], x_recv[c * 256:(c + 1) * 256, :],
                   reads=[("xsend", ps * 1024 + j * 128) for j in range(8)], wait=False)

    def make_cat_loader(i, selt, tmpA, tmpB):
        rv = cat_recv[i].rearrange("(k q r p) t -> p r k q t", q=4, r=2, p=128)

        def loader(dst, key, g):
            off = (g % 2) * 512
            for r in range(2):
                P.dma_("sp", tmpA[:], rv[:, r, :, g // 2, off:off + 512], writes=["catA"])
                P.dma_("act", tmpB[:], rv[:, r, :, 2 + g // 2, off:off + 512], writes=["catB"])
                P.ts("dve", tmpB[:], tmpB[:], selt[:, 1:2], ALU.mult, reads=["catB", "sel"], writes=["catB"])
                P.stt("dve", dst[:, r * 6:(r + 1) * 6, :], tmpA[:], selt[:, 0:1], tmpB[:], ALU.mult, ALU.add,
                      reads=["catA", "catB", "sel"], writes=[key])
        return loader

    xs_v = x_send.rearrange("(kc q p) t -> p kc q t", q=2, p=128)

    def xT_dst(tok):
        return xs_v[:, :, tok // 1024, tok % 1024:tok % 1024 + 128]

    C.pfx = "A_"
    with ExitStack() as st:
        build_gdn_phase(C, a_xT, True, mem, mlng, mlnb, wmk, wmv, a_w["wq"], a_w["wk"], a_w["wv"], a_w["wz"], a_w["wab"], a_w["wmq"],
                        a_convw, a_alog, a_dtb, a_gw, a_cm, ident, cat_dst_fn(0), st, dict(dbg, after_seg=cat_after_seg(0)))
        finish_exchange()
    C.pfx = "B_"
    with ExitStack() as st:
        selt = st.enter_context(nc.sbuf_tensor("B_sel", [128, 2], F32))
        tmpA = st.enter_context(nc.sbuf_tensor("B_tmpA", [128, 6, 512], BF16))
        tmpB = st.enter_context(nc.sbuf_tensor("B_tmpB", [128, 6, 512], BF16))
        P.dma_("sp", selt[:], sel_d, writes=["sel"])
        d2 = dict(dbg)
        d2.update(cat_loader=make_cat_loader(0, selt, tmpA, tmpB), xT_dst=xT_dst, after_pass=x_after_pass)
        fi = f_in["b"]
        build_ffn_phase(C, None, b_xres, fi["w_out"], fi["w_ff1"], fi["w_ff2"], *fi["lns"], ident, x1_d, None, st, d2)
        finish_exchange()
    C.pfx = "C_"
    xr_v = x_recv.rearrange("(kc q r p) t -> p kc q r t", q=2, r=2, p=128)

    class _XT:
        def rearrange(self, *_a, **_k):
            return self

        def __getitem__(self, idx):
            tsl = idx[2]
            tg = tsl.start // 512
            r, l = tg // 4, (tg % 4)
            return xr_v[:, :, l // 2, r, (l % 2) * 512:(l % 2 + 1) * 512]

    with ExitStack() as st:
        build_diff_phase(C, _XT(), False, mem, mlng, mlnb, wmk, wmv, c_w["wq"], c_w["wk"], c_w["wv"], c_w["wmq"], c_lamv, c_subw,
                         c_masks, ident, cat_dst_fn(1), st, dict(dbg, after_seg=cat_after_seg(1)))
        finish_exchange()
    C.pfx = "D_"
    with ExitStack() as st:
        selt = st.enter_context(nc.sbuf_tensor("D_sel", [128, 2], F32))
        tmpA = st.enter_context(nc.sbuf_tensor("D_tmpA", [128, 6, 512], BF16))
        tmpB = st.enter_context(nc.sbuf_tensor("D_tmpB", [128, 6, 512], BF16))
        P.dma_("sp", selt[:], sel_d, writes=["sel"])
        d2 = dict(dbg)
        d2.update(cat_loader=make_cat_loader(1, selt, tmpA, tmpB), xT_dst=None)
        fi = f_in["d"]
        outs = build_ffn_phase(C, None, x1_d, fi["w_out"], fi["w_ff1"], fi["w_ff2"], *fi["lns"], ident, out_d, None, st, d2)
        with ExitStack() as st2:
            stats = P.emit(st2, final_wait_ops=outs)
    return nc, stats


def _perm_w_out(w):
    idx = []
    for r in range(2):
        idx += list(range(r * 512, r * 512 + 512))
        idx += list(range(1024 + r * 256, 1024 + r * 256 + 256))
    return np.ascontiguousarray(w[np.asarray(idx)])


def _fused_inputs(inp, c):
    b, hh = c // 2, c % 2
    g = _gdn_inputs(inp, b, hh)
    d = dict(ident=g["ident"], mem=g["mem"], mlng=g["mlng"], mlnb=g["mlnb"], wmk=g["wmk"], wmv=g["wmv"])
    sel = np.zeros((128, 2), np.float32)
    sel[:, hh] = 1.0
    d["sel"] = sel
    d["a_xT"] = g["xT"]
    for n in ("wq", "wk", "wv", "wz", "wab", "wmq"):
        d["a_" + n] = g[n]
    d.update(a_convw=g["convw"], a_alog=g["alog"], a_dtb=g["dtb"], a_gw=g["gw"], a_cm=g["cm"])
    w1 = inp["l1_w_in"]
    d.update(c_wq=_c(w1[:, hh * 512:hh * 512 + 512]), c_wk=_c(w1[:, 1024 + hh * 512:1024 + hh * 512 + 512]),
             c_wv=_c(w1[:, 2048 + hh * 512:2048 + hh * 512 + 512]), c_wmq=_c(w1[:, 3072 + hh * 256:3072 + hh * 256 + 256]))
    lam = np.stack([inp["l1_lambda_q1"], inp["l1_lambda_k1"], inp["l1_lambda_q2"], inp["l1_lambda_k2"]]).astype(np.float32)
    d.update(c_lamv=_c(np.broadcast_to(lam[None], (128, 4, 64))), c_subw=_c(np.asarray(inp["l1_subln_w"], np.float32)[:, None]),
             c_masks=_diff_masks())
    for L, layer in (("b", 0), ("d", 1)):
        p = f"l{layer}_"
        d[L + "_w_out"] = _perm_w_out(inp[p + "w_out"])
        d[L + "_w_ff1"] = _c(inp[p + "w_ff1"])
        d[L + "_w_ff2"] = _c(inp[p + "w_ff2"])
        for n, k in (("ln1g", "ln1_g"), ("ln1b", "ln1_b"), ("ln2g", "ln2_g"), ("ln2b", "ln2_b")):
            d[L + "_" + n] = _rep(inp[p + k])
    d["b_xres"] = _c(inp["x"][b, hh * 2048:(hh + 1) * 2048, :])
    return d


def kernel_unfused(**inputs):
    return _kernel_unfused(**inputs)


_kernel_unfused = kernel


def kernel(**inputs):
    inp = {k: np.asarray(v) for k, v in inputs.items()}
    cores = list(range(8))
    nc, _ = make_fused_program(8)
    res = run_bass_kernel_spmd(nc, [_fused_inputs(inp, c) for c in cores], core_ids=cores)
    out = np.zeros((NB, T, D), np.float32)
    for c in cores:
        out[c // 2, (c % 2) * 2048:(c % 2 + 1) * 2048, :] = np.asarray(res.results[c]["out"])
    return out
```

```python
from contextlib import ExitStack
import numpy as np
import ml_dtypes
import concourse.bass as bass
import concourse.mybir as mybir
from concourse.bass_utils import run_bass_kernel_spmd

F32 = mybir.dt.float32
BF16 = mybir.dt.bfloat16
F32R = mybir.dt.float32r
AF = mybir.ActivationFunctionType
ALU = mybir.AluOpType
NPBF = ml_dtypes.bfloat16

D = 1024
T = 4096
NB = 4
DFF = 4096
ALPHA = 4.0 ** 0.25
LN_EPS = 1e-5
RMS_EPS = 1e-6

SAME_ENGINE_SYNC = {"pe": False, "act": True, "dve": True, "pool": True, "sp": False}
N_DMA_SEMS = 64
N_HW_SEMS = 44


class Op:
    __slots__ = ("eng", "fn", "deps", "inc", "val", "sem", "is_dma")

    def __init__(self, eng, fn, deps, is_dma=False):
        self.eng = eng
        self.fn = fn
        self.deps = deps
        self.inc = False
        self.val = None
        self.sem = None
        self.is_dma = is_dma


class Prog:
    def __init__(self, nc):
        self.nc = nc
        self.engs = {"pe": nc.tensor, "act": nc.scalar, "dve": nc.vector,
                     "pool": nc.gpsimd, "sp": nc.sync}
        self.ops = []
        self.last_w = {}
        self.readers = {}
        self.bank_last = {}
        self.dma_rr = 0
        self.dma_rr_sw = 0
        self.cc_sems = []
        self.dma_last = [None] * N_DMA_SEMS
        self.uid = 0

    def _deps_for(self, eng, reads, writes, banks):
        deps = []
        for k in reads:
            w = self.last_w.get(k)
            if w is not None:
                deps.append(w)
        for k in writes:
            w = self.last_w.get(k)
            if w is not None:
                deps.append(w)
            deps.extend(self.readers.get(k, ()))
        for b in banks:
            for e, o in self.bank_last.get(b, {}).items():
                if e != eng:
                    deps.append(o)
        return deps

    def _record(self, op, reads, writes, banks):
        for k in reads:
            self.readers.setdefault(k, []).append(op)
        for k in writes:
            self.last_w[k] = op
            self.readers[k] = []
        for b in banks:
            self.bank_last.setdefault(b, {})[op.eng] = op
        self.ops.append(op)

    def op(self, eng, fn, reads=(), writes=(), banks=()):
        o = Op(eng, fn, self._deps_for(eng, reads, writes, banks))
        self._record(o, reads, writes, banks)
        return o

    def dma(self, eng, fn, reads=(), writes=()):
        deps = self._deps_for(eng, reads, writes, ())
        if eng == "pool":
            i = N_HW_SEMS + self.dma_rr_sw
            self.dma_rr_sw = (self.dma_rr_sw + 1) % (N_DMA_SEMS - N_HW_SEMS)
        else:
            i = self.dma_rr
            self.dma_rr = (i + 1) % N_HW_SEMS
        if self.dma_last[i] is not None:
            deps.append(self.dma_last[i])
        o = Op(eng, fn, deps, is_dma=True)
        o.sem = i
        o.inc = True
        self.dma_last[i] = o
        self._record(o, reads, writes, ())
        return o

    def barrier(self):
        last = {}
        for o in self.ops:
            if not o.is_dma:
                last[o.eng] = o
        dmas = [d for d in self.dma_last if d is not None]
        for e in self.engs:
            deps = [last[x] for x in last if x != e] + dmas
            self.ops.append(Op(e, lambda E: E.nop(), deps))

    def coll_wait(self):
        def f(E):
            if self.cc_sems:
                E.wait_ge(self.cc_sems[0], self.cc_count)
            return E.nop()
        return self.op("pool", f)

    def coll(self, kind, groups, in_ap, out_ap, reads=(), writes=(), wait=True):
        nc = self.nc
        def f(E):
            if not self.cc_sems:
                self.cc_sems.append(nc.alloc_semaphore(name="s_cc"))
                self.cc_count = 0
            sem = self.cc_sems[0]
            self.cc_count += 1
            E.collective_compute(kind, ALU.bypass, replica_groups=groups, ins=[in_ap.opt()], outs=[out_ap.opt()]).then_inc(sem)
            if wait:
                E.wait_ge(sem, self.cc_count)
            return E.nop()
        return self.op("pool", f, reads, writes)

    def dma_(self, eng, out, in_, reads=(), writes=()):
        return self.dma(eng, lambda E: E.dma_start(out=out, in_=in_), reads, writes)

    def mm(self, out, lhsT, rhs, start, stop, reads=(), banks=()):
        return self.op("pe", lambda E: E.matmul(out, lhsT=lhsT, rhs=rhs, start=start, stop=stop), reads, (), banks)

    def tr(self, out, in_, ident, reads=(), banks=()):
        return self.op("pe", lambda E: E.transpose(out, in_, ident), reads, (), banks)

    def act(self, out, in_, func, bias=0.0, scale=1.0, accum_out=None, reads=(), writes=(), banks=()):
        if accum_out is None:
            f = lambda E: E.activation(out=out, in_=in_, func=func, bias=bias, scale=scale)
        else:
            f = lambda E: E.activation(out=out, in_=in_, func=func, bias=bias, scale=scale, accum_out=accum_out)
        return self.op("act", f, reads, writes, banks)

    def copy(self, eng, out, in_, reads=(), writes=(), banks=()):
        if eng == "act":
            return self.op("act", lambda E: E.copy(out=out, in_=in_), reads, writes, banks)
        return self.op(eng, lambda E: E.tensor_copy(out=out, in_=in_), reads, writes, banks)

    def tt(self, eng, out, in0, in1, op, reads=(), writes=(), banks=()):
        return self.op(eng, lambda E: E.tensor_tensor(out=out, in0=in0, in1=in1, op=op), reads, writes, banks)

    def ts(self, eng, out, in0, s1, op0, s2=None, op1=None, reads=(), writes=(), banks=()):
        if op1 is None:
            f = lambda E: E.tensor_scalar(out=out, in0=in0, scalar1=s1, scalar2=None, op0=op0)
        else:
            f = lambda E: E.tensor_scalar(out=out, in0=in0, scalar1=s1, scalar2=s2, op0=op0, op1=op1)
        return self.op(eng, f, reads, writes, banks)

    def stt(self, eng, out, in0, scalar, in1, op0, op1, reads=(), writes=(), banks=()):
        return self.op(eng, lambda E: E.scalar_tensor_tensor(out=out, in0=in0, scalar=scalar, in1=in1, op0=op0, op1=op1),
                       reads, writes, banks)

    def emit(self, stack, final_wait_ops=()):
        nc = self.nc
        for o in self.ops:
            for d in o.deps:
                if d.is_dma:
                    continue
                if d.eng == o.eng and not SAME_ENGINE_SYNC[o.eng]:
                    continue
                d.inc = True
        for d in final_wait_ops:
            d.inc = True
        esem = {e: stack.enter_context(nc.semaphore(f"s_{e}")) for e in self.engs}
        dsem = [stack.enter_context(nc.semaphore(f"s_dma{i}")) for i in range(N_DMA_SEMS)]
        cnt = {e: 0 for e in self.engs}
        dcnt = [0] * N_DMA_SEMS
        for o in self.ops:
            if o.is_dma:
                dcnt[o.sem] += 16
                o.val = dcnt[o.sem]
                assert o.val <= 96, 'DMA semaphore value limit (device faults above ~100)'
            elif o.inc:
                cnt[o.eng] += 1
                o.val = cnt[o.eng]
        waited = {e: {} for e in self.engs}
        nwaits = 0
        for o in self.ops:
            E = self.engs[o.eng]
            need = {}
            for d in o.deps:
                if d.is_dma:
                    sk = ("d", d.sem)
                    sh = dsem[d.sem]
                else:
                    if d.eng == o.eng and not SAME_ENGINE_SYNC[o.eng]:
                        continue
                    sk = ("e", d.eng)
                    sh = esem[d.eng]
                if waited[o.eng].get(sk, 0) >= d.val:
                    continue
                if sk not in need or need[sk][1] < d.val:
                    need[sk] = (sh, d.val)
            for sk, (sh, v) in need.items():
                E.wait_ge(sh, v)
                waited[o.eng][sk] = v
                nwaits += 1
            ins = o.fn(E)
            if o.is_dma:
                ins.then_inc(dsem[o.sem], 16)
            elif o.inc:
                ins.then_inc(esem[o.eng], 1)
        for d in final_wait_ops:
            if d.is_dma:
                nc.sync.wait_ge(dsem[d.sem], d.val)
            else:
                nc.sync.wait_ge(esem[d.eng], d.val)
        self.stats = dict(n_ops=len(self.ops), n_waits=nwaits, cnt=dict(cnt))
        return self.stats


class Ctx:
    def __init__(self, nc):
        self.nc = nc
        self.P = Prog(nc)
        self.banks = [nc.alloc_psum_tensor(f"psb{i}", [128, 512], F32) for i in range(8)]
        self.rr = {}
        self.uid = 0
        self.pfx = ""

    def bank(self, pool, ids):
        i = self.rr.get(pool, 0)
        self.rr[pool] = i + 1
        return ids[i % len(ids)]

    def eng(self, pool, engs):
        i = self.rr.get(("e", pool), 0)
        self.rr[("e", pool)] = i + 1
        return engs[i % len(engs)]

    def key(self, name):
        self.uid += 1
        return (name, self.uid)


def layer_norm_tile(C, r, rkey, out, okey, g_rep, b_rep, gkey, scr, tag):
    P = C.P
    st, mv, sd, rstd, nmr, xn = scr["stats"], scr["mv"], scr["sd"], scr["rstd"], scr["nmr"], scr["xn"]
    k = lambda n: (tag, n)
    P.op("dve", lambda E: E.bn_stats(out=st[:, 0:6], in_=r[:, 0:512]), reads=[rkey], writes=[k("st0")])
    P.op("dve", lambda E: E.bn_stats(out=st[:, 6:12], in_=r[:, 512:1024]), reads=[rkey], writes=[k("st1")])
    P.op("dve", lambda E: E.bn_aggr(out=mv[:, 0:2], in_=st[:, 0:12]), reads=[k("st0"), k("st1")], writes=[k("mv")])
    P.op("act", lambda E: E.activation(out=sd[:], in_=mv[:, 1:2], func=AF.Sqrt, bias=LN_EPS, scale=1.0),
         reads=[k("mv")], writes=[k("sd")])
    P.op("dve", lambda E: E.reciprocal(out=rstd[:], in_=sd[:]), reads=[k("sd")], writes=[k("rstd")])
    P.op("dve", lambda E: E.scalar_tensor_tensor(out=nmr[:], in0=mv[:, 0:1], scalar=-1.0, in1=rstd[:],
                                                 op0=ALU.mult, op1=ALU.mult),
         reads=[k("mv"), k("rstd")], writes=[k("nmr")])
    P.op("act", lambda E: E.activation(out=xn[:], in_=r[:], func=AF.Identity, bias=nmr[:], scale=rstd[:]),
         reads=[rkey, k("rstd"), k("nmr")], writes=[k("xn")])
    P.op("pool", lambda E: E.tensor_tensor(out=xn[:], in0=xn[:], in1=g_rep[:], op=ALU.mult),
         reads=[k("xn"), gkey], writes=[k("xn")])
    P.op("pool", lambda E: E.tensor_tensor(out=out, in0=xn[:], in1=b_rep[:], op=ALU.add),
         reads=[k("xn"), gkey], writes=[okey])


def transpose_tile_bf16(C, src_bf, skey, dst3, dkey, ident_bf, nchunks, bank_ids, evac_engs=("dve", "act")):
    P = C.P
    b = C.bank("tr", bank_ids)
    pb = C.banks[b][:].bitcast(BF16)
    for kc in range(nchunks):
        P.op("pe", lambda E, kc=kc: E.transpose(pb[:, kc * 128:(kc + 1) * 128], src_bf[:, kc * 128:(kc + 1) * 128], ident_bf[:]),
             reads=[skey, "ident_bf"], banks=[b])
    e = C.eng("tr", evac_engs)
    src_v = pb[:, 0:nchunks * 128].rearrange("p (k t) -> p k t", k=nchunks)
    if e == "act":
        P.op("act", lambda E: E.copy(out=dst3, in_=src_v), writes=[dkey], banks=[b])
    else:
        P.op(e, lambda E: E.tensor_copy(out=dst3, in_=src_v), writes=[dkey], banks=[b])


def build_ffn_phase(C, catT, xres, w_out, w_ff1, w_ff2, ln1g, ln1b, ln2g, ln2b, ident,
                    x_out, xT_out, stack, dbg=None):
    nc, P = C.nc, C.P
    sb = lambda name, shape, dt: stack.enter_context(nc.sbuf_tensor(C.pfx + name, shape, dt))
    dbg = dbg or {}
    NPASS, TP = dbg.get('npass', 2), 1024
    NT = TP // 128
    wo = sb("wo", [128, 12, 1024], BF16)
    lnp = sb("lnp", [128, 4, 1024], F32)
    idb = sb("idb", [128, 128], BF16)
    idf = sb("idf", [128, 128], F32)
    cat_sb = [sb(f"cat{i}", [128, 12, 512], BF16) for i in range(1 if (dbg or {}).get("cat_loader") else 2)]
    xmT = sb("xmT", [128, 8, TP], BF16)
    yacc = sb("yacc", [128, NT, 1024], F32)
    hT = [sb(f"hT{i}", [128, 4, TP], BF16) for i in range(2)]
    w1g = [sb(f"w1g{i}", [128, 8, 512], BF16) for i in range(2)]
    w2g = [sb(f"w2g{i}", [128, 4, 1024], BF16) for i in range(2)]
    rt = [sb(f"rt{i}", [128, 1024], F32) for i in range(3)]
    xr = [sb(f"xr{i}", [128, 1024], F32) for i in range(3)]
    xb = [sb(f"xb{i}", [128, 1024], BF16) for i in range(3)]
    relu_t = [sb(f"relu{i}", [128, 512], F32) for i in range(2)]
    scr = [dict(stats=sb(f"lst{i}", [128, 12], F32), mv=sb(f"lmv{i}", [128, 2], F32), sd=sb(f"lsd{i}", [128, 1], F32),
                rstd=sb(f"lrs{i}", [128, 1], F32), nmr=sb(f"lnm{i}", [128, 1], F32))
           for i in range(3)]
    outs = []

    P.dma("sp", lambda E: E.dma_start(out=idf[:], in_=ident), writes=["ident_f"])
    P.op("dve", lambda E: E.tensor_copy(out=idb[:], in_=idf[:]), reads=["ident_f"], writes=["ident_bf"])
    for i, src in enumerate((ln1g, ln1b, ln2g, ln2b)):
        P.dma("act", lambda E, i=i, src=src: E.dma_start(out=lnp[:, i, :], in_=src), writes=[("lnp", i)])
    wov = w_out.rearrange("(kc p) n -> p kc n", p=128)
    for j in range(3):
        P.dma("pool", lambda E, j=j: E.dma_start(out=wo[:, 4 * j:4 * j + 4, :], in_=wov[:, 4 * j:4 * j + 4, :]),
              writes=[("wo", j)])
    cat_loader = dbg.get("cat_loader")
    xT_dst = dbg.get("xT_dst")
    catv = catT.rearrange("(kc p) t -> p kc t", p=128) if cat_loader is None else None
    w1v = w_ff1.rearrange("(kc p) n -> p kc n", p=128)
    w2v = w_ff2.rearrange("(fc p) n -> p fc n", p=128)

    ncat = 0
    nrt = 0
    gcount = 0
    for ps in range(NPASS):
        t0 = ps * TP
        def s1_gen(tl):
            nonlocal ncat, nrt
            tg = tl // 4
            if tl % 4 == 0:
                cb_ = ncat % len(cat_sb)
                ncat += 1
                s1_gen.cb = cb_
                if cat_loader is None:
                    P.dma_("sp", cat_sb[cb_][:], catv[:, :, t0 + tg * 512:t0 + (tg + 1) * 512], writes=[("cat", cb_)])
                else:
                    cat_loader(cat_sb[cb_], ("cat", cb_), ps * 2 + tg)
            cb = s1_gen.cb
            ri = nrt % 3
            nrt += 1
            tok = t0 + tl * 128
            P.dma_("act", xr[ri][:], xres[tok:tok + 128, :], writes=[("xr", ri)])
            for half in range(2):
                b = C.bank("mm", [0, 1, 2, 3])
                for kc in range(12):
                    P.mm(C.banks[b][:], cat_sb[cb][:, kc, (tl % 4) * 128:(tl % 4 + 1) * 128], wo[:, kc, half * 512:(half + 1) * 512],
                         kc == 0, kc == 11, reads=[("cat", cb), ("wo", kc // 4)], banks=[b])
                P.stt("dve", rt[ri][:, half * 512:(half + 1) * 512], xr[ri][:, half * 512:(half + 1) * 512], ALPHA, C.banks[b][:],
                      ALU.mult, ALU.add, reads=[("xr", ri)], writes=[("rt", ri, half)], banks=[b])
            yield
            yield from ln_gen(C, rt[ri][:], [("rt", ri, 0), ("rt", ri, 1)], xr[ri][:], ("xr", ri), lnp[:, 0, :], lnp[:, 1, :],
                              [("lnp", 0), ("lnp", 1)], scr[ri], ("ln", ri))
            P.act(yacc[:, tl, :], xr[ri][:], AF.Copy, scale=ALPHA, reads=[("xr", ri)], writes=[("yacc", tl, 0), ("yacc", tl, 1)])
            P.copy("dve", xb[ri][:], xr[ri][:], reads=[("xr", ri)], writes=[("xb", ri)])
            yield
            transpose_tile_bf16(C, xb[ri], ("xb", ri), xmT[:, :, tl * 128:(tl + 1) * 128], ("xmT", tl), idb, 8, [4, 5])

        run_window([s1_gen(tl) for tl in range(NT)], 3)

        NG = dbg.get('ngroups', 8)

        def ld_w1(g):
            P.dma_("pool", w1g[(gbase + g) % 2][:], w1v[:, :, g * 512:(g + 1) * 512], writes=[("w1g", (gbase + g) % 2)])

        def ld_w2(g):
            P.dma_("pool", w2g[(gbase + g) % 2][:], w2v[:, g * 4:(g + 1) * 4, :], writes=[("w2g", (gbase + g) % 2)])

        def ff1(g):
            w3, h2 = (gbase + g) % 2, (gbase + g) % 2
            for fc in range(4):
                for tg in range(2):
                    b = C.bank("mm", [0, 1, 2, 3])
                    for kc in range(8):
                        P.mm(C.banks[b][:], w1g[w3][:, kc, fc * 128:(fc + 1) * 128], xmT[:, kc, tg * 512:(tg + 1) * 512], kc == 0, kc == 7,
                             reads=[("w1g", w3)] + [("xmT", tg * 4 + j) for j in range(4)], banks=[b])
                    rb = C.bank("relu", [0, 1])
                    P.act(relu_t[rb][:], C.banks[b][:], AF.Relu, writes=[("relu", rb)], banks=[b])
                    P.tt("pool", hT[h2][:, fc, tg * 512:(tg + 1) * 512], relu_t[rb][:], relu_t[rb][:], ALU.mult,
                         reads=[("relu", rb)], writes=[("hT", h2, fc, tg)])

        def ff2(g):
            w3, h2 = (gbase + g) % 2, (gbase + g) % 2
            for tl in range(NT):
                for half in range(2):
                    b = C.bank("mm2", [4, 5, 6, 7])
                    for fc in range(4):
                        P.mm(C.banks[b][:], hT[h2][:, fc, tl * 128:(tl + 1) * 128], w2g[w3][:, fc, half * 512:(half + 1) * 512], fc == 0, fc == 3,
                             reads=[("w2g", w3), ("hT", h2, fc, tl // 4)], banks=[b])
                    P.tt("dve", yacc[:, tl, half * 512:(half + 1) * 512], yacc[:, tl, half * 512:(half + 1) * 512], C.banks[b][:], ALU.add,
                         reads=[("yacc", tl, half)], writes=[("yacc", tl, half)], banks=[b])

        gbase = gcount
        gcount += NG
        if NG > 0:
            ld_w1(0)
            ld_w2(0)
            if NG > 1:
                ld_w1(1)
                ld_w2(1)
            ff1(0)
            for g in range(NG):
                if g + 2 < NG:
                    ld_w1(g + 2)
                if g + 1 < NG:
                    ff1(g + 1)
                ff2(g)
                if g + 2 < NG:
                    ld_w2(g + 2)
        def s3_gen(tl):
            nonlocal nrt
            ri = nrt % 3
            nrt += 1
            tok = t0 + tl * 128
            yield from ln_gen(C, yacc[:, tl, :], [("yacc", tl, 0), ("yacc", tl, 1)], xr[ri][:], ("xr", ri), lnp[:, 2, :], lnp[:, 3, :],
                              [("lnp", 2), ("lnp", 3)], scr[ri], ("ln", ri))
            outs.append(P.dma_("sp", x_out[tok:tok + 128, :], xr[ri][:], reads=[("xr", ri)]))
            if xT_out is not None or xT_dst is not None:
                P.copy("dve", xb[ri][:], xr[ri][:], reads=[("xr", ri)], writes=[("xb", ri)])
                yield
                transpose_tile_bf16(C, xb[ri], ("xb", ri), xmT[:, :, tl * 128:(tl + 1) * 128], ("xmT", tl), idb, 8, [4, 5])
                if xT_dst is None:
                    dstT = xT_out.rearrange("(kc p) t -> p kc t", p=128)[:, :, tok:tok + 128]
                else:
                    dstT = xT_dst(tok)
                outs.append(P.dma_("act", dstT, xmT[:, :, tl * 128:(tl + 1) * 128], reads=[("xmT", tl)], writes=[("xsend", tok)]))

        run_window([s3_gen(tl) for tl in range(NT)], 3)
        if dbg.get('after_pass'):
            dbg['after_pass'](ps)
    return outs


def ln_gen(C, r, rkeys, out, okey, g_rep, b_rep, gkeys, scr, tag):
    P = C.P
    st, mv, sd, rstd, nmr = scr["stats"], scr["mv"], scr["sd"], scr["rstd"], scr["nmr"]
    k = lambda n: (tag, n)
    P.op("dve", lambda E: E.bn_stats(out=st[:, 0:6], in_=r[:, 0:512]), reads=rkeys, writes=[k("st0")])
    P.op("dve", lambda E: E.bn_stats(out=st[:, 6:12], in_=r[:, 512:1024]), reads=rkeys, writes=[k("st1")])
    yield
    P.op("dve", lambda E: E.bn_aggr(out=mv[:, 0:2], in_=st[:, 0:12]), reads=[k("st0"), k("st1")], writes=[k("mv")])
    yield
    P.op("act", lambda E: E.activation(out=sd[:], in_=mv[:, 1:2], func=AF.Sqrt, bias=LN_EPS, scale=1.0),
         reads=[k("mv")], writes=[k("sd")])
    yield
    P.op("dve", lambda E: E.reciprocal(out=rstd[:], in_=sd[:]), reads=[k("sd")], writes=[k("rstd")])
    yield
    P.op("dve", lambda E: E.scalar_tensor_tensor(out=nmr[:], in0=mv[:, 0:1], scalar=-1.0, in1=rstd[:],
                                                 op0=ALU.mult, op1=ALU.mult),
         reads=[k("mv"), k("rstd")], writes=[k("nmr")])
    yield
    P.op("act", lambda E: E.activation(out=out, in_=r, func=AF.Identity, bias=nmr[:], scale=rstd[:]),
         reads=list(rkeys) + [k("rstd"), k("nmr")], writes=[okey])
    yield
    P.op("dve", lambda E: E.tensor_tensor(out=out, in0=out, in1=g_rep, op=ALU.mult), reads=[okey, gkeys[0]], writes=[okey])
    yield
    P.op("dve", lambda E: E.tensor_tensor(out=out, in0=out, in1=b_rep, op=ALU.add), reads=[okey, gkeys[1]], writes=[okey])
    yield


def _ln_two_keys(C, r, rkeys, out, okey, g_rep, b_rep, gkeys, scr, tag):
    for _ in ln_gen(C, r, rkeys, out, okey, g_rep, b_rep, gkeys, scr, tag):
        pass


def run_window(gen_list, width):
    pending = list(gen_list)
    active = []
    while pending or active:
        while pending and len(active) < width:
            active.append(pending.pop(0))
        for g_ in list(active):
            try:
                next(g_)
            except StopIteration:
                active.remove(g_)


def make_ffn_program(dbg=None):
    nc = bass.Bass("TRN2", target_bir_lowering=False)
    dt = lambda n, s, d, k: nc.dram_tensor(n, s, d, kind=k).ap()
    catT = dt("catT", [1536, 2048], BF16, "ExternalInput")
    xres = dt("xres", [2048, 1024], F32, "ExternalInput")
    w_out = dt("w_out", [1536, 1024], F32, "ExternalInput")
    w_ff1 = dt("w_ff1", [1024, 4096], F32, "ExternalInput")
    w_ff2 = dt("w_ff2", [4096, 1024], F32, "ExternalInput")
    lns = [dt(n, [128, 1024], F32, "ExternalInput") for n in ("ln1g", "ln1b", "ln2g", "ln2b")]
    ident = dt("ident", [128, 128], F32, "ExternalInput")
    x_out = dt("x_out", [2048, 1024], F32, "ExternalOutput")
    xT_out = dt("xT_out", [1024, 2048], BF16, "ExternalOutput")
    C = Ctx(nc)
    with ExitStack() as st:
        outs = build_ffn_phase(C, catT, xres, w_out, w_ff1, w_ff2, *lns, ident, x_out, (None if (dbg or {}).get('noxt') else xT_out), st, dbg)
        with ExitStack() as st2:
            stats = C.P.emit(st2, final_wait_ops=outs)
    return nc, stats


def load_xT_group(C, xT_dram, xT_sb, is_f32, tg):
    v = xT_dram.rearrange("(kc p) t -> p kc t", p=128)
    sl = tg % 4
    C.P.dma_("pool" if is_f32 else ("sp" if tg % 2 == 0 else "act"),
             xT_sb[:, :, sl * 512:(sl + 1) * 512], v[:, :, tg * 512:(tg + 1) * 512], writes=[("xT", sl)])


def proj_fm(C, w_sb, wkey, col0, xT_sb, tg, banks_ids):
    P = C.P
    b = C.bank("pf", banks_ids)
    for kc in range(8):
        P.mm(C.banks[b][:], w_sb[:, kc, col0:col0 + 128], xT_sb[:, kc, (tg % 4) * 512:(tg % 4 + 1) * 512], kc == 0, kc == 7,
             reads=[wkey, ("xT", tg % 4)], banks=[b])
    return b


def build_mem_kv(C, mem, lng, lnb, wmk_d, wmv_d, idb, sbt, stack):
    nc, P = C.nc, C.P
    KmT = sbt("KmT", [128, 2, 256], BF16)
    Vm = sbt("Vm", [128, 2, 256], BF16)
    with ExitStack() as st:
        sb = lambda name, shape, dt: st.enter_context(nc.sbuf_tensor(C.pfx + name, shape, dt))
        mt_ = [sb(f"memt{i}", [128, 1024], F32) for i in range(2)]
        mo = [sb(f"memo{i}", [128, 1024], F32) for i in range(2)]
        mb = [sb(f"memb{i}", [128, 1024], BF16) for i in range(2)]
        lnp = sb("mlnp", [128, 2, 1024], F32)
        memT = sb("memT", [128, 8, 256], BF16)
        wmk = sb("wmk_sb", [128, 8, 256], BF16)
        wmv = sb("wmv_sb", [128, 8, 256], BF16)
        scr = [dict(stats=sb(f"mst{i}", [128, 12], F32), mv=sb(f"mmv{i}", [128, 2], F32), sd=sb(f"msd{i}", [128, 1], F32),
                    rstd=sb(f"mrs{i}", [128, 1], F32), nmr=sb(f"mnm{i}", [128, 1], F32), xn=sb(f"mxn{i}", [128, 1024], F32))
               for i in range(2)]
        P.dma_("sp", lnp[:, 0, :], lng, writes=[("mlnp", 0)])
        P.dma_("sp", lnp[:, 1, :], lnb, writes=[("mlnp", 1)])
        P.dma_("pool", wmk[:], wmk_d.rearrange("(kc p) n -> p kc n", p=128), writes=["wmk"])
        P.dma_("pool", wmv[:], wmv_d.rearrange("(kc p) n -> p kc n", p=128), writes=["wmv"])
        for i in range(2):
            P.dma_("act", mt_[i][:], mem[i * 128:(i + 1) * 128, :], writes=[("memt", i)])
            _ln_two_keys(C, mt_[i][:], [("memt", i)], mo[i][:], ("memo", i), lnp[:, 0, :], lnp[:, 1, :],
                         [("mlnp", 0), ("mlnp", 1)], scr[i], ("mln", i))
            P.copy("dve", mb[i][:], mo[i][:], reads=[("memo", i)], writes=[("memb", i)])
            transpose_tile_bf16(C, mb[i], ("memb", i), memT[:, :, i * 128:(i + 1) * 128], ("memT", i), idb, 8, [6, 7])
        for h in range(2):
            b = C.bank("mkv", [4, 5])
            for kc in range(8):
                P.mm(C.banks[b][:, 0:256], wmk[:, kc, h * 128:(h + 1) * 128], memT[:, kc, :], kc == 0, kc == 7,
                     reads=["wmk", ("memT", 0), ("memT", 1)], banks=[b])
            P.copy("dve", KmT[:, h, :], C.banks[b][:, 0:256], writes=[("KmT", h)], banks=[b])
        for mt in range(2):
            b = C.bank("mkv", [4, 5])
            for kc in range(8):
                P.mm(C.banks[b][:, 0:256], memT[:, kc, mt * 128:(mt + 1) * 128], wmv[:, kc, :], kc == 0, kc == 7,
                     reads=["wmv", ("memT", mt)], banks=[b])
            P.copy("act", Vm[:, mt, :], C.banks[b][:, 0:256], writes=[("Vm", mt)], banks=[b])
        P.barrier()
    return KmT, Vm


def _cat_dst(catT_out, k, tg):
    if callable(catT_out):
        return catT_out(k, tg)
    return catT_out[k * 128:(k + 1) * 128, tg * 512:(tg + 1) * 512]


def mem_attention_group(C, KmT, Vm, ones_bf, mqT, mqkey, out_tile, okey, pT_bufs, rden, tagi):
    P = C.P
    h = tagi
    pk = []
    for mt in range(2):
        b = C.bank("ms", [4, 5])
        P.mm(C.banks[b][:], KmT[:, h, mt * 128:(mt + 1) * 128], mqT, True, True, reads=[("KmT", h), mqkey], banks=[b])
        pi = C.bank("mpT", list(range(len(pT_bufs))))
        P.act(pT_bufs[pi][:], C.banks[b][:], AF.Exp, writes=[("mpT", pi)], banks=[b])
        pk.append(pi)
    bo = C.bank("mo", [6])
    bd = C.bank("md", [7])
    for mt in range(2):
        P.mm(C.banks[bo][:], Vm[:, mt, h * 128:(h + 1) * 128], pT_bufs[pk[mt]][:], mt == 0, mt == 1,
             reads=[("Vm", mt), ("mpT", pk[mt])], banks=[bo])
    for mt in range(2):
        P.mm(C.banks[bd][:], ones_bf[:], pT_bufs[pk[mt]][:], mt == 0, mt == 1,
             reads=["ones_bf", ("mpT", pk[mt])], banks=[bd])
    P.act(rden[:], C.banks[bd][:], AF.Ln, writes=["mrden0"], banks=[bd])
    P.act(rden[:], rden[:], AF.Exp, scale=-1.0, reads=["mrden0"], writes=["mrden"])
    P.tt("dve", out_tile, C.banks[bo][:], rden[:], ALU.mult, reads=["mrden"], writes=[okey], banks=[bo])


LAM_INIT1 = 0.8 - 0.6 * float(np.exp(-0.3))


def build_diff_phase(C, xT_d, x_is_f32, mem, mlng, mlnb, wmk_d, wmv_d, wq_d, wk_d, wv_d, wmq_d, lamv, subw, masks_d,
                     ident, catT_out, stack, dbg=None):
    nc, P = C.nc, C.P
    dbg = dbg or {}
    sbt = lambda name, shape, dt: stack.enter_context(nc.sbuf_tensor(C.pfx + name, shape, dt))
    outs = []
    idf = sbt("idf", [128, 128], F32)
    idb = sbt("idb", [128, 128], BF16)
    ones_bf = sbt("ones_bf", [128, 128], BF16)
    ones_f = sbt("ones_f", [128, 128], F32)
    ones_r = sbt("ones_r", [128, 128], F32R)
    P.dma_("sp", idf[:], ident, writes=["ident_f"])
    P.copy("dve", idb[:], idf[:], reads=["ident_f"], writes=["ident_bf"])
    P.op("dve", lambda E: E.memset(ones_bf[:], 1.0), writes=["ones_bf"])
    P.op("dve", lambda E: E.memset(ones_f[:], 1.0), writes=["ones_f"])
    P.copy("dve", ones_r[:], ones_f[:], reads=["ones_f"], writes=["ones_r"])
    KmT, Vm = build_mem_kv(C, mem, mlng, mlnb, wmk_d, wmv_d, idb, sbt, stack)

    qT = sbt("qT", [128, 4, T], BF16)
    kT = sbt("kT", [128, 4, T], BF16)
    vtok = sbt("vtok", [128, 32, 512], BF16)
    mqT = sbt("mqT", [128, 2, T], BF16)
    masks = sbt("masks_sb", [128, 4, 512], BF16)
    P.dma_("pool", masks[:], masks_d.rearrange("r p q -> p r q"), writes=["masks"])
    lv = sbt("lv", [128, 4, 64], F32)
    lprod = sbt("lprod", [128, 2, 64], F32)
    lsum = sbt("lsum", [128, 2], F32)
    lexp = sbt("lexp", [128, 2], F32)
    nlam = sbt("nlam", [128, 1], F32)
    swc = sbt("swc", [128, 2], F32)
    P.dma_("sp", lv[:], lamv, writes=["lv"])
    P.dma_("sp", swc[:, 0:1], subw, writes=["swc0"])
    P.tt("dve", lprod[:, 0, :], lv[:, 0, :], lv[:, 1, :], ALU.mult, reads=["lv"], writes=["lprod0"])
    P.tt("dve", lprod[:, 1, :], lv[:, 2, :], lv[:, 3, :], ALU.mult, reads=["lv"], writes=["lprod1"])
    P.op("dve", lambda E: E.reduce_sum(out=lsum[:, 0:1], in_=lprod[:, 0, :], axis=mybir.AxisListType.X), reads=["lprod0"], writes=["lsum0"])
    P.op("dve", lambda E: E.reduce_sum(out=lsum[:, 1:2], in_=lprod[:, 1, :], axis=mybir.AxisListType.X), reads=["lprod1"], writes=["lsum1"])
    P.act(lexp[:], lsum[:], AF.Exp, reads=["lsum0", "lsum1"], writes=["lexp"])
    P.tt("dve", nlam[:], lexp[:, 1:2], lexp[:, 0:1], ALU.subtract, reads=["lexp"], writes=["nlam0"])
    P.ts("dve", nlam[:], nlam[:], -LAM_INIT1, ALU.add, reads=["nlam0"], writes=["nlam"])
    P.ts("dve", swc[:, 1:2], swc[:, 0:1], 1.0 - LAM_INIT1, ALU.mult, reads=["swc0"], writes=["swc"])

    with ExitStack() as st:
        sb = lambda name, shape, dt: st.enter_context(nc.sbuf_tensor(C.pfx + name, shape, dt))
        xT_sb = sb("xT_sb", [128, 8, 2048], BF16)
        wq = sb("wq_sb", [128, 8, 512], BF16)
        wk = sb("wk_sb", [128, 8, 512], BF16)
        wv = sb("wv_sb", [128, 8, 512], BF16)
        wmq = sb("wmq_sb", [128, 8, 256], BF16)
        for wsb, wd, key in ((wq, wq_d, "wq"), (wk, wk_d, "wk"), (wv, wv_d, "wv"), (wmq, wmq_d, "wmq")):
            P.dma_("pool", wsb[:], wd.rearrange("(kc p) n -> p kc n", p=128), writes=[key])
        for tg in range(4):
            load_xT_group(C, xT_d, xT_sb, x_is_f32, tg)
        for tg in range(8):
            if tg >= 4:
                load_xT_group(C, xT_d, xT_sb, x_is_f32, tg)
            for h in range(4):
                b = proj_fm(C, wq, "wq", h * 128, xT_sb, tg, [0, 1, 2, 3])
                P.act(qT[:, h, tg * 512:(tg + 1) * 512], C.banks[b][:], AF.Copy, scale=0.125, writes=[("qT", h, tg)], banks=[b])
                b = proj_fm(C, wk, "wk", h * 128, xT_sb, tg, [0, 1, 2, 3])
                P.copy("dve", kT[:, h, tg * 512:(tg + 1) * 512], C.banks[b][:], writes=[("kT", h, tg)], banks=[b])
            for h in range(2):
                b = proj_fm(C, wmq, "wmq", h * 128, xT_sb, tg, [0, 1, 2, 3])
                P.act(mqT[:, h, tg * 512:(tg + 1) * 512], C.banks[b][:], AF.Copy, scale=128.0 ** -0.5, writes=[("mqT", h, tg)], banks=[b])
            for tl in range(4):
                t = tg * 4 + tl
                b = C.bank("pf", [0, 1, 2, 3])
                for kc in range(8):
                    P.mm(C.banks[b][:], xT_sb[:, kc, (t % 16) * 128:(t % 16 + 1) * 128], wv[:, kc, :], kc == 0, kc == 7,
                         reads=["wv", ("xT", tg % 4)], banks=[b])
                P.copy("dve" if tl % 2 else "act", vtok[:, t, :], C.banks[b][:], writes=[("vtok", t)], banks=[b])
        P.op("dve", lambda E: E.memset(idf[:, 0:1], 1.0) if False else E.memset(ones_f[:, 0:1], 1.0),
             reads=[], writes=[("xT", g) for g in range(4)] + ["wq", "wk", "wv", "wmq", "ones_f"])
        C.free_key = [("xT", g) for g in range(4)] + ["wq", "wk", "wv", "wmq"]
        P.barrier()

    pT = [sbt(f"pT{i}", [128, 512], BF16) for i in range(6)]
    mpT = [sbt(f"mpT{i}", [128, 512], BF16) for i in range(4)]
    rden = sbt("rden", [128, 512], F32)
    r12 = [sbt(f"r12_{i}", [128, 512], F32) for i in range(2)]
    o12 = [sbt(f"o12_{i}", [128, 512], F32) for i in range(2)]
    od = sbt("od", [128, 512], F32)
    sq = sbt("sq", [128, 512], F32R)
    rs = sbt("rs", [128, 512], F32)
    cat_t = [sbt(f"cat_t{i}", [128, 512], BF16) for i in range(3)]
    fk = C.free_key
    catv = catT_out
    nq = dbg.get("nqg", 8)
    for qg in range(nq):
        qs = slice(qg * 512, (qg + 1) * 512)
        for h in range(4):
            nkt = 4 * (qg + 1)

            def score(kt):
                pis = []
                for half in range(2):
                    b = C.bank("sT", [4, 5, 6, 7])
                    hp = slice(half * 64, (half + 1) * 64)
                    P.mm(C.banks[b][:], kT[hp, h, kt * 128:(kt + 1) * 128], qT[hp, h, qs], True, True,
                         reads=[("kT", h, kt // 4), ("qT", h, qg)], banks=[b])
                    pi = C.bank("pT", list(range(6)))
                    P.act(pT[pi][:], C.banks[b][:], AF.Exp, reads=fk, writes=[("pT", pi)], banks=[b])
                    if kt >= 4 * qg:
                        r = kt - 4 * qg
                        P.tt("pool", pT[pi][:], pT[pi][:], masks[:, r, :], ALU.mult, reads=["masks", ("pT", pi)], writes=[("pT", pi)])
                    pis.append(pi)
                return pis

            nxt = score(0)
            for kt in range(nkt):
                pis = nxt
                if kt + 1 < nkt:
                    nxt = score(kt + 1)
                for half in range(2):
                    P.mm(C.banks[half][:], vtok[:, kt, h * 128:(h + 1) * 128], pT[pis[half]][:], kt == 0, kt == nkt - 1,
                         reads=[("vtok", kt), ("pT", pis[half])], banks=[half])
                    P.mm(C.banks[2 + half][:], ones_bf[:], pT[pis[half]][:], kt == 0, kt == nkt - 1,
                         reads=["ones_bf", ("pT", pis[half])], banks=[2 + half])
            for half in range(2):
                P.act(r12[half][:], C.banks[2 + half][:], AF.Ln, reads=fk, writes=[("r12", half)], banks=[2 + half])
                P.act(r12[half][:], r12[half][:], AF.Exp, scale=-1.0, reads=[("r12", half)], writes=[("r12", half)])
                P.tt("dve", o12[half][:], C.banks[half][:], r12[half][:], ALU.mult, reads=[("r12", half)] + fk,
                     writes=[("o12", half)], banks=[half])
            P.stt("dve", od[:], o12[1][:], nlam[:], o12[0][:], ALU.mult, ALU.add, reads=[("o12", 0), ("o12", 1), "nlam"] + fk, writes=["od"])
            P.act(sq[:], od[:], AF.Square, reads=["od"] + fk, writes=["sq"])
            b = C.bank("sT", [4, 5, 6, 7])
            P.mm(C.banks[b][:], ones_r[:], sq[:], True, True, reads=["ones_r", "sq"], banks=[b])
            P.act(rs[:], C.banks[b][:], AF.Ln, bias=RMS_EPS, scale=1.0 / 128.0, reads=fk, writes=["rs0"], banks=[b])
            P.act(rs[:], rs[:], AF.Exp, scale=-0.5, reads=["rs0"], writes=["rs"])
            ci = C.bank("cat_t", [0, 1, 2])
            P.stt("dve", cat_t[ci][:], od[:], swc[:, 1:2], rs[:], ALU.mult, ALU.mult, reads=["od", "swc", "rs"] + fk, writes=[("cat_t", ci)])
            outs.append(P.dma_("sp", _cat_dst(catv, h, qg), cat_t[ci][:], reads=[("cat_t", ci)], writes=[("catdst", h, qg)]))
        for h in range(2):
            ci = C.bank("cat_t", [0, 1, 2])
            mem_attention_group(C, KmT, Vm, ones_bf, mqT[:, h, qs], ("mqT", h, qg), cat_t[ci][:], ("cat_t", ci), mpT, rden, h)
            outs.append(P.dma_("act", _cat_dst(catv, 4 + h, qg), cat_t[ci][:], reads=[("cat_t", ci)], writes=[("catdst", 4 + h, qg)]))
        if dbg.get("after_seg"):
            dbg["after_seg"](qg)
    return outs


def make_diff_program(x_is_f32=False, dbg=None):
    nc = bass.Bass("TRN2", target_bir_lowering=False)
    dt = lambda n, s, d, k="ExternalInput": nc.dram_tensor(n, s, d, kind=k).ap()
    xT_d = dt("xT", [1024, T], F32 if x_is_f32 else BF16)
    mem = dt("mem", [256, 1024], F32)
    mlng = dt("mlng", [128, 1024], F32)
    mlnb = dt("mlnb", [128, 1024], F32)
    wmk = dt("wmk", [1024, 256], F32)
    wmv = dt("wmv", [1024, 256], F32)
    wq = dt("wq", [1024, 512], F32)
    wk = dt("wk", [1024, 512], F32)
    wv = dt("wv", [1024, 512], F32)
    wmq = dt("wmq", [1024, 256], F32)
    lamv = dt("lamv", [128, 4, 64], F32)
    subw = dt("subw", [128, 1], F32)
    masks = dt("masks", [4, 128, 512], F32)
    ident = dt("ident", [128, 128], F32)
    catT_out = dt("catT_out", [768, T], BF16, "ExternalOutput")
    C = Ctx(nc)
    with ExitStack() as st:
        outs = build_diff_phase(C, xT_d, x_is_f32, mem, mlng, mlnb, wmk, wmv, wq, wk, wv, wmq, lamv, subw, masks, ident,
                                catT_out, st, dbg)
        with ExitStack() as st2:
            stats = C.P.emit(st2, final_wait_ops=outs)
    return nc, stats


def build_gdn_phase(C, xT_d, x_is_f32, mem, mlng, mlnb, wmk_d, wmv_d, wq_d, wk_d, wv_d, wz_d, wab_d, wmq_d,
                    convw_d, alog_d, dtb_d, gw_d, cm_d, ident, catT_out, stack, dbg=None):
    nc, P = C.nc, C.P
    dbg = dbg or {}
    sbt = lambda name, shape, dt: stack.enter_context(nc.sbuf_tensor(C.pfx + name, shape, dt))
    outs = []
    idf = sbt("idf", [128, 128], F32)
    idb = sbt("idb", [128, 128], BF16)
    ones_bf = sbt("ones_bf", [128, 128], BF16)
    ones_f = sbt("ones_f", [128, 128], F32)
    ones_r = sbt("ones_r", [128, 128], F32R)
    P.dma_("sp", idf[:], ident, writes=["ident_f"])
    P.copy("dve", idb[:], idf[:], reads=["ident_f"], writes=["ident_bf"])
    P.op("dve", lambda E: E.memset(ones_bf[:], 1.0), writes=["ones_bf"])
    P.op("dve", lambda E: E.memset(ones_f[:], 1.0), writes=["ones_f"])
    P.copy("dve", ones_r[:], ones_f[:], reads=["ones_f"], writes=["ones_r"])
    KmT, Vm = build_mem_kv(C, mem, mlng, mlnb, wmk_d, wmv_d, idb, sbt, stack)

    cm = sbt("cm_sb", [128, 6, 128], F32)
    P.dma_("sp", cm[:], cm_d.rearrange("r p q -> p r q"), writes=["cm"])
    TRI, SUT, MSL, MUI, ONA, ONB = [cm[:, i, :] for i in range(6)]
    sutr = sbt("sutr", [128, 128], F32R)
    P.copy("dve", sutr[:], SUT, reads=["cm"], writes=["sutr"])
    convw = sbt("convw_sb", [128, 12, 4], F32)
    P.dma_("sp", convw[:], convw_d, writes=["convw"])
    alog = sbt("alog_sb", [128, 4], F32)
    dtb = sbt("dtb_sb", [128, 4], F32)
    negA = sbt("negA", [128, 4], F32)
    gw = sbt("gw_sb", [128, 128], F32)
    P.dma_("act", alog[:], alog_d, writes=["alog"])
    P.dma_("act", dtb[:], dtb_d, writes=["dtb"])
    P.dma_("act", gw[:], gw_d, writes=["gw"])
    P.act(negA[:], alog[:], AF.Exp, reads=["alog"], writes=["negA0"])
    P.ts("dve", negA[:], negA[:], -1.0, ALU.mult, reads=["negA0"], writes=["negA"])

    xT_sb = sbt("xT_sb", [128, 8, 2048], BF16)
    wsb = {}
    for n, wd, cols in (("wq", wq_d, 512), ("wk", wk_d, 512), ("wv", wv_d, 512), ("wz", wz_d, 512), ("wab", wab_d, 8), ("wmq", wmq_d, 256)):
        wsb[n] = sbt(n + "_sb", [128, 8, cols], BF16)
        P.dma_("pool", wsb[n][:], wd.rearrange("(kc p) n -> p kc n", p=128), writes=[n])
    pre = [sbt(f"pre{c}", [128, 515], F32) for c in range(12)]
    cv = [sbt(f"cv{i}", [128, 512], F32) for i in range(4)]
    sl = [sbt(f"sl{i}", [128, 512], F32) for i in range(4)]
    sqb = [sbt(f"sqb{i}", [128, 512], F32R) for i in range(4)]
    rnb = [sbt(f"rnb{i}", [128, 512], F32) for i in range(4)]
    qT = sbt("gqT", [128, 4, 512], BF16)
    kT = sbt("gkT", [128, 4, 512], BF16)
    vT = sbt("gvT", [128, 4, 512], BF16)
    sz = sbt("sz", [128, 4, 512], BF16)
    mqT = sbt("gmqT", [128, 2, 512], BF16)
    catseg = sbt("catseg", [128, 6, 512], BF16)
    S = [sbt(f"S{h}", [128, 128], F32) for h in range(4)]
    Sb = [sbt(f"Sb{h}", [128, 128], BF16) for h in range(4)]
    for h in range(4):
        P.op("dve", lambda E, h=h: E.memset(S[h][:], 0.0), writes=[("S", h)])
        P.op("dve", lambda E, h=h: E.memset(Sb[h][:], 0.0), writes=[("Sb", h)])
    for c in range(12):
        P.op("pool", lambda E, c=c: E.memset(pre[c][:, 0:3], 0.0), writes=[("pre", c)])
    sc_sets = [{n: sbt(f"sc{j}_" + n, [128, 4], F32) for n in ("beta", "y", "ey", "sp", "g", "gc", "gam", "dca", "dcb", "tmp", "kdec", "bg")}
               for j in range(4)]
    NB2 = 4
    wt = {}
    for n, dt_ in (("gtri", F32R), ("Ds", F32), ("DmT", F32), ("L0", F32R), ("L1", F32R), ("U0", F32R), ("U1", F32R),
                   ("R0", F32R), ("R1", F32R), ("Av", F32), ("o", F32), ("og", F32)):
        wt[n] = [sbt(f"wt_{n}{i}", [128, 128], dt_) for i in range(NB2)]
    for n in ("AT", "Rb", "kbg", "kd", "vb", "nwT", "vn", "og2"):
        wt[n] = [sbt(f"wt_{n}{i}", [128, 128], BF16) for i in range(8 if n in ("AT", "Rb", "kd", "vb", "nwT") else NB2)]
    gss = [sbt(f"gss{i}", [128, 1], F32) for i in range(NB2)]
    grs = [sbt(f"grs{i}", [128, 1], F32) for i in range(NB2)]
    junk = [sbt(f"junk{i}", [128, 128], F32) for i in range(NB2)]
    mpT = [sbt(f"mpT{i}", [128, 512], BF16) for i in range(4)]
    rden = sbt("rden", [128, 512], F32)

    def small_bank(pool="sm"):
        return C.bank(pool, [2, 3, 4, 5, 6, 7])

    nseg = dbg.get("nseg", 8)
    for tg in range(nseg):
        load_xT_group(C, xT_d, xT_sb, x_is_f32, tg)
        slot = tg % 4
        def s2_chain(c):
            qkv, h = c // 4, c % 4
            wn = ("wq", "wk", "wv")[qkv]
            if tg > 0:
                P.copy("pool", pre[c][:, 0:3], pre[c][:, 512:515], reads=[("pre", c)], writes=[("pre", c)])
                yield
            b = proj_fm(C, wsb[wn], wn, h * 128, xT_sb, tg, [0, 1, 2, 3, 4, 5, 6, 7])
            P.copy("act", pre[c][:, 3:515], C.banks[b][:], reads=[("pre", c)], writes=[("pre", c)], banks=[b])
            yield
            ci = c % 4
            ce = "dve"
            P.ts(ce, cv[ci][:], pre[c][:, 3:515], convw[:, c, 3:4], ALU.mult, reads=[("pre", c), "convw"], writes=[("cv", ci)])
            yield
            for j in (2, 1, 0):
                P.stt(ce, cv[ci][:], pre[c][:, j:j + 512], convw[:, c, j:j + 1], cv[ci][:], ALU.mult, ALU.add,
                      reads=[("pre", c), ("cv", ci), "convw"], writes=[("cv", ci)])
                yield
            if qkv == 2:
                P.act(vT[:, h, :], cv[ci][:], AF.Silu, reads=[("cv", ci)], writes=[("vT", h)])
                yield
            else:
                si = c % 4
                P.act(sl[si][:], cv[ci][:], AF.Silu, reads=[("cv", ci)], writes=[("sl", si)])
                yield
                P.act(sqb[si][:], sl[si][:], AF.Square, reads=[("sl", si)], writes=[("sqb", si)])
                yield
                b2 = C.bank("pf", [0, 1, 2, 3, 4, 5, 6, 7])
                P.mm(C.banks[b2][:], ones_r[:], sqb[si][:], True, True, reads=["ones_r", ("sqb", si)], banks=[b2])
                yield
                if qkv == 0:
                    P.act(rnb[si][:], C.banks[b2][:], AF.Ln, bias=128.0 * RMS_EPS, scale=128.0, writes=[("rnb", si)], banks=[b2])
                else:
                    P.act(rnb[si][:], C.banks[b2][:], AF.Ln, bias=RMS_EPS, scale=1.0, writes=[("rnb", si)], banks=[b2])
                P.act(rnb[si][:], rnb[si][:], AF.Exp, scale=-0.5, reads=[("rnb", si)], writes=[("rnb", si)])
                yield
                dst = qT if qkv == 0 else kT
                P.tt("pool", dst[:, h, :], sl[si][:], rnb[si][:], ALU.mult, reads=[("sl", si), ("rnb", si)],
                     writes=[(("qT", "kT")[qkv], h)])
                yield
        for w_ in range(3):
            gens = [s2_chain(c) for c in range(w_ * 4, w_ * 4 + 4)]
            while gens:
                for g_ in list(gens):
                    try:
                        next(g_)
                    except StopIteration:
                        gens.remove(g_)
        for h in range(2):
            b = proj_fm(C, wsb["wmq"], "wmq", h * 128, xT_sb, tg, [0, 1])
            P.act(mqT[:, h, :], C.banks[b][:], AF.Copy, scale=128.0 ** -0.5, writes=[("mqT", h)], banks=[b])
        def scal(tl):
            ts_ = slice(tl * 128, (tl + 1) * 128)
            xs = slice(slot * 512 + tl * 128, slot * 512 + (tl + 1) * 128)
            sc = sc_sets[tl]
            sk = lambda n, _p=tl: (n, _p)
            b = C.bank("pf", [0, 1])
            for kc in range(8):
                P.mm(C.banks[b][:], xT_sb[:, kc, xs], wsb["wz"][:, kc, :], kc == 0, kc == 7, reads=["wz", ("xT", slot)], banks=[b])
            P.act(sz[:, tl, :], C.banks[b][:], AF.Silu, writes=[("sz", tl)], banks=[b])
            yield
            b = small_bank()
            pab = C.banks[b][:, 0:8]
            for kc in range(8):
                P.mm(pab, xT_sb[:, kc, xs], wsb["wab"][:, kc, :], kc == 0, kc == 7, reads=["wab", ("xT", slot)], banks=[b])
            P.act(sc["beta"][:], C.banks[b][:, 4:8], AF.Sigmoid, writes=[sk("beta")], banks=[b])
            yield
            P.tt("dve", sc["y"][:], C.banks[b][:, 0:4], dtb[:], ALU.add, reads=["dtb"], writes=[sk("y")], banks=[b])
            yield
            P.act(sc["ey"][:], sc["y"][:], AF.Exp, reads=[sk("y")], writes=[sk("ey")])
            yield
            P.act(sc["sp"][:], sc["ey"][:], AF.Ln, bias=1.0, reads=[sk("ey")], writes=[sk("sp")])
            yield
            P.tt("dve", sc["g"][:], sc["sp"][:], negA[:], ALU.mult, reads=[sk("sp"), "negA"], writes=[sk("g")])
            yield
            b = small_bank()
            pb = C.banks[b]
            P.mm(pb[:, 0:4], TRI, sc["g"][:], True, True, reads=["cm", sk("g")], banks=[b])
            yield
            P.mm(pb[:, 8:12], ONA, sc["g"][:], True, True, reads=["cm", sk("g")], banks=[b])
            yield
            P.mm(pb[:, 16:20], ONB, sc["g"][:], True, True, reads=["cm", sk("g")], banks=[b])
            yield
            P.copy("dve", sc["gc"][:], pb[:, 0:4], writes=[sk("gc")], banks=[b])
            yield
            P.act(sc["gam"][:], pb[:, 0:4], AF.Exp, writes=[sk("gam")], banks=[b])
            yield
            P.act(sc["dca"][:], pb[:, 8:12], AF.Exp, writes=[sk("dca")], banks=[b])
            yield
            P.act(sc["dcb"][:], pb[:, 16:20], AF.Exp, writes=[sk("dcb")], banks=[b])
            yield
            P.tt("dve", sc["tmp"][0:64, :], pb[0:64, 8:12], sc["gc"][0:64, :], ALU.subtract, reads=[sk("gc")], writes=[sk("tmpa")], banks=[b])
            yield
            P.tt("dve", sc["tmp"][64:128, :], pb[64:128, 16:20], sc["gc"][64:128, :], ALU.subtract, reads=[sk("gc")], writes=[sk("tmpb")], banks=[b])
            yield
            P.act(sc["kdec"][:], sc["tmp"][:], AF.Exp, reads=[sk("tmpa"), sk("tmpb")], writes=[sk("kdec")])
            yield
            P.tt("dve", sc["bg"][:], sc["beta"][:], sc["gam"][:], ALU.mult, reads=[sk("beta"), sk("gam")], writes=[sk("bg")])
            yield
        gens = [scal(tl) for tl in range(4)]
        while gens:
            for g_ in list(gens):
                try:
                    next(g_)
                except StopIteration:
                    gens.remove(g_)
        HAND = ("AT", "Rb", "kd", "vb", "nwT")

        def pre_gen(tl, h):
                ts_ = slice(tl * 128, (tl + 1) * 128)
                sc = sc_sets[tl]
                sk = lambda n, _p=tl: (n, _p)
                i = h
                ih = (tl % 2) * 4 + h
                W = {n: (wt[n][ih] if n in HAND else wt[n][h]) for n in wt}
                k_ = lambda n: (n, ih if n in HAND else h)
                kt_ap = kT[:, h, ts_]
                qt_ap = qT[:, h, ts_]
                P.ts("dve", W["gtri"][:], TRI, sc["g"][:, h:h + 1], ALU.mult, reads=["cm", sk("g")], writes=[k_("gtri")])
                bE = small_bank()
                P.mm(C.banks[bE][:, 0:128], W["gtri"][:], sutr[:], True, True, reads=[k_("gtri"), "sutr"], banks=[bE])
                P.mm(C.banks[bE][:, 128:256], sutr[:], W["gtri"][:], True, True, reads=[k_("gtri"), "sutr"], banks=[bE])
                P.act(W["Ds"][:], C.banks[bE][:, 0:128], AF.Exp, writes=[k_("Ds")], banks=[bE])
                P.act(W["DmT"][:], C.banks[bE][:, 128:256], AF.Exp, writes=[k_("DmT")], banks=[bE])
                P.tt("pool", W["Ds"][:], W["Ds"][:], MSL, ALU.mult, reads=[k_("Ds"), "cm"], writes=[k_("Ds")])
                P.tt("pool", W["DmT"][:], W["DmT"][:], MUI, ALU.mult, reads=[k_("DmT"), "cm"], writes=[k_("DmT")])
                yield
                bK = small_bank()
                P.mm(C.banks[bK][:, 0:128], kt_ap, kt_ap, True, True, reads=[("kT", h)], banks=[bK])
                P.mm(C.banks[bK][:, 128:256], kt_ap, qt_ap, True, True, reads=[("kT", h), ("qT", h)], banks=[bK])
                P.stt("dve", W["L0"][:], C.banks[bK][:, 0:128], sc["beta"][:, h:h + 1], W["Ds"][:], ALU.mult, ALU.mult,
                      reads=[sk("beta"), k_("Ds")], writes=[k_("L0")], banks=[bK])
                P.tt("dve", W["AT"][:], C.banks[bK][:, 128:256], W["DmT"][:], ALU.mult, reads=[k_("DmT")], writes=[k_("AT")], banks=[bK])
                yield
                bU = small_bank()
                P.tr(C.banks[bU][:, 0:128], W["L0"][:].bitcast(F32), idf[:], reads=[k_("L0"), "ident_f"], banks=[bU])
                P.copy("act", W["U0"][:], C.banks[bU][:, 0:128], writes=[k_("U0")], banks=[bU])
                P.tt("dve", W["R0"][:], idf[:], W["U0"][:], ALU.subtract, reads=["ident_f", k_("U0")], writes=[k_("R0")])
                yield
                Lc, Uc, Rc = "L0", "U0", "R0"
                for it in range(5):
                    Ln_, Un_, Rn_ = ("L1", "U1", "R1") if Lc == "L0" else ("L0", "U0", "R0")
                    bN = small_bank()
                    P.mm(C.banks[bN][:, 0:128], W[Uc][:], W[Lc][:], True, True, reads=[k_(Uc), k_(Lc)], banks=[bN])
                    if it < 4:
                        P.mm(C.banks[bN][:, 128:256], W[Lc][:], W[Uc][:], True, True, reads=[k_(Uc), k_(Lc)], banks=[bN])
                    P.copy("act", W[Ln_][:], C.banks[bN][:, 0:128], writes=[k_(Ln_)], banks=[bN])
                    if it < 4:
                        P.copy("dve", W[Un_][:], C.banks[bN][:, 128:256], writes=[k_(Un_)], banks=[bN])
                    bR = small_bank()
                    P.mm(C.banks[bR][:, 0:128], W[Ln_][:], W[Rc][:], True, True, reads=[k_(Ln_), k_(Rc)], banks=[bR])
                    P.tt("dve", W[Rn_][:], C.banks[bR][:, 0:128], W[Rc][:], ALU.add, reads=[k_(Rc)], writes=[k_(Rn_)], banks=[bR])
                    yield
                    Lc, Uc, Rc = Ln_, Un_, Rn_
                P.copy("act", W["Rb"][:], W[Rc][:], reads=[k_(Rc)], writes=[k_("Rb")])
                bT = small_bank()
                pbt = C.banks[bT][:].bitcast(BF16)
                P.tr(pbt[:, 0:128], kt_ap, idb[:], reads=[("kT", h), "ident_bf"], banks=[bT])
                P.tr(pbt[:, 128:256], vT[:, h, ts_], idb[:], reads=[("vT", h), "ident_bf"], banks=[bT])
                P.ts("dve", W["kbg"][:], pbt[:, 0:128], sc["bg"][:, h:h + 1], ALU.mult, reads=[sk("bg")], writes=[k_("kbg")], banks=[bT])
                P.act(W["kd"][:], pbt[:, 0:128], AF.Identity, scale=sc["kdec"][:, h:h + 1], reads=[sk("kdec")], writes=[k_("kd")], banks=[bT])
                P.ts("dve", W["vb"][:], pbt[:, 128:256], sc["beta"][:, h:h + 1], ALU.mult, reads=[sk("beta")], writes=[k_("vb")], banks=[bT])
                yield
                bW = small_bank()
                P.mm(C.banks[bW][:, 0:128], W["kbg"][:], W["Rb"][:], True, True, reads=[k_("kbg"), k_("Rb")], banks=[bW])
                P.act(W["nwT"][:], C.banks[bW][:, 0:128], AF.Copy, scale=-1.0, writes=[k_("nwT")], banks=[bW])
                yield

        def run_gen(tl, h):
                ts_ = slice(tl * 128, (tl + 1) * 128)
                sc = sc_sets[tl]
                sk = lambda n, _p=tl: (n, _p)
                i = h
                ih = (tl % 2) * 4 + h
                W = {n: (wt[n][ih] if n in HAND else wt[n][h]) for n in wt}
                k_ = lambda n: (n, ih if n in HAND else h)
                kt_ap = kT[:, h, ts_]
                qt_ap = qT[:, h, ts_]
                for half in range(2):
                    rows = slice(half * 64, (half + 1) * 64)
                    M = 64 if half == 0 else 128
                    dc = sc["dca"] if half == 0 else sc["dcb"]
                    bV = small_bank()
                    P.mm(C.banks[bV][0:M, 0:128], W["Rb"][rows, 0:M], W["vb"][rows, :], True, False,
                         reads=[k_("Rb"), k_("vb")], banks=[bV])
                    P.mm(C.banks[bV][0:M, 0:128], W["nwT"][:, 0:M], Sb[h][:], False, True,
                         reads=[k_("nwT"), ("Sb", h)], banks=[bV])
                    P.copy("act", W["vn"][rows, :], C.banks[bV][rows, 0:128], writes=[(k_("vn"), half)], banks=[bV])
                    yield
                    bO = small_bank()
                    P.mm(C.banks[bO][0:M, 0:128], qt_ap[:, 0:M], Sb[h][:], True, True, reads=[("qT", h), ("Sb", h)], banks=[bO])
                    P.mm(C.banks[bO][0:M, 128:256], W["AT"][rows, 0:M], W["vn"][rows, :], True, True,
                         reads=[k_("AT"), (k_("vn"), half)], banks=[bO])
                    P.copy("act", W["Av"][rows, :], C.banks[bO][rows, 128:256], writes=[(k_("Av"), half)], banks=[bO])
                    P.stt("dve", W["o"][rows, :], C.banks[bO][rows, 0:128], sc["gam"][rows, h:h + 1], W["Av"][rows, :], ALU.mult, ALU.add,
                          reads=[sk("gam"), (k_("Av"), half)], writes=[(k_("o"), half)], banks=[bO])
                    yield
                    bS = small_bank()
                    P.mm(C.banks[bS][:, 0:128], W["kd"][rows, :], W["vn"][rows, :], True, True,
                         reads=[k_("kd"), (k_("vn"), half)], banks=[bS])
                    P.stt("dve", S[h][:], S[h][:], dc[:, h:h + 1], C.banks[bS][:, 0:128], ALU.mult, ALU.add,
                          reads=[("S", h), sk("dca"), sk("dcb")], writes=[("S", h)], banks=[bS])
                    P.copy("act", Sb[h][:], S[h][:], reads=[("S", h)], writes=[("Sb", h)])
                    yield
                okeys = [(k_("o"), 0), (k_("o"), 1)]
                P.op("pool", lambda E, i=i: E.memset(gss[i][:], 0.0), writes=[("gss", i)])
                P.act(junk[i][:], W["o"][:], AF.Square, accum_out=gss[i][:], reads=okeys + [("gss", i)], writes=[("junk", i), ("gss", i)])
                P.act(grs[i][:], gss[i][:], AF.Sqrt, bias=RMS_EPS, scale=1.0 / 128.0, reads=[("gss", i)], writes=[("grs", i)])
                P.op("dve", lambda E, i=i: E.reciprocal(out=grs[i][:], in_=grs[i][:]), reads=[("grs", i)], writes=[("grs", i)])
                P.stt("dve", W["og"][:], W["o"][:], grs[i][:], gw[:], ALU.mult, ALU.mult, reads=okeys + [("grs", i), "gw"], writes=[k_("og")])
                yield
                P.tt("pool", W["og2"][:], W["og"][:], sz[:, tl, h * 128:(h + 1) * 128], ALU.mult, reads=[k_("og"), ("sz", tl)], writes=[k_("og2")])
                bG = small_bank()
                pbg = C.banks[bG][:].bitcast(BF16)
                P.tr(pbg[:, 0:128], W["og2"][:], idb[:], reads=[k_("og2"), "ident_bf"], banks=[bG])
                P.copy("act", catseg[:, h, ts_], pbg[:, 0:128], writes=[("catseg", h, tl)], banks=[bG])

        for step in range(5):
            gens = []
            if step >= 1:
                gens += [run_gen(step - 1, h) for h in range(4)]
            if step < 4:
                gens += [pre_gen(step, h) for h in range(4)]
            while gens:
                for g_ in list(gens):
                    try:
                        next(g_)
                    except StopIteration:
                        gens.remove(g_)
        qs = slice(tg * 512, (tg + 1) * 512)
        for h in range(4):
            outs.append(P.dma_("sp", _cat_dst(catT_out, h, tg), catseg[:, h, :], reads=[("catseg", h, tl) for tl in range(4)], writes=[("catdst", h, tg)]))
        for h in range(2):
            mem_attention_group(C, KmT, Vm, ones_bf, mqT[:, h, :], ("mqT", h), catseg[:, 4 + h, :], ("catseg", 4 + h), mpT, rden, h)
            outs.append(P.dma_("act", _cat_dst(catT_out, 4 + h, tg), catseg[:, 4 + h, :], reads=[("catseg", 4 + h)], writes=[("catdst", 4 + h, tg)]))
        if dbg.get("after_seg"):
            dbg["after_seg"](tg)
    return outs


def make_gdn_program(x_is_f32=True, dbg=None):
    nc = bass.Bass("TRN2", target_bir_lowering=False)
    dt = lambda n, s, d, k="ExternalInput": nc.dram_tensor(n, s, d, kind=k).ap()
    xT_d = dt("xT", [1024, T], F32 if x_is_f32 else BF16)
    mem = dt("mem", [256, 1024], F32)
    mlng = dt("mlng", [128, 1024], F32)
    mlnb = dt("mlnb", [128, 1024], F32)
    wmk = dt("wmk", [1024, 256], F32)
    wmv = dt("wmv", [1024, 256], F32)
    wq = dt("wq", [1024, 512], F32)
    wk = dt("wk", [1024, 512], F32)
    wv = dt("wv", [1024, 512], F32)
    wz = dt("wz", [1024, 512], F32)
    wab = dt("wab", [1024, 8], F32)
    wmq = dt("wmq", [1024, 256], F32)
    convw = dt("convw", [128, 12, 4], F32)
    alog = dt("alog", [128, 4], F32)
    dtb = dt("dtb", [128, 4], F32)
    gw = dt("gw", [128, 128], F32)
    cm = dt("cm", [6, 128, 128], F32)
    ident = dt("ident", [128, 128], F32)
    catT_out = dt("catT_out", [768, T], BF16, "ExternalOutput")
    C = Ctx(nc)
    with ExitStack() as st:
        outs = build_gdn_phase(C, xT_d, x_is_f32, mem, mlng, mlnb, wmk, wmv, wq, wk, wv, wz, wab, wmq, convw, alog, dtb, gw,
                               cm, ident, catT_out, st, dbg)
        with ExitStack() as st2:
            stats = C.P.emit(st2, final_wait_ops=outs)
    return nc, stats


def gdn_const_masks():
    i = np.arange(128)
    same = (i[:, None] // 64) == (i[None, :] // 64)
    tri = ((i[:, None] <= i[None, :]) & same)
    sut = (i[:, None] > i[None, :])
    msl = ((i[:, None] > i[None, :]) & same)
    mui = ((i[:, None] <= i[None, :]) & same)
    ona = np.broadcast_to((i[:, None] < 64), (128, 128))
    onb = np.broadcast_to((i[:, None] >= 64), (128, 128))
    return np.stack([tri, sut, msl, mui, ona, onb]).astype(np.float32)


def _rep(v, n=128):
    v = np.asarray(v, np.float32)
    return np.ascontiguousarray(np.broadcast_to(v[None, :], (n, v.shape[0])))


def _c(a):
    return np.ascontiguousarray(a)


def _diff_masks():
    masks = np.zeros((4, 128, 512), np.float32)
    for r in range(4):
        masks[r] = (128 * r + np.arange(128)[:, None] <= np.arange(512)[None, :]).astype(np.float32)
    return masks


def _mem_inputs(inp, b, hh):
    w = inp["w_mem_kv"]
    return dict(mem=_c(inp["mem"][b]), mlng=_rep(inp["mem_ln_g"]), mlnb=_rep(inp["mem_ln_b"]),
                wmk=_c(w[:, hh * 256:hh * 256 + 256]), wmv=_c(w[:, 512 + hh * 256:512 + hh * 256 + 256]),
                ident=np.eye(128, dtype=np.float32))


def _gdn_inputs(inp, b, hh):
    w_in = inp["l0_w_in"]
    conv_w = inp["l0_conv_w"]
    convw = np.zeros((128, 12, 4), np.float32)
    for qkv in range(3):
        for h in range(4):
            ch0 = qkv * 1024 + (hh * 4 + h) * 128
            convw[:, qkv * 4 + h, :] = conv_w[:, ch0:ch0 + 128].T
    ab = np.concatenate([w_in[:, 4096 + hh * 4:4096 + hh * 4 + 4], w_in[:, 4104 + hh * 4:4104 + hh * 4 + 4]], axis=1)
    d = _mem_inputs(inp, b, hh)
    d.update(xT=_c(inp["x"][b].T),
             wq=_c(w_in[:, hh * 512:hh * 512 + 512]), wk=_c(w_in[:, 1024 + hh * 512:1024 + hh * 512 + 512]),
             wv=_c(w_in[:, 2048 + hh * 512:2048 + hh * 512 + 512]), wz=_c(w_in[:, 3072 + hh * 512:3072 + hh * 512 + 512]),
             wab=_c(ab), wmq=_c(w_in[:, 4112 + hh * 256:4112 + hh * 256 + 256]),
             convw=convw, alog=_rep(inp["l0_a_log"][hh * 4:hh * 4 + 4]), dtb=_rep(inp["l0_dt_bias"][hh * 4:hh * 4 + 4]),
             gw=_rep(inp["l0_gate_norm_w"]), cm=gdn_const_masks())
    return d


def _diff_inputs(inp, b, hh, xT_bf):
    w_in = inp["l1_w_in"]
    d = _mem_inputs(inp, b, hh)
    lam = np.stack([inp["l1_lambda_q1"], inp["l1_lambda_k1"], inp["l1_lambda_q2"], inp["l1_lambda_k2"]]).astype(np.float32)
    d.update(xT=xT_bf,
             wq=_c(w_in[:, hh * 512:hh * 512 + 512]), wk=_c(w_in[:, 1024 + hh * 512:1024 + hh * 512 + 512]),
             wv=_c(w_in[:, 2048 + hh * 512:2048 + hh * 512 + 512]), wmq=_c(w_in[:, 3072 + hh * 256:3072 + hh * 256 + 256]),
             lamv=_c(np.broadcast_to(lam[None], (128, 4, 64))), subw=_c(np.asarray(inp["l1_subln_w"], np.float32)[:, None]),
             masks=_diff_masks())
    return d


def _ffn_inputs(inp, layer, catT_pair, th, xres):
    p = f"l{layer}_"
    c0, c1 = catT_pair
    ts = slice(th * 2048, (th + 1) * 2048)
    catT = np.concatenate([c0[0:512, ts], c1[0:512, ts], c0[512:768, ts], c1[512:768, ts]], axis=0)
    return dict(catT=_c(catT), xres=_c(xres), w_out=_c(inp[p + "w_out"]), w_ff1=_c(inp[p + "w_ff1"]), w_ff2=_c(inp[p + "w_ff2"]),
                ln1g=_rep(inp[p + "ln1_g"]), ln1b=_rep(inp[p + "ln1_b"]), ln2g=_rep(inp[p + "ln2_g"]), ln2b=_rep(inp[p + "ln2_b"]),
                ident=np.eye(128, dtype=np.float32))


def kernel(**inputs):
    inp = {k: np.asarray(v) for k, v in inputs.items()}
    cores = list(range(8))
    ncA, _ = make_gdn_program(x_is_f32=True)
    resA = run_bass_kernel_spmd(ncA, [_gdn_inputs(inp, c // 2, c % 2) for c in cores], core_ids=cores)
    catA = [np.asarray(r["catT_out"]) for r in resA.results]
    ncB, _ = make_ffn_program()
    inB = [_ffn_inputs(inp, 0, (catA[2 * (c // 2)], catA[2 * (c // 2) + 1]), c % 2,
                       inp["x"][c // 2, (c % 2) * 2048:(c % 2 + 1) * 2048, :]) for c in cores]
    resB = run_bass_kernel_spmd(ncB, inB, core_ids=cores)
    x1 = [np.asarray(r["x_out"]) for r in resB.results]
    x1T = [np.asarray(r["xT_out"]) for r in resB.results]
    ncC, _ = make_diff_program(x_is_f32=False)
    inC = []
    for c in cores:
        b = c // 2
        xT_bf = _c(np.concatenate([x1T[2 * b], x1T[2 * b + 1]], axis=1))
        inC.append(_diff_inputs(inp, b, c % 2, xT_bf))
    resC = run_bass_kernel_spmd(ncC, inC, core_ids=cores)
    catC = [np.asarray(r["catT_out"]) for r in resC.results]
    ncD, _ = make_ffn_program()
    inD = [_ffn_inputs(inp, 1, (catC[2 * (c // 2)], catC[2 * (c // 2) + 1]), c % 2, x1[c]) for c in cores]
    resD = run_bass_kernel_spmd(ncD, inD, core_ids=cores)
    out = np.zeros((NB, T, D), np.float32)
    for c in cores:
        out[c // 2, (c % 2) * 2048:(c % 2 + 1) * 2048, :] = np.asarray(resD.results[c]["x_out"])
    return out


def make_fused_program(n_cores=8, dbg=None):
    dbg = dbg or {}
    nc = bass.Bass("TRN2", target_bir_lowering=False)
    dt = lambda n, s, d, k="ExternalInput": nc.dram_tensor(n, s, d, kind=k).ap()
    groups = [[2 * i, 2 * i + 1] for i in range(n_cores // 2)]
    ident = dt("ident", [128, 128], F32)
    sel_d = dt("sel", [128, 2], F32)
    mem = dt("mem", [256, 1024], F32)
    mlng = dt("mlng", [128, 1024], F32)
    mlnb = dt("mlnb", [128, 1024], F32)
    wmk = dt("wmk", [1024, 256], F32)
    wmv = dt("wmv", [1024, 256], F32)
    a_xT = dt("a_xT", [1024, T], F32)
    a_w = {n: dt("a_" + n, [1024, c], F32) for n, c in (("wq", 512), ("wk", 512), ("wv", 512), ("wz", 512), ("wab", 8), ("wmq", 256))}
    a_convw = dt("a_convw", [128, 12, 4], F32)
    a_alog = dt("a_alog", [128, 4], F32)
    a_dtb = dt("a_dtb", [128, 4], F32)
    a_gw = dt("a_gw", [128, 128], F32)
    a_cm = dt("a_cm", [6, 128, 128], F32)
    c_w = {n: dt("c_" + n, [1024, c], F32) for n, c in (("wq", 512), ("wk", 512), ("wv", 512), ("wmq", 256))}
    c_lamv = dt("c_lamv", [128, 4, 64], F32)
    c_subw = dt("c_subw", [128, 1], F32)
    c_masks = dt("c_masks", [4, 128, 512], F32)
    f_in = {}
    for L in ("b", "d"):
        f_in[L] = dict(w_out=dt(L + "_w_out", [1536, 1024], F32), w_ff1=dt(L + "_w_ff1", [1024, 4096], F32),
                       w_ff2=dt(L + "_w_ff2", [4096, 1024], F32),
                       lns=[dt(L + "_" + n, [128, 1024], F32) for n in ("ln1g", "ln1b", "ln2g", "ln2b")])
    b_xres = dt("b_xres", [2048, 1024], F32)
    out_d = dt("out", [2048, 1024], F32, "ExternalOutput")
    it = lambda n, s, d: nc.dram_tensor(n, s, d).ap()
    cat_send = [it(f"cat_send{i}", [24 * 128, 1024], BF16) for i in range(2)]
    cat_recv = [it(f"cat_recv{i}", [24 * 256, 1024], BF16) for i in range(2)]
    x_send = it("x_send", [16 * 128, 1024], BF16)
    x_recv = it("x_recv", [16 * 256, 1024], BF16)
    x1_d = it("x1_d", [2048, 1024], F32)

    C = Ctx(nc)
    P = C.P

    def cat_dst_fn(i):
        v = cat_send[i].rearrange("(k q p) t -> k q p t", q=4, p=128)
        return lambda k, tg: v[k, tg // 2, :, (tg % 2) * 512:(tg % 2 + 1) * 512]

    def finish_exchange():
        P.coll_wait()
        P.barrier()

    def cat_after_seg(i):
        def f(tg):
            if tg % 2 == 1:
                q = tg // 2
                for k in range(6):
                    c = k * 4 + q
                    P.coll("AllGather", groups, cat_send[i][c * 128:(c + 1) * 128, :], cat_recv[i][c * 256:(c + 1) * 256, :],
                           reads=[("catdst", k, tg - 1), ("catdst", k, tg)], wait=False)
        return f

    def x_after_pass(ps):
        for kc in range(8):
            c = kc * 2 + ps
            P.coll("AllGather", groups, x_send[c * 128:(c + 1) * 128, :], x_recv[c * 256:(c + 1) * 256, :],
                   reads=[("xsend", ps * 1024 + j * 128) for j in range(8)], wait=False)

    def make_cat_loader(i, selt, tmpA, tmpB):
        rv = cat_recv[i].rearrange("(k q r p) t -> p r k q t", q=4, r=2, p=128)

        def loader(dst, key, g):
            off = (g % 2) * 512
            for r in range(2):
                P.dma_("sp", tmpA[:], rv[:, r, :, g // 2, off:off + 512], writes=["catA"])
                P.dma_("act", tmpB[:], rv[:, r, :, 2 + g // 2, off:off + 512], writes=["catB"])
                P.ts("dve", tmpB[:], tmpB[:], selt[:, 1:2], ALU.mult, reads=["catB", "sel"], writes=["catB"])
                P.stt("dve", dst[:, r * 6:(r + 1) * 6, :], tmpA[:], selt[:, 0:1], tmpB[:], ALU.mult, ALU.add,
                      reads=["catA", "catB", "sel"], writes=[key])
        return loader

    xs_v = x_send.rearrange("(kc q p) t -> p kc q t", q=2, p=128)

    def xT_dst(tok):
        return xs_v[:, :, tok // 1024, tok % 1024:tok % 1024 + 128]

    C.pfx = "A_"
    with ExitStack() as st:
        build_gdn_phase(C, a_xT, True, mem, mlng, mlnb, wmk, wmv, a_w["wq"], a_w["wk"], a_w["wv"], a_w["wz"], a_w["wab"], a_w["wmq"],
                        a_convw, a_alog, a_dtb, a_gw, a_cm, ident, cat_dst_fn(0), st, dict(dbg, after_seg=cat_after_seg(0)))
        finish_exchange()
    C.pfx = "B_"
    with ExitStack() as st:
        selt = st.enter_context(nc.sbuf_tensor("B_sel", [128, 2], F32))
        tmpA = st.enter_context(nc.sbuf_tensor("B_tmpA", [128, 6, 512], BF16))
        tmpB = st.enter_context(nc.sbuf_tensor("B_tmpB", [128, 6, 512], BF16))
        P.dma_("sp", selt[:], sel_d, writes=["sel"])
        d2 = dict(dbg)
        d2.update(cat_loader=make_cat_loader(0, selt, tmpA, tmpB), xT_dst=xT_dst, after_pass=x_after_pass)
        fi = f_in["b"]
        build_ffn_phase(C, None, b_xres, fi["w_out"], fi["w_ff1"], fi["w_ff2"], *fi["lns"], ident, x1_d, None, st, d2)
        finish_exchange()
    C.pfx = "C_"
    xr_v = x_recv.rearrange("(kc q r p) t -> p kc q r t", q=2, r=2, p=128)

    class _XT:
        def rearrange(self, *_a, **_k):
            return self

        def __getitem__(self, idx):
            tsl = idx[2]
            tg = tsl.start // 512
            r, l = tg // 4, (tg % 4)
            return xr_v[:, :, l // 2, r, (l % 2) * 512:(l % 2 + 1) * 512]

    with ExitStack() as st:
        build_diff_phase(C, _XT(), False, mem, mlng, mlnb, wmk, wmv, c_w["wq"], c_w["wk"], c_w["wv"], c_w["wmq"], c_lamv, c_subw,
                         c_masks, ident, cat_dst_fn(1), st, dict(dbg, after_seg=cat_after_seg(1)))
        finish_exchange()
    C.pfx = "D_"
    with ExitStack() as st:
        selt = st.enter_context(nc.sbuf_tensor("D_sel", [128, 2], F32))
        tmpA = st.enter_context(nc.sbuf_tensor("D_tmpA", [128, 6, 512], BF16))
        tmpB = st.enter_context(nc.sbuf_tensor("D_tmpB", [128, 6, 512], BF16))
        P.dma_("sp", selt[:], sel_d, writes=["sel"])
        d2 = dict(dbg)
        d2.update(cat_loader=make_cat_loader(1, selt, tmpA, tmpB), xT_dst=None)
        fi = f_in["d"]
        outs = build_ffn_phase(C, None, x1_d, fi["w_out"], fi["w_ff1"], fi["w_ff2"], *fi["lns"], ident, out_d, None, st, d2)
        with ExitStack() as st2:
            stats = P.emit(st2, final_wait_ops=outs)
    return nc, stats


def _perm_w_out(w):
    idx = []
    for r in range(2):
        idx += list(range(r * 512, r * 512 + 512))
        idx += list(range(1024 + r * 256, 1024 + r * 256 + 256))
    return np.ascontiguousarray(w[np.asarray(idx)])


def _fused_inputs(inp, c):
    b, hh = c // 2, c % 2
    g = _gdn_inputs(inp, b, hh)
    d = dict(ident=g["ident"], mem=g["mem"], mlng=g["mlng"], mlnb=g["mlnb"], wmk=g["wmk"], wmv=g["wmv"])
    sel = np.zeros((128, 2), np.float32)
    sel[:, hh] = 1.0
    d["sel"] = sel
    d["a_xT"] = g["xT"]
    for n in ("wq", "wk", "wv", "wz", "wab", "wmq"):
        d["a_" + n] = g[n]
    d.update(a_convw=g["convw"], a_alog=g["alog"], a_dtb=g["dtb"], a_gw=g["gw"], a_cm=g["cm"])
    w1 = inp["l1_w_in"]
    d.update(c_wq=_c(w1[:, hh * 512:hh * 512 + 512]), c_wk=_c(w1[:, 1024 + hh * 512:1024 + hh * 512 + 512]),
             c_wv=_c(w1[:, 2048 + hh * 512:2048 + hh * 512 + 512]), c_wmq=_c(w1[:, 3072 + hh * 256:3072 + hh * 256 + 256]))
    lam = np.stack([inp["l1_lambda_q1"], inp["l1_lambda_k1"], inp["l1_lambda_q2"], inp["l1_lambda_k2"]]).astype(np.float32)
    d.update(c_lamv=_c(np.broadcast_to(lam[None], (128, 4, 64))), c_subw=_c(np.asarray(inp["l1_subln_w"], np.float32)[:, None]),
             c_masks=_diff_masks())
    for L, layer in (("b", 0), ("d", 1)):
        p = f"l{layer}_"
        d[L + "_w_out"] = _perm_w_out(inp[p + "w_out"])
        d[L + "_w_ff1"] = _c(inp[p + "w_ff1"])
        d[L + "_w_ff2"] = _c(inp[p + "w_ff2"])
        for n, k in (("ln1g", "ln1_g"), ("ln1b", "ln1_b"), ("ln2g", "ln2_g"), ("ln2b", "ln2_b")):
            d[L + "_" + n] = _rep(inp[p + k])
    d["b_xres"] = _c(inp["x"][b, hh * 2048:(hh + 1) * 2048, :])
    return d


def kernel_unfused(**inputs):
    return _kernel_unfused(**inputs)


_kernel_unfused = kernel


def kernel(**inputs):
    inp = {k: np.asarray(v) for k, v in inputs.items()}
    cores = list(range(8))
    nc, _ = make_fused_program(8)
    res = run_bass_kernel_spmd(nc, [_fused_inputs(inp, c) for c in cores], core_ids=cores)
    out = np.zeros((NB, T, D), np.float32)
    for c in cores:
        out[c // 2, (c % 2) * 2048:(c % 2 + 1) * 2048, :] = np.asarray(res.results[c]["out"])
    return out
```

```python
from contextlib import ExitStack
import numpy as np
import ml_dtypes
import concourse.bass as bass
import concourse.mybir as mybir
from concourse.bass_utils import run_bass_kernel_spmd

F32 = mybir.dt.float32
BF16 = mybir.dt.bfloat16
F32R = mybir.dt.float32r
AF = mybir.ActivationFunctionType
ALU = mybir.AluOpType
NPBF = ml_dtypes.bfloat16

D = 1024
T = 4096
NB = 4
DFF = 4096
ALPHA = 4.0 ** 0.25
LN_EPS = 1e-5
RMS_EPS = 1e-6

SAME_ENGINE_SYNC = {"pe": False, "act": True, "dve": True, "pool": True, "sp": False}
N_DMA_SEMS = 64
N_HW_SEMS = 44


class Op:
    __slots__ = ("eng", "fn", "deps", "inc", "val", "sem", "is_dma")

    def __init__(self, eng, fn, deps, is_dma=False):
        self.eng = eng
        self.fn = fn
        self.deps = deps
        self.inc = False
        self.val = None
        self.sem = None
        self.is_dma = is_dma


class Prog:
    def __init__(self, nc):
        self.nc = nc
        self.engs = {"pe": nc.tensor, "act": nc.scalar, "dve": nc.vector,
                     "pool": nc.gpsimd, "sp": nc.sync}
        self.ops = []
        self.last_w = {}
        self.readers = {}
        self.bank_last = {}
        self.dma_rr = 0
        self.dma_rr_sw = 0
        self.cc_sems = []
        self.dma_last = [None] * N_DMA_SEMS
        self.uid = 0

    def _deps_for(self, eng, reads, writes, banks):
        deps = []
        for k in reads:
            w = self.last_w.get(k)
            if w is not None:
                deps.append(w)
        for k in writes:
            w = self.last_w.get(k)
            if w is not None:
                deps.append(w)
            deps.extend(self.readers.get(k, ()))
        for b in banks:
            for e, o in self.bank_last.get(b, {}).items():
                if e != eng:
                    deps.append(o)
        return deps

    def _record(self, op, reads, writes, banks):
        for k in reads:
            self.readers.setdefault(k, []).append(op)
        for k in writes:
            self.last_w[k] = op
            self.readers[k] = []
        for b in banks:
            self.bank_last.setdefault(b, {})[op.eng] = op
        self.ops.append(op)

    def op(self, eng, fn, reads=(), writes=(), banks=()):
        o = Op(eng, fn, self._deps_for(eng, reads, writes, banks))
        self._record(o, reads, writes, banks)
        return o

    def dma(self, eng, fn, reads=(), writes=()):
        deps = self._deps_for(eng, reads, writes, ())
        if eng == "pool":
            i = N_HW_SEMS + self.dma_rr_sw
            self.dma_rr_sw = (self.dma_rr_sw + 1) % (N_DMA_SEMS - N_HW_SEMS)
        else:
            i = self.dma_rr
            self.dma_rr = (i + 1) % N_HW_SEMS
        if self.dma_last[i] is not None:
            deps.append(self.dma_last[i])
        o = Op(eng, fn, deps, is_dma=True)
        o.sem = i
        o.inc = True
        self.dma_last[i] = o
        self._record(o, reads, writes, ())
        return o

    def barrier(self):
        last = {}
        for o in self.ops:
            if not o.is_dma:
                last[o.eng] = o
        dmas = [d for d in self.dma_last if d is not None]
        for e in self.engs:
            deps = [last[x] for x in last if x != e] + dmas
            self.ops.append(Op(e, lambda E: E.nop(), deps))

    def coll_wait(self):
        def f(E):
            if self.cc_sems:
                E.wait_ge(self.cc_sems[0], self.cc_count)
            return E.nop()
        return self.op("pool", f)

    def coll(self, kind, groups, in_ap, out_ap, reads=(), writes=(), wait=True):
        nc = self.nc
        def f(E):
            if not self.cc_sems:
                self.cc_sems.append(nc.alloc_semaphore(name="s_cc"))
                self.cc_count = 0
            sem = self.cc_sems[0]
            self.cc_count += 1
            E.collective_compute(kind, ALU.bypass, replica_groups=groups, ins=[in_ap.opt()], outs=[out_ap.opt()]).then_inc(sem)
            if wait:
                E.wait_ge(sem, self.cc_count)
            return E.nop()
        return self.op("pool", f, reads, writes)

    def dma_(self, eng, out, in_, reads=(), writes=()):
        return self.dma(eng, lambda E: E.dma_start(out=out, in_=in_), reads, writes)

    def mm(self, out, lhsT, rhs, start, stop, reads=(), banks=()):
        return self.op("pe", lambda E: E.matmul(out, lhsT=lhsT, rhs=rhs, start=start, stop=stop), reads, (), banks)

    def tr(self, out, in_, ident, reads=(), banks=()):
        return self.op("pe", lambda E: E.transpose(out, in_, ident), reads, (), banks)

    def act(self, out, in_, func, bias=0.0, scale=1.0, accum_out=None, reads=(), writes=(), banks=()):
        if accum_out is None:
            f = lambda E: E.activation(out=out, in_=in_, func=func, bias=bias, scale=scale)
        else:
            f = lambda E: E.activation(out=out, in_=in_, func=func, bias=bias, scale=scale, accum_out=accum_out)
        return self.op("act", f, reads, writes, banks)

    def copy(self, eng, out, in_, reads=(), writes=(), banks=()):
        if eng == "act":
            return self.op("act", lambda E: E.copy(out=out, in_=in_), reads, writes, banks)
        return self.op(eng, lambda E: E.tensor_copy(out=out, in_=in_), reads, writes, banks)

    def tt(self, eng, out, in0, in1, op, reads=(), writes=(), banks=()):
        return self.op(eng, lambda E: E.tensor_tensor(out=out, in0=in0, in1=in1, op=op), reads, writes, banks)

    def ts(self, eng, out, in0, s1, op0, s2=None, op1=None, reads=(), writes=(), banks=()):
        if op1 is None:
            f = lambda E: E.tensor_scalar(out=out, in0=in0, scalar1=s1, scalar2=None, op0=op0)
        else:
            f = lambda E: E.tensor_scalar(out=out, in0=in0, scalar1=s1, scalar2=s2, op0=op0, op1=op1)
        return self.op(eng, f, reads, writes, banks)

    def stt(self, eng, out, in0, scalar, in1, op0, op1, reads=(), writes=(), banks=()):
        return self.op(eng, lambda E: E.scalar_tensor_tensor(out=out, in0=in0, scalar=scalar, in1=in1, op0=op0, op1=op1),
                       reads, writes, banks)

    def emit(self, stack, final_wait_ops=()):
        nc = self.nc
        for o in self.ops:
            for d in o.deps:
                if d.is_dma:
                    continue
                if d.eng == o.eng and not SAME_ENGINE_SYNC[o.eng]:
                    continue
                d.inc = True
        for d in final_wait_ops:
            d.inc = True
        esem = {e: stack.enter_context(nc.semaphore(f"s_{e}")) for e in self.engs}
        dsem = [stack.enter_context(nc.semaphore(f"s_dma{i}")) for i in range(N_DMA_SEMS)]
        cnt = {e: 0 for e in self.engs}
        dcnt = [0] * N_DMA_SEMS
        for o in self.ops:
            if o.is_dma:
                dcnt[o.sem] += 16
                o.val = dcnt[o.sem]
                assert o.val <= 96, 'DMA semaphore value limit (device faults above ~100)'
            elif o.inc:
                cnt[o.eng] += 1
                o.val = cnt[o.eng]
        waited = {e: {} for e in self.engs}
        nwaits = 0
        for o in self.ops:
            E = self.engs[o.eng]
            need = {}
            for d in o.deps:
                if d.is_dma:
                    sk = ("d", d.sem)
                    sh = dsem[d.sem]
                else:
                    if d.eng == o.eng and not SAME_ENGINE_SYNC[o.eng]:
                        continue
                    sk = ("e", d.eng)
                    sh = esem[d.eng]
                if waited[o.eng].get(sk, 0) >= d.val:
                    continue
                if sk not in need or need[sk][1] < d.val:
                    need[sk] = (sh, d.val)
            for sk, (sh, v) in need.items():
                E.wait_ge(sh, v)
                waited[o.eng][sk] = v
                nwaits += 1
            ins = o.fn(E)
            if o.is_dma:
                ins.then_inc(dsem[o.sem], 16)
            elif o.inc:
                ins.then_inc(esem[o.eng], 1)
        for d in final_wait_ops:
            if d.is_dma:
                nc.sync.wait_ge(dsem[d.sem], d.val)
            else:
                nc.sync.wait_ge(esem[d.eng], d.val)
        self.stats = dict(n_ops=len(self.ops), n_waits=nwaits, cnt=dict(cnt))
        return self.stats


class Ctx:
    def __init__(self, nc):
        self.nc = nc
        self.P = Prog(nc)
        self.banks = [nc.alloc_psum_tensor(f"psb{i}", [128, 512], F32) for i in range(8)]
        self.rr = {}
        self.uid = 0
        self.pfx = ""

    def bank(self, pool, ids):
        i = self.rr.get(pool, 0)
        self.rr[pool] = i + 1
        return ids[i % len(ids)]

    def eng(self, pool, engs):
        i = self.rr.get(("e", pool), 0)
        self.rr[("e", pool)] = i + 1
        return engs[i % len(engs)]

    def key(self, name):
        self.uid += 1
        return (name, self.uid)


def layer_norm_tile(C, r, rkey, out, okey, g_rep, b_rep, gkey, scr, tag):
    P = C.P
    st, mv, sd, rstd, nmr, xn = scr["stats"], scr["mv"], scr["sd"], scr["rstd"], scr["nmr"], scr["xn"]
    k = lambda n: (tag, n)
    P.op("dve", lambda E: E.bn_stats(out=st[:, 0:6], in_=r[:, 0:512]), reads=[rkey], writes=[k("st0")])
    P.op("dve", lambda E: E.bn_stats(out=st[:, 6:12], in_=r[:, 512:1024]), reads=[rkey], writes=[k("st1")])
    P.op("dve", lambda E: E.bn_aggr(out=mv[:, 0:2], in_=st[:, 0:12]), reads=[k("st0"), k("st1")], writes=[k("mv")])
    P.op("act", lambda E: E.activation(out=sd[:], in_=mv[:, 1:2], func=AF.Sqrt, bias=LN_EPS, scale=1.0),
         reads=[k("mv")], writes=[k("sd")])
    P.op("dve", lambda E: E.reciprocal(out=rstd[:], in_=sd[:]), reads=[k("sd")], writes=[k("rstd")])
    P.op("dve", lambda E: E.scalar_tensor_tensor(out=nmr[:], in0=mv[:, 0:1], scalar=-1.0, in1=rstd[:],
                                                 op0=ALU.mult, op1=ALU.mult),
         reads=[k("mv"), k("rstd")], writes=[k("nmr")])
    P.op("act", lambda E: E.activation(out=xn[:], in_=r[:], func=AF.Identity, bias=nmr[:], scale=rstd[:]),
         reads=[rkey, k("rstd"), k("nmr")], writes=[k("xn")])
    P.op("pool", lambda E: E.tensor_tensor(out=xn[:], in0=xn[:], in1=g_rep[:], op=ALU.mult),
         reads=[k("xn"), gkey], writes=[k("xn")])
    P.op("pool", lambda E: E.tensor_tensor(out=out, in0=xn[:], in1=b_rep[:], op=ALU.add),
         reads=[k("xn"), gkey], writes=[okey])


def transpose_tile_bf16(C, src_bf, skey, dst3, dkey, ident_bf, nchunks, bank_ids, evac_engs=("dve", "act")):
    P = C.P
    b = C.bank("tr", bank_ids)
    pb = C.banks[b][:].bitcast(BF16)
    for kc in range(nchunks):
        P.op("pe", lambda E, kc=kc: E.transpose(pb[:, kc * 128:(kc + 1) * 128], src_bf[:, kc * 128:(kc + 1) * 128], ident_bf[:]),
             reads=[skey, "ident_bf"], banks=[b])
    e = C.eng("tr", evac_engs)
    src_v = pb[:, 0:nchunks * 128].rearrange("p (k t) -> p k t", k=nchunks)
    if e == "act":
        P.op("act", lambda E: E.copy(out=dst3, in_=src_v), writes=[dkey], banks=[b])
    else:
        P.op(e, lambda E: E.tensor_copy(out=dst3, in_=src_v), writes=[dkey], banks=[b])


def build_ffn_phase(C, catT, xres, w_out, w_ff1, w_ff2, ln1g, ln1b, ln2g, ln2b, ident,
                    x_out, xT_out, stack, dbg=None):
    nc, P = C.nc, C.P
    sb = lambda name, shape, dt: stack.enter_context(nc.sbuf_tensor(C.pfx + name, shape, dt))
    dbg = dbg or {}
    NPASS, TP = dbg.get('npass', 2), 1024
    NT = TP // 128
    wo = sb("wo", [128, 12, 1024], BF16)
    lnp = sb("lnp", [128, 4, 1024], F32)
    idb = sb("idb", [128, 128], BF16)
    idf = sb("idf", [128, 128], F32)
    cat_sb = [sb(f"cat{i}", [128, 12, 512], BF16) for i in range(1 if (dbg or {}).get("cat_loader") else 2)]
    xmT = sb("xmT", [128, 8, TP], BF16)
    yacc = sb("yacc", [128, NT, 1024], F32)
    hT = [sb(f"hT{i}", [128, 4, TP], BF16) for i in range(2)]
    w1g = [sb(f"w1g{i}", [128, 8, 512], BF16) for i in range(2)]
    w2g = [sb(f"w2g{i}", [128, 4, 1024], BF16) for i in range(2)]
    rt = [sb(f"rt{i}", [128, 1024], F32) for i in range(3)]
    xr = [sb(f"xr{i}", [128, 1024], F32) for i in range(3)]
    xb = [sb(f"xb{i}", [128, 1024], BF16) for i in range(3)]
    relu_t = [sb(f"relu{i}", [128, 512], F32) for i in range(2)]
    scr = [dict(stats=sb(f"lst{i}", [128, 12], F32), mv=sb(f"lmv{i}", [128, 2], F32), sd=sb(f"lsd{i}", [128, 1], F32),
                rstd=sb(f"lrs{i}", [128, 1], F32), nmr=sb(f"lnm{i}", [128, 1], F32))
           for i in range(3)]
    outs = []

    P.dma("sp", lambda E: E.dma_start(out=idf[:], in_=ident), writes=["ident_f"])
    P.op("dve", lambda E: E.tensor_copy(out=idb[:], in_=idf[:]), reads=["ident_f"], writes=["ident_bf"])
    for i, src in enumerate((ln1g, ln1b, ln2g, ln2b)):
        P.dma("act", lambda E, i=i, src=src: E.dma_start(out=lnp[:, i, :], in_=src), writes=[("lnp", i)])
    wov = w_out.rearrange("(kc p) n -> p kc n", p=128)
    for j in range(3):
        P.dma("pool", lambda E, j=j: E.dma_start(out=wo[:, 4 * j:4 * j + 4, :], in_=wov[:, 4 * j:4 * j + 4, :]),
              writes=[("wo", j)])
    cat_loader = dbg.get("cat_loader")
    xT_dst = dbg.get("xT_dst")
    catv = catT.rearrange("(kc p) t -> p kc t", p=128) if cat_loader is None else None
    w1v = w_ff1.rearrange("(kc p) n -> p kc n", p=128)
    w2v = w_ff2.rearrange("(fc p) n -> p fc n", p=128)

    ncat = 0
    nrt = 0
    gcount = 0
    for ps in range(NPASS):
        t0 = ps * TP
        def s1_gen(tl):
            nonlocal ncat, nrt
            tg = tl // 4
            if tl % 4 == 0:
                cb_ = ncat % len(cat_sb)
                ncat += 1
                s1_gen.cb = cb_
                if cat_loader is None:
                    P.dma_("sp", cat_sb[cb_][:], catv[:, :, t0 + tg * 512:t0 + (tg + 1) * 512], writes=[("cat", cb_)])
                else:
                    cat_loader(cat_sb[cb_], ("cat", cb_), ps * 2 + tg)
            cb = s1_gen.cb
            ri = nrt % 3
            nrt += 1
            tok = t0 + tl * 128
            P.dma_("act", xr[ri][:], xres[tok:tok + 128, :], writes=[("xr", ri)])
            for half in range(2):
                b = C.bank("mm", [0, 1, 2, 3])
                for kc in range(12):
                    P.mm(C.banks[b][:], cat_sb[cb][:, kc, (tl % 4) * 128:(tl % 4 + 1) * 128], wo[:, kc, half * 512:(half + 1) * 512],
                         kc == 0, kc == 11, reads=[("cat", cb), ("wo", kc // 4)], banks=[b])
                P.stt("dve", rt[ri][:, half * 512:(half + 1) * 512], xr[ri][:, half * 512:(half + 1) * 512], ALPHA, C.banks[b][:],
                      ALU.mult, ALU.add, reads=[("xr", ri)], writes=[("rt", ri, half)], banks=[b])
            yield
            yield from ln_gen(C, rt[ri][:], [("rt", ri, 0), ("rt", ri, 1)], xr[ri][:], ("xr", ri), lnp[:, 0, :], lnp[:, 1, :],
                              [("lnp", 0), ("lnp", 1)], scr[ri], ("ln", ri))
            P.act(yacc[:, tl, :], xr[ri][:], AF.Copy, scale=ALPHA, reads=[("xr", ri)], writes=[("yacc", tl, 0), ("yacc", tl, 1)])
            P.copy("dve", xb[ri][:], xr[ri][:], reads=[("xr", ri)], writes=[("xb", ri)])
            yield
            transpose_tile_bf16(C, xb[ri], ("xb", ri), xmT[:, :, tl * 128:(tl + 1) * 128], ("xmT", tl), idb, 8, [4, 5])

        run_window([s1_gen(tl) for tl in range(NT)], 3)

        NG = dbg.get('ngroups', 8)

        def ld_w1(g):
            P.dma_("pool", w1g[(gbase + g) % 2][:], w1v[:, :, g * 512:(g + 1) * 512], writes=[("w1g", (gbase + g) % 2)])

        def ld_w2(g):
            P.dma_("pool", w2g[(gbase + g) % 2][:], w2v[:, g * 4:(g + 1) * 4, :], writes=[("w2g", (gbase + g) % 2)])

        def ff1(g):
            w3, h2 = (gbase + g) % 2, (gbase + g) % 2
            for fc in range(4):
                for tg in range(2):
                    b = C.bank("mm", [0, 1, 2, 3])
                    for kc in range(8):
                        P.mm(C.banks[b][:], w1g[w3][:, kc, fc * 128:(fc + 1) * 128], xmT[:, kc, tg * 512:(tg + 1) * 512], kc == 0, kc == 7,
                             reads=[("w1g", w3)] + [("xmT", tg * 4 + j) for j in range(4)], banks=[b])
                    rb = C.bank("relu", [0, 1])
                    P.act(relu_t[rb][:], C.banks[b][:], AF.Relu, writes=[("relu", rb)], banks=[b])
                    P.tt("pool", hT[h2][:, fc, tg * 512:(tg + 1) * 512], relu_t[rb][:], relu_t[rb][:], ALU.mult,
                         reads=[("relu", rb)], writes=[("hT", h2, fc, tg)])

        def ff2(g):
            w3, h2 = (gbase + g) % 2, (gbase + g) % 2
            for tl in range(NT):
                for half in range(2):
                    b = C.bank("mm2", [4, 5, 6, 7])
                    for fc in range(4):
                        P.mm(C.banks[b][:], hT[h2][:, fc, tl * 128:(tl + 1) * 128], w2g[w3][:, fc, half * 512:(half + 1) * 512], fc == 0, fc == 3,
                             reads=[("w2g", w3), ("hT", h2, fc, tl // 4)], banks=[b])
                    P.tt("dve", yacc[:, tl, half * 512:(half + 1) * 512], yacc[:, tl, half * 512:(half + 1) * 512], C.banks[b][:], ALU.add,
                         reads=[("yacc", tl, half)], writes=[("yacc", tl, half)], banks=[b])

        gbase = gcount
        gcount += NG
        if NG > 0:
            ld_w1(0)
            ld_w2(0)
            if NG > 1:
                ld_w1(1)
                ld_w2(1)
            ff1(0)
            for g in range(NG):
                if g + 2 < NG:
                    ld_w1(g + 2)
                if g + 1 < NG:
                    ff1(g + 1)
                ff2(g)
                if g + 2 < NG:
                    ld_w2(g + 2)
        def s3_gen(tl):
            nonlocal nrt
            ri = nrt % 3
            nrt += 1
            tok = t0 + tl * 128
            yield from ln_gen(C, yacc[:, tl, :], [("yacc", tl, 0), ("yacc", tl, 1)], xr[ri][:], ("xr", ri), lnp[:, 2, :], lnp[:, 3, :],
                              [("lnp", 2), ("lnp", 3)], scr[ri], ("ln", ri))
            outs.append(P.dma_("sp", x_out[tok:tok + 128, :], xr[ri][:], reads=[("xr", ri)]))
            if xT_out is not None or xT_dst is not None:
                P.copy("dve", xb[ri][:], xr[ri][:], reads=[("xr", ri)], writes=[("xb", ri)])
                yield
                transpose_tile_bf16(C, xb[ri], ("xb", ri), xmT[:, :, tl * 128:(tl + 1) * 128], ("xmT", tl), idb, 8, [4, 5])
                if xT_dst is None:
                    dstT = xT_out.rearrange("(kc p) t -> p kc t", p=128)[:, :, tok:tok + 128]
                else:
                    dstT = xT_dst(tok)
                outs.append(P.dma_("act", dstT, xmT[:, :, tl * 128:(tl + 1) * 128], reads=[("xmT", tl)], writes=[("xsend", tok)]))

        run_window([s3_gen(tl) for tl in range(NT)], 3)
        if dbg.get('after_pass'):
            dbg['after_pass'](ps)
    return outs


def ln_gen(C, r, rkeys, out, okey, g_rep, b_rep, gkeys, scr, tag):
    P = C.P
    st, mv, sd, rstd, nmr = scr["stats"], scr["mv"], scr["sd"], scr["rstd"], scr["nmr"]
    k = lambda n: (tag, n)
    P.op("dve", lambda E: E.bn_stats(out=st[:, 0:6], in_=r[:, 0:512]), reads=rkeys, writes=[k("st0")])
    P.op("dve", lambda E: E.bn_stats(out=st[:, 6:12], in_=r[:, 512:1024]), reads=rkeys, writes=[k("st1")])
    yield
    P.op("dve", lambda E: E.bn_aggr(out=mv[:, 0:2], in_=st[:, 0:12]), reads=[k("st0"), k("st1")], writes=[k("mv")])
    yield
    P.op("act", lambda E: E.activation(out=sd[:], in_=mv[:, 1:2], func=AF.Sqrt, bias=LN_EPS, scale=1.0),
         reads=[k("mv")], writes=[k("sd")])
    yield
    P.op("dve", lambda E: E.reciprocal(out=rstd[:], in_=sd[:]), reads=[k("sd")], writes=[k("rstd")])
    yield
    P.op("dve", lambda E: E.scalar_tensor_tensor(out=nmr[:], in0=mv[:, 0:1], scalar=-1.0, in1=rstd[:],
                                                 op0=ALU.mult, op1=ALU.mult),
         reads=[k("mv"), k("rstd")], writes=[k("nmr")])
    yield
    P.op("act", lambda E: E.activation(out=out, in_=r, func=AF.Identity, bias=nmr[:], scale=rstd[:]),
         reads=list(rkeys) + [k("rstd"), k("nmr")], writes=[okey])
    yield
    P.op("dve", lambda E: E.tensor_tensor(out=out, in0=out, in1=g_rep, op=ALU.mult), reads=[okey, gkeys[0]], writes=[okey])
    yield
    P.op("dve", lambda E: E.tensor_tensor(out=out, in0=out, in1=b_rep, op=ALU.add), reads=[okey, gkeys[1]], writes=[okey])
    yield


def _ln_two_keys(C, r, rkeys, out, okey, g_rep, b_rep, gkeys, scr, tag):
    for _ in ln_gen(C, r, rkeys, out, okey, g_rep, b_rep, gkeys, scr, tag):
        pass


def run_window(gen_list, width):
    pending = list(gen_list)
    active = []
    while pending or active:
        while pending and len(active) < width:
            active.append(pending.pop(0))
        for g_ in list(active):
            try:
                next(g_)
            except StopIteration:
                active.remove(g_)


def make_ffn_program(dbg=None):
    nc = bass.Bass("TRN2", target_bir_lowering=False)
    dt = lambda n, s, d, k: nc.dram_tensor(n, s, d, kind=k).ap()
    catT = dt("catT", [1536, 2048], BF16, "ExternalInput")
    xres = dt("xres", [2048, 1024], F32, "ExternalInput")
    w_out = dt("w_out", [1536, 1024], F32, "ExternalInput")
    w_ff1 = dt("w_ff1", [1024, 4096], F32, "ExternalInput")
    w_ff2 = dt("w_ff2", [4096, 1024], F32, "ExternalInput")
    lns = [dt(n, [128, 1024], F32, "ExternalInput") for n in ("ln1g", "ln1b", "ln2g", "ln2b")]
    ident = dt("ident", [128, 128], F32, "ExternalInput")
    x_out = dt("x_out", [2048, 1024], F32, "ExternalOutput")
    xT_out = dt("xT_out", [1024, 2048], BF16, "ExternalOutput")
    C = Ctx(nc)
    with ExitStack() as st:
        outs = build_ffn_phase(C, catT, xres, w_out, w_ff1, w_ff2, *lns, ident, x_out, (None if (dbg or {}).get('noxt') else xT_out), st, dbg)
        with ExitStack() as st2:
            stats = C.P.emit(st2, final_wait_ops=outs)
    return nc, stats


def load_xT_group(C, xT_dram, xT_sb, is_f32, tg):
    v = xT_dram.rearrange("(kc p) t -> p kc t", p=128)
    sl = tg % 4
    C.P.dma_("pool" if is_f32 else ("sp" if tg % 2 == 0 else "act"),
             xT_sb[:, :, sl * 512:(sl + 1) * 512], v[:, :, tg * 512:(tg + 1) * 512], writes=[("xT", sl)])


def proj_fm(C, w_sb, wkey, col0, xT_sb, tg, banks_ids):
    P = C.P
    b = C.bank("pf", banks_ids)
    for kc in range(8):
        P.mm(C.banks[b][:], w_sb[:, kc, col0:col0 + 128], xT_sb[:, kc, (tg % 4) * 512:(tg % 4 + 1) * 512], kc == 0, kc == 7,
             reads=[wkey, ("xT", tg % 4)], banks=[b])
    return b


def build_mem_kv(C, mem, lng, lnb, wmk_d, wmv_d, idb, sbt, stack):
    nc, P = C.nc, C.P
    KmT = sbt("KmT", [128, 2, 256], BF16)
    Vm = sbt("Vm", [128, 2, 256], BF16)
    with ExitStack() as st:
        sb = lambda name, shape, dt: st.enter_context(nc.sbuf_tensor(C.pfx + name, shape, dt))
        mt_ = [sb(f"memt{i}", [128, 1024], F32) for i in range(2)]
        mo = [sb(f"memo{i}", [128, 1024], F32) for i in range(2)]
        mb = [sb(f"memb{i}", [128, 1024], BF16) for i in range(2)]
        lnp = sb("mlnp", [128, 2, 1024], F32)
        memT = sb("memT", [128, 8, 256], BF16)
        wmk = sb("wmk_sb", [128, 8, 256], BF16)
        wmv = sb("wmv_sb", [128, 8, 256], BF16)
        scr = [dict(stats=sb(f"mst{i}", [128, 12], F32), mv=sb(f"mmv{i}", [128, 2], F32), sd=sb(f"msd{i}", [128, 1], F32),
                    rstd=sb(f"mrs{i}", [128, 1], F32), nmr=sb(f"mnm{i}", [128, 1], F32), xn=sb(f"mxn{i}", [128, 1024], F32))
               for i in range(2)]
        P.dma_("sp", lnp[:, 0, :], lng, writes=[("mlnp", 0)])
        P.dma_("sp", lnp[:, 1, :], lnb, writes=[("mlnp", 1)])
        P.dma_("pool", wmk[:], wmk_d.rearrange("(kc p) n -> p kc n", p=128), writes=["wmk"])
        P.dma_("pool", wmv[:], wmv_d.rearrange("(kc p) n -> p kc n", p=128), writes=["wmv"])
        for i in range(2):
            P.dma_("act", mt_[i][:], mem[i * 128:(i + 1) * 128, :], writes=[("memt", i)])
            _ln_two_keys(C, mt_[i][:], [("memt", i)], mo[i][:], ("memo", i), lnp[:, 0, :], lnp[:, 1, :],
                         [("mlnp", 0), ("mlnp", 1)], scr[i], ("mln", i))
            P.copy("dve", mb[i][:], mo[i][:], reads=[("memo", i)], writes=[("memb", i)])
            transpose_tile_bf16(C, mb[i], ("memb", i), memT[:, :, i * 128:(i + 1) * 128], ("memT", i), idb, 8, [6, 7])
        for h in range(2):
            b = C.bank("mkv", [4, 5])
            for kc in range(8):
                P.mm(C.banks[b][:, 0:256], wmk[:, kc, h * 128:(h + 1) * 128], memT[:, kc, :], kc == 0, kc == 7,
                     reads=["wmk", ("memT", 0), ("memT", 1)], banks=[b])
            P.copy("dve", KmT[:, h, :], C.banks[b][:, 0:256], writes=[("KmT", h)], banks=[b])
        for mt in range(2):
            b = C.bank("mkv", [4, 5])
            for kc in range(8):
                P.mm(C.banks[b][:, 0:256], memT[:, kc, mt * 128:(mt + 1) * 128], wmv[:, kc, :], kc == 0, kc == 7,
                     reads=["wmv", ("memT", mt)], banks=[b])
            P.copy("act", Vm[:, mt, :], C.banks[b][:, 0:256], writes=[("Vm", mt)], banks=[b])
        P.barrier()
    return KmT, Vm


def _cat_dst(catT_out, k, tg):
    if callable(catT_out):
        return catT_out(k, tg)
    return catT_out[k * 128:(k + 1) * 128, tg * 512:(tg + 1) * 512]


def mem_attention_group(C, KmT, Vm, ones_bf, mqT, mqkey, out_tile, okey, pT_bufs, rden, tagi):
    P = C.P
    h = tagi
    pk = []
    for mt in range(2):
        b = C.bank("ms", [4, 5])
        P.mm(C.banks[b][:], KmT[:, h, mt * 128:(mt + 1) * 128], mqT, True, True, reads=[("KmT", h), mqkey], banks=[b])
        pi = C.bank("mpT", list(range(len(pT_bufs))))
        P.act(pT_bufs[pi][:], C.banks[b][:], AF.Exp, writes=[("mpT", pi)], banks=[b])
        pk.append(pi)
    bo = C.bank("mo", [6])
    bd = C.bank("md", [7])
    for mt in range(2):
        P.mm(C.banks[bo][:], Vm[:, mt, h * 128:(h + 1) * 128], pT_bufs[pk[mt]][:], mt == 0, mt == 1,
             reads=[("Vm", mt), ("mpT", pk[mt])], banks=[bo])
    for mt in range(2):
        P.mm(C.banks[bd][:], ones_bf[:], pT_bufs[pk[mt]][:], mt == 0, mt == 1,
             reads=["ones_bf", ("mpT", pk[mt])], banks=[bd])
    P.act(rden[:], C.banks[bd][:], AF.Ln, writes=["mrden0"], banks=[bd])
    P.act(rden[:], rden[:], AF.Exp, scale=-1.0, reads=["mrden0"], writes=["mrden"])
    P.tt("dve", out_tile, C.banks[bo][:], rden[:], ALU.mult, reads=["mrden"], writes=[okey], banks=[bo])


LAM_INIT1 = 0.8 - 0.6 * float(np.exp(-0.3))


def build_diff_phase(C, xT_d, x_is_f32, mem, mlng, mlnb, wmk_d, wmv_d, wq_d, wk_d, wv_d, wmq_d, lamv, subw, masks_d,
                     ident, catT_out, stack, dbg=None):
    nc, P = C.nc, C.P
    dbg = dbg or {}
    sbt = lambda name, shape, dt: stack.enter_context(nc.sbuf_tensor(C.pfx + name, shape, dt))
    outs = []
    idf = sbt("idf", [128, 128], F32)
    idb = sbt("idb", [128, 128], BF16)
    ones_bf = sbt("ones_bf", [128, 128], BF16)
    ones_f = sbt("ones_f", [128, 128], F32)
    ones_r = sbt("ones_r", [128, 128], F32R)
    P.dma_("sp", idf[:], ident, writes=["ident_f"])
    P.copy("dve", idb[:], idf[:], reads=["ident_f"], writes=["ident_bf"])
    P.op("dve", lambda E: E.memset(ones_bf[:], 1.0), writes=["ones_bf"])
    P.op("dve", lambda E: E.memset(ones_f[:], 1.0), writes=["ones_f"])
    P.copy("dve", ones_r[:], ones_f[:], reads=["ones_f"], writes=["ones_r"])
    KmT, Vm = build_mem_kv(C, mem, mlng, mlnb, wmk_d, wmv_d, idb, sbt, stack)

    qT = sbt("qT", [128, 4, T], BF16)
    kT = sbt("kT", [128, 4, T], BF16)
    vtok = sbt("vtok", [128, 32, 512], BF16)
    mqT = sbt("mqT", [128, 2, T], BF16)
    masks = sbt("masks_sb", [128, 4, 512], BF16)
    P.dma_("pool", masks[:], masks_d.rearrange("r p q -> p r q"), writes=["masks"])
    lv = sbt("lv", [128, 4, 64], F32)
    lprod = sbt("lprod", [128, 2, 64], F32)
    lsum = sbt("lsum", [128, 2], F32)
    lexp = sbt("lexp", [128, 2], F32)
    nlam = sbt("nlam", [128, 1], F32)
    swc = sbt("swc", [128, 2], F32)
    P.dma_("sp", lv[:], lamv, writes=["lv"])
    P.dma_("sp", swc[:, 0:1], subw, writes=["swc0"])
    P.tt("dve", lprod[:, 0, :], lv[:, 0, :], lv[:, 1, :], ALU.mult, reads=["lv"], writes=["lprod0"])
    P.tt("dve", lprod[:, 1, :], lv[:, 2, :], lv[:, 3, :], ALU.mult, reads=["lv"], writes=["lprod1"])
    P.op("dve", lambda E: E.reduce_sum(out=lsum[:, 0:1], in_=lprod[:, 0, :], axis=mybir.AxisListType.X), reads=["lprod0"], writes=["lsum0"])
    P.op("dve", lambda E: E.reduce_sum(out=lsum[:, 1:2], in_=lprod[:, 1, :], axis=mybir.AxisListType.X), reads=["lprod1"], writes=["lsum1"])
    P.act(lexp[:], lsum[:], AF.Exp, reads=["lsum0", "lsum1"], writes=["lexp"])
    P.tt("dve", nlam[:], lexp[:, 1:2], lexp[:, 0:1], ALU.subtract, reads=["lexp"], writes=["nlam0"])
    P.ts("dve", nlam[:], nlam[:], -LAM_INIT1, ALU.add, reads=["nlam0"], writes=["nlam"])
    P.ts("dve", swc[:, 1:2], swc[:, 0:1], 1.0 - LAM_INIT1, ALU.mult, reads=["swc0"], writes=["swc"])

    with ExitStack() as st:
        sb = lambda name, shape, dt: st.enter_context(nc.sbuf_tensor(C.pfx + name, shape, dt))
        xT_sb = sb("xT_sb", [128, 8, 2048], BF16)
        wq = sb("wq_sb", [128, 8, 512], BF16)
        wk = sb("wk_sb", [128, 8, 512], BF16)
        wv = sb("wv_sb", [128, 8, 512], BF16)
        wmq = sb("wmq_sb", [128, 8, 256], BF16)
        for wsb, wd, key in ((wq, wq_d, "wq"), (wk, wk_d, "wk"), (wv, wv_d, "wv"), (wmq, wmq_d, "wmq")):
            P.dma_("pool", wsb[:], wd.rearrange("(kc p) n -> p kc n", p=128), writes=[key])
        for tg in range(4):
            load_xT_group(C, xT_d, xT_sb, x_is_f32, tg)
        for tg in range(8):
            if tg >= 4:
                load_xT_group(C, xT_d, xT_sb, x_is_f32, tg)
            for h in range(4):
                b = proj_fm(C, wq, "wq", h * 128, xT_sb, tg, [0, 1, 2, 3])
                P.act(qT[:, h, tg * 512:(tg + 1) * 512], C.banks[b][:], AF.Copy, scale=0.125, writes=[("qT", h, tg)], banks=[b])
                b = proj_fm(C, wk, "wk", h * 128, xT_sb, tg, [0, 1, 2, 3])
                P.copy("dve", kT[:, h, tg * 512:(tg + 1) * 512], C.banks[b][:], writes=[("kT", h, tg)], banks=[b])
            for h in range(2):
                b = proj_fm(C, wmq, "wmq", h * 128, xT_sb, tg, [0, 1, 2, 3])
                P.act(mqT[:, h, tg * 512:(tg + 1) * 512], C.banks[b][:], AF.Copy, scale=128.0 ** -0.5, writes=[("mqT", h, tg)], banks=[b])
            for tl in range(4):
                t = tg * 4 + tl
                b = C.bank("pf", [0, 1, 2, 3])
                for kc in range(8):
                    P.mm(C.banks[b][:], xT_sb[:, kc, (t % 16) * 128:(t % 16 + 1) * 128], wv[:, kc, :], kc == 0, kc == 7,
                         reads=["wv", ("xT", tg % 4)], banks=[b])
                P.copy("dve" if tl % 2 else "act", vtok[:, t, :], C.banks[b][:], writes=[("vtok", t)], banks=[b])
        P.op("dve", lambda E: E.memset(idf[:, 0:1], 1.0) if False else E.memset(ones_f[:, 0:1], 1.0),
             reads=[], writes=[("xT", g) for g in range(4)] + ["wq", "wk", "wv", "wmq", "ones_f"])
        C.free_key = [("xT", g) for g in range(4)] + ["wq", "wk", "wv", "wmq"]
        P.barrier()

    pT = [sbt(f"pT{i}", [128, 512], BF16) for i in range(6)]
    mpT = [sbt(f"mpT{i}", [128, 512], BF16) for i in range(4)]
    rden = sbt("rden", [128, 512], F32)
    r12 = [sbt(f"r12_{i}", [128, 512], F32) for i in range(2)]
    o12 = [sbt(f"o12_{i}", [128, 512], F32) for i in range(2)]
    od = sbt("od", [128, 512], F32)
    sq = sbt("sq", [128, 512], F32R)
    rs = sbt("rs", [128, 512], F32)
    cat_t = [sbt(f"cat_t{i}", [128, 512], BF16) for i in range(3)]
    fk = C.free_key
    catv = catT_out
    nq = dbg.get("nqg", 8)
    for qg in range(nq):
        qs = slice(qg * 512, (qg + 1) * 512)
        for h in range(4):
            nkt = 4 * (qg + 1)

            def score(kt):
                pis = []
                for half in range(2):
                    b = C.bank("sT", [4, 5, 6, 7])
                    hp = slice(half * 64, (half + 1) * 64)
                    P.mm(C.banks[b][:], kT[hp, h, kt * 128:(kt + 1) * 128], qT[hp, h, qs], True, True,
                         reads=[("kT", h, kt // 4), ("qT", h, qg)], banks=[b])
                    pi = C.bank("pT", list(range(6)))
                    P.act(pT[pi][:], C.banks[b][:], AF.Exp, reads=fk, writes=[("pT", pi)], banks=[b])
                    if kt >= 4 * qg:
                        r = kt - 4 * qg
                        P.tt("pool", pT[pi][:], pT[pi][:], masks[:, r, :], ALU.mult, reads=["masks", ("pT", pi)], writes=[("pT", pi)])
                    pis.append(pi)
                return pis

            nxt = score(0)
            for kt in range(nkt):
                pis = nxt
                if kt + 1 < nkt:
                    nxt = score(kt + 1)
                for half in range(2):
                    P.mm(C.banks[half][:], vtok[:, kt, h * 128:(h + 1) * 128], pT[pis[half]][:], kt == 0, kt == nkt - 1,
                         reads=[("vtok", kt), ("pT", pis[half])], banks=[half])
                    P.mm(C.banks[2 + half][:], ones_bf[:], pT[pis[half]][:], kt == 0, kt == nkt - 1,
                         reads=["ones_bf", ("pT", pis[half])], banks=[2 + half])
            for half in range(2):
                P.act(r12[half][:], C.banks[2 + half][:], AF.Ln, reads=fk, writes=[("r12", half)], banks=[2 + half])
                P.act(r12[half][:], r12[half][:], AF.Exp, scale=-1.0, reads=[("r12", half)], writes=[("r12", half)])
                P.tt("dve", o12[half][:], C.banks[half][:], r12[half][:], ALU.mult, reads=[("r12", half)] + fk,
                     writes=[("o12", half)], banks=[half])
            P.stt("dve", od[:], o12[1][:], nlam[:], o12[0][:], ALU.mult, ALU.add, reads=[("o12", 0), ("o12", 1), "nlam"] + fk, writes=["od"])
            P.act(sq[:], od[:], AF.Square, reads=["od"] + fk, writes=["sq"])
            b = C.bank("sT", [4, 5, 6, 7])
            P.mm(C.banks[b][:], ones_r[:], sq[:], True, True, reads=["ones_r", "sq"], banks=[b])
            P.act(rs[:], C.banks[b][:], AF.Ln, bias=RMS_EPS, scale=1.0 / 128.0, reads=fk, writes=["rs0"], banks=[b])
            P.act(rs[:], rs[:], AF.Exp, scale=-0.5, reads=["rs0"], writes=["rs"])
            ci = C.bank("cat_t", [0, 1, 2])
            P.stt("dve", cat_t[ci][:], od[:], swc[:, 1:2], rs[:], ALU.mult, ALU.mult, reads=["od", "swc", "rs"] + fk, writes=[("cat_t", ci)])
            outs.append(P.dma_("sp", _cat_dst(catv, h, qg), cat_t[ci][:], reads=[("cat_t", ci)], writes=[("catdst", h, qg)]))
        for h in range(2):
            ci = C.bank("cat_t", [0, 1, 2])
            mem_attention_group(C, KmT, Vm, ones_bf, mqT[:, h, qs], ("mqT", h, qg), cat_t[ci][:], ("cat_t", ci), mpT, rden, h)
            outs.append(P.dma_("act", _cat_dst(catv, 4 + h, qg), cat_t[ci][:], reads=[("cat_t", ci)], writes=[("catdst", 4 + h, qg)]))
        if dbg.get("after_seg"):
            dbg["after_seg"](qg)
    return outs


def make_diff_program(x_is_f32=False, dbg=None):
    nc = bass.Bass("TRN2", target_bir_lowering=False)
    dt = lambda n, s, d, k="ExternalInput": nc.dram_tensor(n, s, d, kind=k).ap()
    xT_d = dt("xT", [1024, T], F32 if x_is_f32 else BF16)
    mem = dt("mem", [256, 1024], F32)
    mlng = dt("mlng", [128, 1024], F32)
    mlnb = dt("mlnb", [128, 1024], F32)
    wmk = dt("wmk", [1024, 256], F32)
    wmv = dt("wmv", [1024, 256], F32)
    wq = dt("wq", [1024, 512], F32)
    wk = dt("wk", [1024, 512], F32)
    wv = dt("wv", [1024, 512], F32)
    wmq = dt("wmq", [1024, 256], F32)
    lamv = dt("lamv", [128, 4, 64], F32)
    subw = dt("subw", [128, 1], F32)
    masks = dt("masks", [4, 128, 512], F32)
    ident = dt("ident", [128, 128], F32)
    catT_out = dt("catT_out", [768, T], BF16, "ExternalOutput")
    C = Ctx(nc)
    with ExitStack() as st:
        outs = build_diff_phase(C, xT_d, x_is_f32, mem, mlng, mlnb, wmk, wmv, wq, wk, wv, wmq, lamv, subw, masks, ident,
                                catT_out, st, dbg)
        with ExitStack() as st2:
            stats = C.P.emit(st2, final_wait_ops=outs)
    return nc, stats


def build_gdn_phase(C, xT_d, x_is_f32, mem, mlng, mlnb, wmk_d, wmv_d, wq_d, wk_d, wv_d, wz_d, wab_d, wmq_d,
                    convw_d, alog_d, dtb_d, gw_d, cm_d, ident, catT_out, stack, dbg=None):
    nc, P = C.nc, C.P
    dbg = dbg or {}
    sbt = lambda name, shape, dt: stack.enter_context(nc.sbuf_tensor(C.pfx + name, shape, dt))
    outs = []
    idf = sbt("idf", [128, 128], F32)
    idb = sbt("idb", [128, 128], BF16)
    ones_bf = sbt("ones_bf", [128, 128], BF16)
    ones_f = sbt("ones_f", [128, 128], F32)
    ones_r = sbt("ones_r", [128, 128], F32R)
    P.dma_("sp", idf[:], ident, writes=["ident_f"])
    P.copy("dve", idb[:], idf[:], reads=["ident_f"], writes=["ident_bf"])
    P.op("dve", lambda E: E.memset(ones_bf[:], 1.0), writes=["ones_bf"])
    P.op("dve", lambda E: E.memset(ones_f[:], 1.0), writes=["ones_f"])
    P.copy("dve", ones_r[:], ones_f[:], reads=["ones_f"], writes=["ones_r"])
    KmT, Vm = build_mem_kv(C, mem, mlng, mlnb, wmk_d, wmv_d, idb, sbt, stack)

    cm = sbt("cm_sb", [128, 6, 128], F32)
    P.dma_("sp", cm[:], cm_d.rearrange("r p q -> p r q"), writes=["cm"])
    TRI, SUT, MSL, MUI, ONA, ONB = [cm[:, i, :] for i in range(6)]
    sutr = sbt("sutr", [128, 128], F32R)
    P.copy("dve", sutr[:], SUT, reads=["cm"], writes=["sutr"])
    idr = sbt("idr", [128, 128], F32R)
    P.copy("dve", idr[:], idf[:], reads=["ident_f"], writes=["ident_r"])
    convw = sbt("convw_sb", [128, 12, 4], F32)
    P.dma_("sp", convw[:], convw_d, writes=["convw"])
    alog = sbt("alog_sb", [128, 4], F32)
    dtb = sbt("dtb_sb", [128, 4], F32)
    negA = sbt("negA", [128, 4], F32)
    gw = sbt("gw_sb", [128, 128], F32)
    P.dma_("act", alog[:], alog_d, writes=["alog"])
    P.dma_("act", dtb[:], dtb_d, writes=["dtb"])
    P.dma_("act", gw[:], gw_d, writes=["gw"])
    P.act(negA[:], alog[:], AF.Exp, reads=["alog"], writes=["negA0"])
    P.ts("dve", negA[:], negA[:], -1.0, ALU.mult, reads=["negA0"], writes=["negA"])

    xT_sb = sbt("xT_sb", [128, 8, 2048], BF16)
    wsb = {}
    for n, wd, cols in (("wq", wq_d, 512), ("wk", wk_d, 512), ("wv", wv_d, 512), ("wz", wz_d, 512), ("wab", wab_d, 8), ("wmq", wmq_d, 256)):
        wsb[n] = sbt(n + "_sb", [128, 8, cols], BF16)
        P.dma_("pool", wsb[n][:], wd.rearrange("(kc p) n -> p kc n", p=128), writes=[n])
    pre = [sbt(f"pre{c}", [128, 515], F32) for c in range(12)]
    cv = [sbt(f"cv{i}", [128, 512], F32) for i in range(4)]
    sl = [sbt(f"sl{i}", [128, 512], F32) for i in range(4)]
    sqb = [sbt(f"sqb{i}", [128, 512], F32R) for i in range(4)]
    rnb = [sbt(f"rnb{i}", [128, 512], F32) for i in range(4)]
    qT = sbt("gqT", [128, 4, 512], BF16)
    kT = sbt("gkT", [128, 4, 512], BF16)
    vT = sbt("gvT", [128, 4, 512], BF16)
    sz = sbt("sz", [128, 4, 512], BF16)
    mqT = sbt("gmqT", [128, 2, 512], BF16)
    catseg = sbt("catseg", [128, 6, 512], BF16)
    S = [sbt(f"S{h}", [128, 128], F32) for h in range(4)]
    Sb = [sbt(f"Sb{h}", [128, 128], BF16) for h in range(4)]
    for h in range(4):
        P.op("dve", lambda E, h=h: E.memset(S[h][:], 0.0), writes=[("S", h)])
        P.op("dve", lambda E, h=h: E.memset(Sb[h][:], 0.0), writes=[("Sb", h)])
    for c in range(12):
        P.op("pool", lambda E, c=c: E.memset(pre[c][:, 0:3], 0.0), writes=[("pre", c)])
    sc_sets = [{n: sbt(f"sc{j}_" + n, [128, 4], F32) for n in ("beta", "y", "ey", "sp", "g", "gc", "gam", "dca", "dcb", "tmp", "kdec", "bg")}
               for j in range(4)]
    NB2 = 4
    wt = {}
    for n, dt_ in (("gtri", F32R), ("Ds", F32), ("DmT", F32), ("L0", F32R), ("L1", F32R), ("U0", F32R), ("U1", F32R),
                   ("R0", F32R), ("R1", F32R), ("Av", F32), ("o", F32), ("og", F32)):
        wt[n] = [sbt(f"wt_{n}{i}", [128, 128], dt_) for i in range(NB2)]
    for n in ("AT", "Rb", "kbg", "kd", "vb", "nwT", "vn", "og2"):
        wt[n] = [sbt(f"wt_{n}{i}", [128, 128], BF16) for i in range(8 if n in ("AT", "Rb", "kd", "vb", "nwT") else NB2)]
    gss = [sbt(f"gss{i}", [128, 1], F32) for i in range(NB2)]
    grs = [sbt(f"grs{i}", [128, 1], F32) for i in range(NB2)]
    junk = [sbt(f"junk{i}", [128, 128], F32) for i in range(NB2)]
    mpT = [sbt(f"mpT{i}", [128, 512], BF16) for i in range(4)]
    rden = sbt("rden", [128, 512], F32)

    def small_bank(pool="sm"):
        return C.bank(pool, [2, 3, 4, 5, 6, 7])

    nseg = dbg.get("nseg", 8)
    for tg in range(nseg):
        load_xT_group(C, xT_d, xT_sb, x_is_f32, tg)
        slot = tg % 4
        def s2_chain(c):
            qkv, h = c // 4, c % 4
            wn = ("wq", "wk", "wv")[qkv]
            if tg > 0:
                P.copy("pool", pre[c][:, 0:3], pre[c][:, 512:515], reads=[("pre", c)], writes=[("pre", c)])
                yield
            b = proj_fm(C, wsb[wn], wn, h * 128, xT_sb, tg, [0, 1, 2, 3, 4, 5, 6, 7])
            P.copy("act", pre[c][:, 3:515], C.banks[b][:], reads=[("pre", c)], writes=[("pre", c)], banks=[b])
            yield
            ci = c % 4
            ce = "dve"
            P.ts(ce, cv[ci][:], pre[c][:, 3:515], convw[:, c, 3:4], ALU.mult, reads=[("pre", c), "convw"], writes=[("cv", ci)])
            yield
            for j in (2, 1, 0):
                P.stt(ce, cv[ci][:], pre[c][:, j:j + 512], convw[:, c, j:j + 1], cv[ci][:], ALU.mult, ALU.add,
                      reads=[("pre", c), ("cv", ci), "convw"], writes=[("cv", ci)])
                yield
            if qkv == 2:
                P.act(vT[:, h, :], cv[ci][:], AF.Silu, reads=[("cv", ci)], writes=[("vT", h)])
                yield
            else:
                si = c % 4
                P.act(sl[si][:], cv[ci][:], AF.Silu, reads=[("cv", ci)], writes=[("sl", si)])
                yield
                P.act(sqb[si][:], sl[si][:], AF.Square, reads=[("sl", si)], writes=[("sqb", si)])
                yield
                b2 = C.bank("pf", [0, 1, 2, 3, 4, 5, 6, 7])
                P.mm(C.banks[b2][:], ones_r[:], sqb[si][:], True, True, reads=["ones_r", ("sqb", si)], banks=[b2])
                yield
                if qkv == 0:
                    P.act(rnb[si][:], C.banks[b2][:], AF.Ln, bias=128.0 * RMS_EPS, scale=128.0, writes=[("rnb", si)], banks=[b2])
                else:
                    P.act(rnb[si][:], C.banks[b2][:], AF.Ln, bias=RMS_EPS, scale=1.0, writes=[("rnb", si)], banks=[b2])
                P.act(rnb[si][:], rnb[si][:], AF.Exp, scale=-0.5, reads=[("rnb", si)], writes=[("rnb", si)])
                yield
                dst = qT if qkv == 0 else kT
                P.tt("pool", dst[:, h, :], sl[si][:], rnb[si][:], ALU.mult, reads=[("sl", si), ("rnb", si)],
                     writes=[(("qT", "kT")[qkv], h)])
                yield
        for w_ in range(3):
            gens = [s2_chain(c) for c in range(w_ * 4, w_ * 4 + 4)]
            while gens:
                for g_ in list(gens):
                    try:
                        next(g_)
                    except StopIteration:
                        gens.remove(g_)
        for h in range(2):
            b = proj_fm(C, wsb["wmq"], "wmq", h * 128, xT_sb, tg, [0, 1])
            P.act(mqT[:, h, :], C.banks[b][:], AF.Copy, scale=128.0 ** -0.5, writes=[("mqT", h)], banks=[b])
        def scal(tl):
            ts_ = slice(tl * 128, (tl + 1) * 128)
            xs = slice(slot * 512 + tl * 128, slot * 512 + (tl + 1) * 128)
            sc = sc_sets[tl]
            sk = lambda n, _p=tl: (n, _p)
            b = C.bank("pf", [0, 1])
            for kc in range(8):
                P.mm(C.banks[b][:], xT_sb[:, kc, xs], wsb["wz"][:, kc, :], kc == 0, kc == 7, reads=["wz", ("xT", slot)], banks=[b])
            P.act(sz[:, tl, :], C.banks[b][:], AF.Silu, writes=[("sz", tl)], banks=[b])
            yield
            b = small_bank()
            pab = C.banks[b][:, 0:8]
            for kc in range(8):
                P.mm(pab, xT_sb[:, kc, xs], wsb["wab"][:, kc, :], kc == 0, kc == 7, reads=["wab", ("xT", slot)], banks=[b])
            P.act(sc["beta"][:], C.banks[b][:, 4:8], AF.Sigmoid, writes=[sk("beta")], banks=[b])
            yield
            P.tt("dve", sc["y"][:], C.banks[b][:, 0:4], dtb[:], ALU.add, reads=["dtb"], writes=[sk("y")], banks=[b])
            yield
            P.act(sc["ey"][:], sc["y"][:], AF.Exp, reads=[sk("y")], writes=[sk("ey")])
            yield
            P.act(sc["sp"][:], sc["ey"][:], AF.Ln, bias=1.0, reads=[sk("ey")], writes=[sk("sp")])
            yield
            P.tt("dve", sc["g"][:], sc["sp"][:], negA[:], ALU.mult, reads=[sk("sp"), "negA"], writes=[sk("g")])
            yield
            b = small_bank()
            pb = C.banks[b]
            P.mm(pb[:, 0:4], TRI, sc["g"][:], True, True, reads=["cm", sk("g")], banks=[b])
            yield
            P.mm(pb[:, 8:12], ONA, sc["g"][:], True, True, reads=["cm", sk("g")], banks=[b])
            yield
            P.mm(pb[:, 16:20], ONB, sc["g"][:], True, True, reads=["cm", sk("g")], banks=[b])
            yield
            P.copy("dve", sc["gc"][:], pb[:, 0:4], writes=[sk("gc")], banks=[b])
            yield
            P.act(sc["gam"][:], pb[:, 0:4], AF.Exp, writes=[sk("gam")], banks=[b])
            yield
            P.act(sc["dca"][:], pb[:, 8:12], AF.Exp, writes=[sk("dca")], banks=[b])
            yield
            P.act(sc["dcb"][:], pb[:, 16:20], AF.Exp, writes=[sk("dcb")], banks=[b])
            yield
            P.tt("dve", sc["tmp"][0:64, :], pb[0:64, 8:12], sc["gc"][0:64, :], ALU.subtract, reads=[sk("gc")], writes=[sk("tmpa")], banks=[b])
            yield
            P.tt("dve", sc["tmp"][64:128, :], pb[64:128, 16:20], sc["gc"][64:128, :], ALU.subtract, reads=[sk("gc")], writes=[sk("tmpb")], banks=[b])
            yield
            P.act(sc["kdec"][:], sc["tmp"][:], AF.Exp, reads=[sk("tmpa"), sk("tmpb")], writes=[sk("kdec")])
            yield
            P.tt("dve", sc["bg"][:], sc["beta"][:], sc["gam"][:], ALU.mult, reads=[sk("beta"), sk("gam")], writes=[sk("bg")])
            yield
        gens = [scal(tl) for tl in range(4)]
        while gens:
            for g_ in list(gens):
                try:
                    next(g_)
                except StopIteration:
                    gens.remove(g_)
        HAND = ("AT", "Rb", "kd", "vb", "nwT")

        def pre_gen(tl, h):
                ts_ = slice(tl * 128, (tl + 1) * 128)
                sc = sc_sets[tl]
                sk = lambda n, _p=tl: (n, _p)
                i = h
                ih = (tl % 2) * 4 + h
                W = {n: (wt[n][ih] if n in HAND else wt[n][h]) for n in wt}
                k_ = lambda n: (n, ih if n in HAND else h)
                kt_ap = kT[:, h, ts_]
                qt_ap = qT[:, h, ts_]
                P.ts("dve", W["gtri"][:], TRI, sc["g"][:, h:h + 1], ALU.mult, reads=["cm", sk("g")], writes=[k_("gtri")])
                bE = small_bank()
                P.mm(C.banks[bE][:, 0:128], W["gtri"][:], sutr[:], True, True, reads=[k_("gtri"), "sutr"], banks=[bE])
                P.mm(C.banks[bE][:, 128:256], sutr[:], W["gtri"][:], True, True, reads=[k_("gtri"), "sutr"], banks=[bE])
                P.act(W["Ds"][:], C.banks[bE][:, 0:128], AF.Exp, writes=[k_("Ds")], banks=[bE])
                P.act(W["DmT"][:], C.banks[bE][:, 128:256], AF.Exp, writes=[k_("DmT")], banks=[bE])
                P.tt("pool", W["Ds"][:], W["Ds"][:], MSL, ALU.mult, reads=[k_("Ds"), "cm"], writes=[k_("Ds")])
                P.tt("pool", W["DmT"][:], W["DmT"][:], MUI, ALU.mult, reads=[k_("DmT"), "cm"], writes=[k_("DmT")])
                yield
                bK = small_bank()
                P.mm(C.banks[bK][:, 0:128], kt_ap, kt_ap, True, True, reads=[("kT", h)], banks=[bK])
                P.mm(C.banks[bK][:, 128:256], kt_ap, qt_ap, True, True, reads=[("kT", h), ("qT", h)], banks=[bK])
                P.stt("dve", W["L0"][:], C.banks[bK][:, 0:128], sc["beta"][:, h:h + 1], W["Ds"][:], ALU.mult, ALU.mult,
                      reads=[sk("beta"), k_("Ds")], writes=[k_("L0")], banks=[bK])
                P.tt("dve", W["AT"][:], C.banks[bK][:, 128:256], W["DmT"][:], ALU.mult, reads=[k_("DmT")], writes=[k_("AT")], banks=[bK])
                yield
                bU = small_bank()
                P.mm(C.banks[bU][:, 0:128], W["L0"][:], idr[:], True, True, reads=[k_("L0"), "ident_r"], banks=[bU])
                P.copy("act", W["U0"][:], C.banks[bU][:, 0:128], writes=[k_("U0")], banks=[bU])
                P.tt("dve", W["R0"][:], idf[:], W["U0"][:], ALU.subtract, reads=["ident_f", k_("U0")], writes=[k_("R0")])
                yield
                Lc, Uc, Rc = "L0", "U0", "R0"
                for it in range(5):
                    Ln_, Un_, Rn_ = ("L1", "U1", "R1") if Lc == "L0" else ("L0", "U0", "R0")
                    bN = small_bank()
                    P.mm(C.banks[bN][:, 0:128], W[Uc][:], W[Lc][:], True, True, reads=[k_(Uc), k_(Lc)], banks=[bN])
                    if it < 4:
                        P.mm(C.banks[bN][:, 128:256], W[Lc][:], W[Uc][:], True, True, reads=[k_(Uc), k_(Lc)], banks=[bN])
                    P.copy("act", W[Ln_][:], C.banks[bN][:, 0:128], writes=[k_(Ln_)], banks=[bN])
                    if it < 4:
                        P.copy("dve", W[Un_][:], C.banks[bN][:, 128:256], writes=[k_(Un_)], banks=[bN])
                    bR = small_bank()
                    P.mm(C.banks[bR][:, 0:128], W[Ln_][:], W[Rc][:], True, True, reads=[k_(Ln_), k_(Rc)], banks=[bR])
                    P.tt("dve", W[Rn_][:], C.banks[bR][:, 0:128], W[Rc][:], ALU.add, reads=[k_(Rc)], writes=[k_(Rn_)], banks=[bR])
                    yield
                    Lc, Uc, Rc = Ln_, Un_, Rn_
                P.copy("act", W["Rb"][:], W[Rc][:], reads=[k_(Rc)], writes=[k_("Rb")])
                bT = small_bank()
                pbt = C.banks[bT][:].bitcast(BF16)
                P.tr(pbt[:, 0:128], kt_ap, idb[:], reads=[("kT", h), "ident_bf"], banks=[bT])
                P.tr(pbt[:, 128:256], vT[:, h, ts_], idb[:], reads=[("vT", h), "ident_bf"], banks=[bT])
                P.ts("dve", W["kbg"][:], pbt[:, 0:128], sc["bg"][:, h:h + 1], ALU.mult, reads=[sk("bg")], writes=[k_("kbg")], banks=[bT])
                P.act(W["kd"][:], pbt[:, 0:128], AF.Identity, scale=sc["kdec"][:, h:h + 1], reads=[sk("kdec")], writes=[k_("kd")], banks=[bT])
                P.ts("dve", W["vb"][:], pbt[:, 128:256], sc["beta"][:, h:h + 1], ALU.mult, reads=[sk("beta")], writes=[k_("vb")], banks=[bT])
                yield
                bW = small_bank()
                P.mm(C.banks[bW][:, 0:128], W["kbg"][:], W["Rb"][:], True, True, reads=[k_("kbg"), k_("Rb")], banks=[bW])
                P.act(W["nwT"][:], C.banks[bW][:, 0:128], AF.Copy, scale=-1.0, writes=[k_("nwT")], banks=[bW])
                yield

        def run_gen(tl, h):
                ts_ = slice(tl * 128, (tl + 1) * 128)
                sc = sc_sets[tl]
                sk = lambda n, _p=tl: (n, _p)
                i = h
                ih = (tl % 2) * 4 + h
                W = {n: (wt[n][ih] if n in HAND else wt[n][h]) for n in wt}
                k_ = lambda n: (n, ih if n in HAND else h)
                kt_ap = kT[:, h, ts_]
                qt_ap = qT[:, h, ts_]
                for half in range(2):
                    rows = slice(half * 64, (half + 1) * 64)
                    M = 64 if half == 0 else 128
                    dc = sc["dca"] if half == 0 else sc["dcb"]
                    bV = small_bank()
                    P.mm(C.banks[bV][0:M, 0:128], W["Rb"][rows, 0:M], W["vb"][rows, :], True, False,
                         reads=[k_("Rb"), k_("vb")], banks=[bV])
                    P.mm(C.banks[bV][0:M, 0:128], W["nwT"][:, 0:M], Sb[h][:], False, True,
                         reads=[k_("nwT"), ("Sb", h)], banks=[bV])
                    P.copy("act", W["vn"][rows, :], C.banks[bV][rows, 0:128], writes=[(k_("vn"), half)], banks=[bV])
                    yield
                    bO = small_bank()
                    P.mm(C.banks[bO][0:M, 0:128], qt_ap[:, 0:M], Sb[h][:], True, True, reads=[("qT", h), ("Sb", h)], banks=[bO])
                    P.mm(C.banks[bO][0:M, 128:256], W["AT"][rows, 0:M], W["vn"][rows, :], True, True,
                         reads=[k_("AT"), (k_("vn"), half)], banks=[bO])
                    P.copy("act", W["Av"][rows, :], C.banks[bO][rows, 128:256], writes=[(k_("Av"), half)], banks=[bO])
                    P.stt("dve", W["o"][rows, :], C.banks[bO][rows, 0:128], sc["gam"][rows, h:h + 1], W["Av"][rows, :], ALU.mult, ALU.add,
                          reads=[sk("gam"), (k_("Av"), half)], writes=[(k_("o"), half)], banks=[bO])
                    yield
                    bS = small_bank()
                    P.mm(C.banks[bS][:, 0:128], W["kd"][rows, :], W["vn"][rows, :], True, True,
                         reads=[k_("kd"), (k_("vn"), half)], banks=[bS])
                    P.stt("dve", S[h][:], S[h][:], dc[:, h:h + 1], C.banks[bS][:, 0:128], ALU.mult, ALU.add,
                          reads=[("S", h), sk("dca"), sk("dcb")], writes=[("S", h)], banks=[bS])
                    P.copy("act", Sb[h][:], S[h][:], reads=[("S", h)], writes=[("Sb", h)])
                    yield
                okeys = [(k_("o"), 0), (k_("o"), 1)]
                P.op("pool", lambda E, i=i: E.memset(gss[i][:], 0.0), writes=[("gss", i)])
                P.act(junk[i][:], W["o"][:], AF.Square, accum_out=gss[i][:], reads=okeys + [("gss", i)], writes=[("junk", i), ("gss", i)])
                P.act(grs[i][:], gss[i][:], AF.Sqrt, bias=RMS_EPS, scale=1.0 / 128.0, reads=[("gss", i)], writes=[("grs", i)])
                P.op("dve", lambda E, i=i: E.reciprocal(out=grs[i][:], in_=grs[i][:]), reads=[("grs", i)], writes=[("grs", i)])
                P.stt("dve", W["og"][:], W["o"][:], grs[i][:], gw[:], ALU.mult, ALU.mult, reads=okeys + [("grs", i), "gw"], writes=[k_("og")])
                yield
                P.tt("pool", W["og2"][:], W["og"][:], sz[:, tl, h * 128:(h + 1) * 128], ALU.mult, reads=[k_("og"), ("sz", tl)], writes=[k_("og2")])
                bG = small_bank()
                pbg = C.banks[bG][:].bitcast(BF16)
                P.tr(pbg[:, 0:128], W["og2"][:], idb[:], reads=[k_("og2"), "ident_bf"], banks=[bG])
                P.copy("act", catseg[:, h, ts_], pbg[:, 0:128], writes=[("catseg", h, tl)], banks=[bG])

        for step in range(5):
            gens = []
            if step >= 1:
                gens += [run_gen(step - 1, h) for h in range(4)]
            if step < 4:
                gens += [pre_gen(step, h) for h in range(4)]
            while gens:
                for g_ in list(gens):
                    try:
                        next(g_)
                    except StopIteration:
                        gens.remove(g_)
        qs = slice(tg * 512, (tg + 1) * 512)
        for h in range(4):
            outs.append(P.dma_("sp", _cat_dst(catT_out, h, tg), catseg[:, h, :], reads=[("catseg", h, tl) for tl in range(4)], writes=[("catdst", h, tg)]))
        for h in range(2):
            mem_attention_group(C, KmT, Vm, ones_bf, mqT[:, h, :], ("mqT", h), catseg[:, 4 + h, :], ("catseg", 4 + h), mpT, rden, h)
            outs.append(P.dma_("act", _cat_dst(catT_out, 4 + h, tg), catseg[:, 4 + h, :], reads=[("catseg", 4 + h)], writes=[("catdst", 4 + h, tg)]))
        if dbg.get("after_seg"):
            dbg["after_seg"](tg)
    return outs


def make_gdn_program(x_is_f32=True, dbg=None):
    nc = bass.Bass("TRN2", target_bir_lowering=False)
    dt = lambda n, s, d, k="ExternalInput": nc.dram_tensor(n, s, d, kind=k).ap()
    xT_d = dt("xT", [1024, T], F32 if x_is_f32 else BF16)
    mem = dt("mem", [256, 1024], F32)
    mlng = dt("mlng", [128, 1024], F32)
    mlnb = dt("mlnb", [128, 1024], F32)
    wmk = dt("wmk", [1024, 256], F32)
    wmv = dt("wmv", [1024, 256], F32)
    wq = dt("wq", [1024, 512], F32)
    wk = dt("wk", [1024, 512], F32)
    wv = dt("wv", [1024, 512], F32)
    wz = dt("wz", [1024, 512], F32)
    wab = dt("wab", [1024, 8], F32)
    wmq = dt("wmq", [1024, 256], F32)
    convw = dt("convw", [128, 12, 4], F32)
    alog = dt("alog", [128, 4], F32)
    dtb = dt("dtb", [128, 4], F32)
    gw = dt("gw", [128, 128], F32)
    cm = dt("cm", [6, 128, 128], F32)
    ident = dt("ident", [128, 128], F32)
    catT_out = dt("catT_out", [768, T], BF16, "ExternalOutput")
    C = Ctx(nc)
    with ExitStack() as st:
        outs = build_gdn_phase(C, xT_d, x_is_f32, mem, mlng, mlnb, wmk, wmv, wq, wk, wv, wz, wab, wmq, convw, alog, dtb, gw,
                               cm, ident, catT_out, st, dbg)
        with ExitStack() as st2:
            stats = C.P.emit(st2, final_wait_ops=outs)
    return nc, stats


def gdn_const_masks():
    i = np.arange(128)
    same = (i[:, None] // 64) == (i[None, :] // 64)
    tri = ((i[:, None] <= i[None, :]) & same)
    sut = (i[:, None] > i[None, :])
    msl = ((i[:, None] > i[None, :]) & same)
    mui = ((i[:, None] <= i[None, :]) & same)
    ona = np.broadcast_to((i[:, None] < 64), (128, 128))
    onb = np.broadcast_to((i[:, None] >= 64), (128, 128))
    return np.stack([tri, sut, msl, mui, ona, onb]).astype(np.float32)


def _rep(v, n=128):
    v = np.asarray(v, np.float32)
    return np.ascontiguousarray(np.broadcast_to(v[None, :], (n, v.shape[0])))


def _c(a):
    return np.ascontiguousarray(a)


def _diff_masks():
    masks = np.zeros((4, 128, 512), np.float32)
    for r in range(4):
        masks[r] = (128 * r + np.arange(128)[:, None] <= np.arange(512)[None, :]).astype(np.float32)
    return masks


def _mem_inputs(inp, b, hh):
    w = inp["w_mem_kv"]
    return dict(mem=_c(inp["mem"][b]), mlng=_rep(inp["mem_ln_g"]), mlnb=_rep(inp["mem_ln_b"]),
                wmk=_c(w[:, hh * 256:hh * 256 + 256]), wmv=_c(w[:, 512 + hh * 256:512 + hh * 256 + 256]),
                ident=np.eye(128, dtype=np.float32))


def _gdn_inputs(inp, b, hh):
    w_in = inp["l0_w_in"]
    conv_w = inp["l0_conv_w"]
    convw = np.zeros((128, 12, 4), np.float32)
    for qkv in range(3):
        for h in range(4):
            ch0 = qkv * 1024 + (hh * 4 + h) * 128
            convw[:, qkv * 4 + h, :] = conv_w[:, ch0:ch0 + 128].T
    ab = np.concatenate([w_in[:, 4096 + hh * 4:4096 + hh * 4 + 4], w_in[:, 4104 + hh * 4:4104 + hh * 4 + 4]], axis=1)
    d = _mem_inputs(inp, b, hh)
    d.update(xT=_c(inp["x"][b].T),
             wq=_c(w_in[:, hh * 512:hh * 512 + 512]), wk=_c(w_in[:, 1024 + hh * 512:1024 + hh * 512 + 512]),
             wv=_c(w_in[:, 2048 + hh * 512:2048 + hh * 512 + 512]), wz=_c(w_in[:, 3072 + hh * 512:3072 + hh * 512 + 512]),
             wab=_c(ab), wmq=_c(w_in[:, 4112 + hh * 256:4112 + hh * 256 + 256]),
             convw=convw, alog=_rep(inp["l0_a_log"][hh * 4:hh * 4 + 4]), dtb=_rep(inp["l0_dt_bias"][hh * 4:hh * 4 + 4]),
             gw=_rep(inp["l0_gate_norm_w"]), cm=gdn_const_masks())
    return d


def _diff_inputs(inp, b, hh, xT_bf):
    w_in = inp["l1_w_in"]
    d = _mem_inputs(inp, b, hh)
    lam = np.stack([inp["l1_lambda_q1"], inp["l1_lambda_k1"], inp["l1_lambda_q2"], inp["l1_lambda_k2"]]).astype(np.float32)
    d.update(xT=xT_bf,
             wq=_c(w_in[:, hh * 512:hh * 512 + 512]), wk=_c(w_in[:, 1024 + hh * 512:1024 + hh * 512 + 512]),
             wv=_c(w_in[:, 2048 + hh * 512:2048 + hh * 512 + 512]), wmq=_c(w_in[:, 3072 + hh * 256:3072 + hh * 256 + 256]),
             lamv=_c(np.broadcast_to(lam[None], (128, 4, 64))), subw=_c(np.asarray(inp["l1_subln_w"], np.float32)[:, None]),
             masks=_diff_masks())
    return d


def _ffn_inputs(inp, layer, catT_pair, th, xres):
    p = f"l{layer}_"
    c0, c1 = catT_pair
    ts = slice(th * 2048, (th + 1) * 2048)
    catT = np.concatenate([c0[0:512, ts], c1[0:512, ts], c0[512:768, ts], c1[512:768, ts]], axis=0)
    return dict(catT=_c(catT), xres=_c(xres), w_out=_c(inp[p + "w_out"]), w_ff1=_c(inp[p + "w_ff1"]), w_ff2=_c(inp[p + "w_ff2"]),
                ln1g=_rep(inp[p + "ln1_g"]), ln1b=_rep(inp[p + "ln1_b"]), ln2g=_rep(inp[p + "ln2_g"]), ln2b=_rep(inp[p + "ln2_b"]),
                ident=np.eye(128, dtype=np.float32))


def kernel(**inputs):
    inp = {k: np.asarray(v) for k, v in inputs.items()}
    cores = list(range(8))
    ncA, _ = make_gdn_program(x_is_f32=True)
    resA = run_bass_kernel_spmd(ncA, [_gdn_inputs(inp, c // 2, c % 2) for c in cores], core_ids=cores)
    catA = [np.asarray(r["catT_out"]) for r in resA.results]
    ncB, _ = make_ffn_program()
    inB = [_ffn_inputs(inp, 0, (catA[2 * (c // 2)], catA[2 * (c // 2) + 1]), c % 2,
                       inp["x"][c // 2, (c % 2) * 2048:(c % 2 + 1) * 2048, :]) for c in cores]
    resB = run_bass_kernel_spmd(ncB, inB, core_ids=cores)
    x1 = [np.asarray(r["x_out"]) for r in resB.results]
    x1T = [np.asarray(r["xT_out"]) for r in resB.results]
    ncC, _ = make_diff_program(x_is_f32=False)
    inC = []
    for c in cores:
        b = c // 2
        xT_bf = _c(np.concatenate([x1T[2 * b], x1T[2 * b + 1]], axis=1))
        inC.append(_diff_inputs(inp, b, c % 2, xT_bf))
    resC = run_bass_kernel_spmd(ncC, inC, core_ids=cores)
    catC = [np.asarray(r["catT_out"]) for r in resC.results]
    ncD, _ = make_ffn_program()
    inD = [_ffn_inputs(inp, 1, (catC[2 * (c // 2)], catC[2 * (c // 2) + 1]), c % 2, x1[c]) for c in cores]
    resD = run_bass_kernel_spmd(ncD, inD, core_ids=cores)
    out = np.zeros((NB, T, D), np.float32)
    for c in cores:
        out[c // 2, (c % 2) * 2048:(c % 2 + 1) * 2048, :] = np.asarray(resD.results[c]["x_out"])
    return out


def make_fused_program(n_cores=8, dbg=None):
    dbg = dbg or {}
    nc = bass.Bass("TRN2", target_bir_lowering=False)
    dt = lambda n, s, d, k="ExternalInput": nc.dram_tensor(n, s, d, kind=k).ap()
    groups = [[2 * i, 2 * i + 1] for i in range(n_cores // 2)]
    ident = dt("ident", [128, 128], F32)
    sel_d = dt("sel", [128, 2], F32)
    mem = dt("mem", [256, 1024], F32)
    mlng = dt("mlng", [128, 1024], F32)
    mlnb = dt("mlnb", [128, 1024], F32)
    wmk = dt("wmk", [1024, 256], F32)
    wmv = dt("wmv", [1024, 256], F32)
    a_xT = dt("a_xT", [1024, T], F32)
    a_w = {n: dt("a_" + n, [1024, c], F32) for n, c in (("wq", 512), ("wk", 512), ("wv", 512), ("wz", 512), ("wab", 8), ("wmq", 256))}
    a_convw = dt("a_convw", [128, 12, 4], F32)
    a_alog = dt("a_alog", [128, 4], F32)
    a_dtb = dt("a_dtb", [128, 4], F32)
    a_gw = dt("a_gw", [128, 128], F32)
    a_cm = dt("a_cm", [6, 128, 128], F32)
    c_w = {n: dt("c_" + n, [1024, c], F32) for n, c in (("wq", 512), ("wk", 512), ("wv", 512), ("wmq", 256))}
    c_lamv = dt("c_lamv", [128, 4, 64], F32)
    c_subw = dt("c_subw", [128, 1], F32)
    c_masks = dt("c_masks", [4, 128, 512], F32)
    f_in = {}
    for L in ("b", "d"):
        f_in[L] = dict(w_out=dt(L + "_w_out", [1536, 1024], F32), w_ff1=dt(L + "_w_ff1", [1024, 4096], F32),
                       w_ff2=dt(L + "_w_ff2", [4096, 1024], F32),
                       lns=[dt(L + "_" + n, [128, 1024], F32) for n in ("ln1g", "ln1b", "ln2g", "ln2b")])
    b_xres = dt("b_xres", [2048, 1024], F32)
    out_d = dt("out", [2048, 1024], F32, "ExternalOutput")
    it = lambda n, s, d: nc.dram_tensor(n, s, d).ap()
    cat_send = [it(f"cat_send{i}", [24 * 128, 1024], BF16) for i in range(2)]
    cat_recv = [it(f"cat_recv{i}", [24 * 256, 1024], BF16) for i in range(2)]
    x_send = it("x_send", [16 * 128, 1024], BF16)
    x_recv = it("x_recv", [16 * 256, 1024], BF16)
    x1_d = it("x1_d", [2048, 1024], F32)

    C = Ctx(nc)
    P = C.P

    def cat_dst_fn(i):
        v = cat_send[i].rearrange("(k q p) t -> k q p t", q=4, p=128)
        return lambda k, tg: v[k, tg // 2, :, (tg % 2) * 512:(tg % 2 + 1) * 512]

    def finish_exchange():
        P.coll_wait()
        P.barrier()

    def cat_after_seg(i):
        def f(tg):
            if tg % 2 == 1:
                q = tg // 2
                for k in range(6):
                    c = k * 4 + q
                    P.coll("AllGather", groups, cat_send[i][c * 128:(c + 1) * 128, :], cat_recv[i][c * 256:(c + 1) * 256, :],
                           reads=[("catdst", k, tg - 1), ("catdst", k, tg)], wait=False)
        return f

    def x_after_pass(ps):
        for kc in range(8):
            c = kc * 2 + ps
            P.coll("AllGather", groups, x_send[c * 128:(c + 1) * 128, :], x_recv[c * 256:(c + 1) * 256, :],
                   reads=[("xsend", ps * 1024 + j * 128) for j in range(8)], wait=False)

    def make_cat_loader(i, selt, tmpA, tmpB):
        rv = cat_recv[i].rearrange("(k q r p) t -> p r k q t", q=4, r=2, p=128)

        def loader(dst, key, g):
            off = (g % 2) * 512
            for r in range(2):
                P.dma_("sp", tmpA[:], rv[:, r, :, g // 2, off:off + 512], writes=["catA"])
                P.dma_("act", tmpB[:], rv[:, r, :, 2 + g // 2, off:off + 512], writes=["catB"])
                P.ts("dve", tmpB[:], tmpB[:], selt[:, 1:2], ALU.mult, reads=["catB", "sel"], writes=["catB"])
                P.stt("dve", dst[:, r * 6:(r + 1) * 6, :], tmpA[:], selt[:, 0:1], tmpB[:], ALU.mult, ALU.add,
                      reads=["catA", "catB", "sel"], writes=[key])
        return loader

    xs_v = x_send.rearrange("(kc q p) t -> p kc q t", q=2, p=128)

    def xT_dst(tok):
        return xs_v[:, :, tok // 1024, tok % 1024:tok % 1024 + 128]

    C.pfx = "A_"
    with ExitStack() as st:
        build_gdn_phase(C, a_xT, True, mem, mlng, mlnb, wmk, wmv, a_w["wq"], a_w["wk"], a_w["wv"], a_w["wz"], a_w["wab"], a_w["wmq"],
                        a_convw, a_alog, a_dtb, a_gw, a_cm, ident, cat_dst_fn(0), st, dict(dbg, after_seg=cat_after_seg(0)))
        finish_exchange()
    C.pfx = "B_"
    with ExitStack() as st:
        selt = st.enter_context(nc.sbuf_tensor("B_sel", [128, 2], F32))
        tmpA = st.enter_context(nc.sbuf_tensor("B_tmpA", [128, 6, 512], BF16))
        tmpB = st.enter_context(nc.sbuf_tensor("B_tmpB", [128, 6, 512], BF16))
        P.dma_("sp", selt[:], sel_d, writes=["sel"])
        d2 = dict(dbg)
        d2.update(cat_loader=make_cat_loader(0, selt, tmpA, tmpB), xT_dst=xT_dst, after_pass=x_after_pass)
        fi = f_in["b"]
        build_ffn_phase(C, None, b_xres, fi["w_out"], fi["w_ff1"], fi["w_ff2"], *fi["lns"], ident, x1_d, None, st, d2)
        finish_exchange()
    C.pfx = "C_"
    xr_v = x_recv.rearrange("(kc q r p) t -> p kc q r t", q=2, r=2, p=128)

    class _XT:
        def rearrange(self, *_a, **_k):
            return self

        def __getitem__(self, idx):
            tsl = idx[2]
            tg = tsl.start // 512
            r, l = tg // 4, (tg % 4)
            return xr_v[:, :, l // 2, r, (l % 2) * 512:(l % 2 + 1) * 512]

    with ExitStack() as st:
        build_diff_phase(C, _XT(), False, mem, mlng, mlnb, wmk, wmv, c_w["wq"], c_w["wk"], c_w["wv"], c_w["wmq"], c_lamv, c_subw,
                         c_masks, ident, cat_dst_fn(1), st, dict(dbg, after_seg=cat_after_seg(1)))
        finish_exchange()
    C.pfx = "D_"
    with ExitStack() as st:
        selt = st.enter_context(nc.sbuf_tensor("D_sel", [128, 2], F32))
        tmpA = st.enter_context(nc.sbuf_tensor("D_tmpA", [128, 6, 512], BF16))
        tmpB = st.enter_context(nc.sbuf_tensor("D_tmpB", [128, 6, 512], BF16))
        P.dma_("sp", selt[:], sel_d, writes=["sel"])
        d2 = dict(dbg)
        d2.update(cat_loader=make_cat_loader(1, selt, tmpA, tmpB), xT_dst=None)
        fi = f_in["d"]
        outs = build_ffn_phase(C, None, x1_d, fi["w_out"], fi["w_ff1"], fi["w_ff2"], *fi["lns"], ident, out_d, None, st, d2)
        with ExitStack() as st2:
            stats = P.emit(st2, final_wait_ops=outs)
    return nc, stats


def _perm_w_out(w):
    idx = []
    for r in range(2):
        idx += list(range(r * 512, r * 512 + 512))
        idx += list(range(1024 + r * 256, 1024 + r * 256 + 256))
    return np.ascontiguousarray(w[np.asarray(idx)])


def _fused_inputs(inp, c):
    b, hh = c // 2, c % 2
    g = _gdn_inputs(inp, b, hh)
    d = dict(ident=g["ident"], mem=g["mem"], mlng=g["mlng"], mlnb=g["mlnb"], wmk=g["wmk"], wmv=g["wmv"])
    sel = np.zeros((128, 2), np.float32)
    sel[:, hh] = 1.0
    d["sel"] = sel
    d["a_xT"] = g["xT"]
    for n in ("wq", "wk", "wv", "wz", "wab", "wmq"):
        d["a_" + n] = g[n]
    d.update(a_convw=g["convw"], a_alog=g["alog"], a_dtb=g["dtb"], a_gw=g["gw"], a_cm=g["cm"])
    w1 = inp["l1_w_in"]
    d.update(c_wq=_c(w1[:, hh * 512:hh * 512 + 512]), c_wk=_c(w1[:, 1024 + hh * 512:1024 + hh * 512 + 512]),
             c_wv=_c(w1[:, 2048 + hh * 512:2048 + hh * 512 + 512]), c_wmq=_c(w1[:, 3072 + hh * 256:3072 + hh * 256 + 256]))
    lam = np.stack([inp["l1_lambda_q1"], inp["l1_lambda_k1"], inp["l1_lambda_q2"], inp["l1_lambda_k2"]]).astype(np.float32)
    d.update(c_lamv=_c(np.broadcast_to(lam[None], (128, 4, 64))), c_subw=_c(np.asarray(inp["l1_subln_w"], np.float32)[:, None]),
             c_masks=_diff_masks())
    for L, layer in (("b", 0), ("d", 1)):
        p = f"l{layer}_"
        d[L + "_w_out"] = _perm_w_out(inp[p + "w_out"])
        d[L + "_w_ff1"] = _c(inp[p + "w_ff1"])
        d[L + "_w_ff2"] = _c(inp[p + "w_ff2"])
        for n, k in (("ln1g", "ln1_g"), ("ln1b", "ln1_b"), ("ln2g", "ln2_g"), ("ln2b", "ln2_b")):
            d[L + "_" + n] = _rep(inp[p + k])
    d["b_xres"] = _c(inp["x"][b, hh * 2048:(hh + 1) * 2048, :])
    return d


def kernel_unfused(**inputs):
    return _kernel_unfused(**inputs)


_kernel_unfused = kernel


def kernel(**inputs):
    inp = {k: np.asarray(v) for k, v in inputs.items()}
    cores = list(range(8))
    nc, _ = make_fused_program(8)
    res = run_bass_kernel_spmd(nc, [_fused_inputs(inp, c) for c in cores], core_ids=cores)
    out = np.zeros((NB, T, D), np.float32)
    for c in cores:
        out[c // 2, (c % 2) * 2048:(c % 2 + 1) * 2048, :] = np.asarray(res.results[c]["out"])
    return out
```
